# Optimizing a Trainium2 kernel written in Bass

```python
import math
import jax, jax.numpy as jnp
from jax import lax
import numpy as np

D_MODEL = 1024
BATCH = 16
SEQ = 4096
DEPTH = 1

GDN_HEADS = 8
GDN_DK = 128
GDN_DV = 128
GDN_CONV = 4
GDN_CHUNK = 64
MLA_HEADS = 8
MLA_Q_RANK = 384
MLA_KV_RANK = 256
MLA_NOPE = 128
MLA_ROPE = 64
MLA_V = 128
ROPE_THETA = 10000.0
Q_BLOCK = 128
D_FF = 2816
FFN_CONV = 3
EPS = 1e-6

SPLIT_SIZES = (
    3 * GDN_HEADS * GDN_DK if GDN_DK == GDN_DV else 2 * GDN_HEADS * GDN_DK + GDN_HEADS * GDN_DV,
    GDN_HEADS * GDN_DV,
    GDN_HEADS,
    GDN_HEADS,
    MLA_Q_RANK,
    MLA_KV_RANK,
    MLA_ROPE,
    D_MODEL,
    D_MODEL,
)
D_IN = sum(SPLIT_SIZES)

kernel_name = "hybrid_gdn_mla_convffn_block"


def rmsnorm(x, g):
    xf = x.astype(jnp.float32)
    xf = xf * lax.rsqrt(jnp.mean(xf * xf, axis=-1, keepdims=True) + EPS)
    return (xf * g.astype(jnp.float32)).astype(x.dtype)


def l2norm(x):
    return x * lax.rsqrt(jnp.sum(x * x, axis=-1, keepdims=True) + EPS)


def causal_dwconv(x, w):
    k = w.shape[0]
    return lax.conv_general_dilated(
        x, w[:, None, :].astype(x.dtype), window_strides=(1,), padding=[(k - 1, 0)],
        dimension_numbers=("NWC", "WIO", "NWC"), feature_group_count=x.shape[-1])


def split_in(z):
    offs = list(np.cumsum(SPLIT_SIZES)[:-1])
    return jnp.split(z, [int(o) for o in offs], axis=-1)


def rope(x, pos):
    half = x.shape[-1] // 2
    inv = ROPE_THETA ** (-jnp.arange(half, dtype=jnp.float32) / half)
    ang = pos.astype(jnp.float32)[:, None] * inv[None, :]
    cos = jnp.cos(ang)[:, None, :]
    sin = jnp.sin(ang)[:, None, :]
    xf = x.astype(jnp.float32)
    x1, x2 = xf[..., :half], xf[..., half:]
    return jnp.concatenate([x1 * cos - x2 * sin, x2 * cos + x1 * sin], axis=-1).astype(x.dtype)


def gdn_chunked(q, k, v, g, beta):
    b, s, h, dk = q.shape
    dv = v.shape[-1]
    c = GDN_CHUNK
    n = s // c
    f32 = jnp.float32
    q = l2norm(q.astype(f32)) * (dk ** -0.5)
    k = l2norm(k.astype(f32))
    v = v.astype(f32)

    def chunk(t):
        t = t.reshape((b, n, c, h) + t.shape[3:])
        return jnp.moveaxis(t, 3, 1)

    q, k, v = chunk(q), chunk(k), chunk(v)
    g = jnp.cumsum(chunk(g.astype(f32)), axis=-1)
    beta = chunk(beta.astype(f32))
    k_beta = k * beta[..., None]
    v_beta = v * beta[..., None]

    causal = jnp.tril(jnp.ones((c, c), dtype=bool))
    diff = g[..., :, None] - g[..., None, :]
    decay = jnp.where(causal, jnp.exp(jnp.where(causal, diff, 0.0)), 0.0)

    lmat = jnp.einsum("bhnid,bhnjd->bhnij", k_beta, k) * decay
    rhs = jnp.concatenate([v_beta, k_beta * jnp.exp(g)[..., None]], axis=-1)
    sol = lax.linalg.triangular_solve(lmat, rhs, left_side=True, lower=True, unit_diagonal=True)
    u, w = sol[..., :dv], sol[..., dv:]

    a_intra = jnp.einsum("bhnid,bhnjd->bhnij", q, k) * decay
    q_dec = q * jnp.exp(g)[..., None]
    k_dec = k * jnp.exp(g[..., -1:] - g)[..., None]
    g_last = jnp.exp(g[..., -1])

    xs = tuple(jnp.moveaxis(t, 2, 0) for t in (u, w, q_dec, k_dec, a_intra, g_last))

    def step(state, inp):
        u_n, w_n, qd_n, kd_n, a_n, gl_n = inp
        v_new = u_n - jnp.einsum("bhck,bhkv->bhcv", w_n, state)
        o = jnp.einsum("bhck,bhkv->bhcv", qd_n, state) + jnp.einsum("bhij,bhjv->bhiv", a_n, v_new)
        state = state * gl_n[..., None, None] + jnp.einsum("bhck,bhcv->bhkv", kd_n, v_new)
        return state, o

    s0 = jnp.zeros((b, h, dk, dv), dtype=f32)
    _, o = lax.scan(step, s0, xs)
    o = jnp.transpose(o, (1, 0, 3, 2, 4)).reshape(b, s, h, dv)
    return o


def mla_causal(q_nope, q_pe, k_nope, k_pe, v):
    s = q_nope.shape[1]
    scale = (MLA_NOPE + MLA_ROPE) ** -0.5
    outs = []
    for i in range(s // Q_BLOCK):
        q0, q1 = i * Q_BLOCK, (i + 1) * Q_BLOCK
        sc = (jnp.einsum("bqhd,bkhd->bhqk", q_nope[:, q0:q1], k_nope[:, :q1])
              + jnp.einsum("bqhr,bkr->bhqk", q_pe[:, q0:q1], k_pe[:, :q1]))
        sc = sc.astype(jnp.float32) * scale
        qpos = q0 + jnp.arange(Q_BLOCK)
        kpos = jnp.arange(q1)
        sc = jnp.where(qpos[:, None] >= kpos[None, :], sc, -jnp.inf)
        p = jax.nn.softmax(sc, axis=-1).astype(v.dtype)
        outs.append(jnp.einsum("bhqk,bkhd->bqhd", p, v[:, :q1]))
    return jnp.concatenate(outs, axis=1)


def setup_inputs(seed: int = 0) -> dict:
    key = jax.random.key(seed)
    ks = jax.random.split(key, 24)
    L, D = DEPTH, D_MODEL
    nrm = lambda k, shape, fan: jax.random.normal(k, shape, jnp.float32) * (fan ** -0.5)
    gain = lambda k, n: 1.0 + 0.05 * jax.random.normal(k, (L, n), jnp.float32)
    a_log = jnp.log(jax.random.uniform(ks[3], (L, GDN_HEADS), jnp.float32, 1.0, 16.0))
    dt = jnp.exp(jax.random.uniform(ks[4], (L, GDN_HEADS), jnp.float32, math.log(1e-3), math.log(1e-1)))
    dt_bias = dt + jnp.log(-jnp.expm1(-dt))
    return {
        "x": jax.random.normal(ks[0], (BATCH, SEQ, D), jnp.float32),
        "norm_mix_g": gain(ks[1], D),
        "w_in": nrm(ks[2], (L, D, D_IN), D),
        "conv_qkv_w": nrm(ks[5], (L, GDN_CONV, SPLIT_SIZES[0]), GDN_CONV),
        "gdn_a_log": a_log,
        "gdn_dt_bias": dt_bias,
        "gdn_norm_g": gain(ks[6], GDN_DV),
        "mla_q_norm_g": gain(ks[7], MLA_Q_RANK),
        "w_uq": nrm(ks[8], (L, MLA_Q_RANK, MLA_HEADS * (MLA_NOPE + MLA_ROPE)), MLA_Q_RANK),
        "mla_kv_norm_g": gain(ks[9], MLA_KV_RANK),
        "w_ukv": nrm(ks[10], (L, MLA_KV_RANK, MLA_HEADS * (MLA_NOPE + MLA_V)), MLA_KV_RANK),
        "w_o_gdn": nrm(ks[11], (L, GDN_HEADS * GDN_DV, D), GDN_HEADS * GDN_DV),
        "w_o_mla": nrm(ks[12], (L, MLA_HEADS * MLA_V, D), MLA_HEADS * MLA_V),
        "w_out": nrm(ks[13], (L, D, D), D),
        "norm_ffn_g": gain(ks[14], D),
        "w_up": nrm(ks[15], (L, D, 2 * D_FF), D),
        "conv_ffn_w": nrm(ks[16], (L, FFN_CONV, 2 * D_FF), FFN_CONV),
        "w_down": nrm(ks[17], (L, D_FF, D), D_FF),
        "norm_final_g": 1.0 + 0.05 * jax.random.normal(ks[18], (D,), jnp.float32),
    }


def reference(x, norm_mix_g, w_in, conv_qkv_w, gdn_a_log, gdn_dt_bias, gdn_norm_g,
              mla_q_norm_g, w_uq, mla_kv_norm_g, w_ukv, w_o_gdn, w_o_mla, w_out,
              norm_ffn_g, w_up, conv_ffn_w, w_down, norm_final_g):
    b, s, _ = x.shape
    pos = jnp.arange(s)
    hA, hB = GDN_HEADS, MLA_HEADS
    for l in range(DEPTH):
        h = rmsnorm(x, norm_mix_g[l])
        z = jnp.einsum("bsd,de->bse", h, w_in[l])
        qkv_a, gate_a, a_a, b_a, c_q, c_kv, k_pe, gate_br_a, gate_br_b = split_in(z)

        qkv_a = jax.nn.silu(causal_dwconv(qkv_a, conv_qkv_w[l]))
        q_a, k_a, v_a = jnp.split(qkv_a, [hA * GDN_DK, 2 * hA * GDN_DK], axis=-1)
        q_a = q_a.reshape(b, s, hA, GDN_DK)
        k_a = k_a.reshape(b, s, hA, GDN_DK)
        v_a = v_a.reshape(b, s, hA, GDN_DV)
        g_log = -jnp.exp(gdn_a_log[l].astype(jnp.float32)) * jax.nn.softplus(
            a_a.astype(jnp.float32) + gdn_dt_bias[l].astype(jnp.float32))
        beta = jax.nn.sigmoid(b_a.astype(jnp.float32))
        o_a = gdn_chunked(q_a, k_a, v_a, g_log, beta).astype(x.dtype)
        o_a = rmsnorm(o_a, gdn_norm_g[l]) * jax.nn.silu(gate_a.reshape(b, s, hA, GDN_DV))
        y_a = jnp.einsum("bse,ed->bsd", o_a.reshape(b, s, hA * GDN_DV), w_o_gdn[l])

        cq = rmsnorm(c_q, mla_q_norm_g[l])
        q_b = jnp.einsum("bsr,re->bse", cq, w_uq[l]).reshape(b, s, hB, MLA_NOPE + MLA_ROPE)
        q_nope, q_pe = q_b[..., :MLA_NOPE], rope(q_b[..., MLA_NOPE:], pos)
        ckv = rmsnorm(c_kv, mla_kv_norm_g[l])
        kv = jnp.einsum("bsr,re->bse", ckv, w_ukv[l]).reshape(b, s, hB, MLA_NOPE + MLA_V)
        k_nope, v_b = kv[..., :MLA_NOPE], kv[..., MLA_NOPE:]
        k_pe_r = rope(k_pe[:, :, None, :], pos)[:, :, 0, :]
        o_b = mla_causal(q_nope, q_pe, k_nope, k_pe_r, v_b)
        y_b = jnp.einsum("bse,ed->bsd", o_b.reshape(b, s, hB * MLA_V), w_o_mla[l])

        merged = jax.nn.sigmoid(gate_br_a) * y_a + jax.nn.sigmoid(gate_br_b) * y_b
        x = x + jnp.einsum("bsd,de->bse", merged, w_out[l])

        h = rmsnorm(x, norm_ffn_g[l])
        u = causal_dwconv(jnp.einsum("bsd,df->bsf", h, w_up[l]), conv_ffn_w[l])
        gate_f, up_f = u[..., :D_FF], u[..., D_FF:]
        x = x + jnp.einsum("bsf,fd->bsd", jax.nn.silu(gate_f) * up_f, w_down[l])
    return rmsnorm(x, norm_final_g)
```

```python
import contextlib
import numpy as np
import concourse.bass as bass
import concourse.mybir as mybir

F32 = mybir.dt.float32
BF16 = mybir.dt.bfloat16
AF = mybir.ActivationFunctionType
ALU = mybir.AluOpType
AX = mybir.AxisListType


class Buf:
    __slots__ = ("name", "w", "rs")

    def __init__(self, name):
        self.name = name
        self.w = None
        self.rs = []


class Op:
    __slots__ = ("id", "eng", "fn", "deps", "dma", "chan", "val", "sem", "need_sig")

    def __init__(self, id, eng, fn, dma=False, chan=None):
        self.id = id
        self.eng = eng
        self.fn = fn
        self.deps = {}
        self.dma = dma
        self.chan = chan
        self.val = 0
        self.need_sig = False


ENGS = ("pe", "act", "dve", "pool", "sp")


class Prog:
    def __init__(self, nc, es, n_chan=30):
        self.nc = nc
        self.ops = []
        self.nid = 0
        self.esem_sets = [{e: es.enter_context(nc.semaphore("s%d_%s" % (i, e))) for e in ENGS if e != "sp"}
                          for i in range(7)]
        self.phase = -1
        self.esem = self.esem_sets[0]
        self.ecount = {e: 0 for e in self.esem}
        self.chan_sems = {q: [(es.enter_context(nc.semaphore("d%s%d" % (q, i))), 0) for i in range(n_chan if q == "sp" else 18)]
                          for q in ("sp", "pool")}
        self.chan = {}
        self.waited = {e: {} for e in ENGS}
        self.last = {e: None for e in ENGS}
        self.barrier_deps = []
        self.bufs = []

    def buf(self, name="b"):
        b = Buf(name)
        self.bufs.append(b)
        return b

    def bufs_n(self, n, name="b"):
        return [self.buf(name + str(i)) for i in range(n)]

    def _mk(self, eng, fn, r, w, dma=False, chan=None):
        op = Op(self.nid, eng, fn, dma, chan)
        self.nid += 1
        for d in self.barrier_deps:
            op.deps[d] = True
        for b in r:
            if b.w is not None:
                op.deps[b.w] = True
        for b in w:
            if b.w is not None and b.w not in op.deps:
                op.deps[b.w] = False
            for rd in b.rs:
                if rd is not op and rd not in op.deps:
                    op.deps[rd] = False
        for b in r:
            b.rs.append(op)
        for b in w:
            b.w = op
            b.rs = []
        self.ops.append(op)
        self.last[eng] = op
        return op

    def op(self, eng, fn, r=(), w=()):
        return self._mk(eng, fn, r, w)

    def dma(self, chan, out, in_, r=(), w=(), q=None):
        if q is None:
            q = "sp" if len(w) > 0 else "pool"
        if chan not in self.chan:
            if not self.chan_sems[q]:
                raise RuntimeError("out of dma channels")
            sem, cnt = self.chan_sems[q].pop()
            self.chan[chan] = [sem, cnt, None, q]
        c = self.chan[chan]
        op = self._mk(q, lambda e, o=out, i=in_: e.dma_start(out=o, in_=i), r, w, dma=True, chan=chan)
        if c[2] is not None and c[2] not in op.deps:
            op.deps[c[2]] = True
        c[1] += 16
        op.val = c[1]
        op.sem = c[0]
        c[2] = op
        return op

    def barrier(self):
        deps = [o for o in self.last.values() if o is not None]
        deps += [c[2] for c in self.chan.values() if c[2] is not None]
        self.barrier_deps = deps
        for key, c in self.chan.items():
            self.chan_sems[c[3]].append((c[0], c[1]))
        self.chan = {}
        for b in self.bufs:
            b.w = None
            b.rs = []
        self.bufs = []

    def _needs_wait(self, op, d, is_raw):
        if d.dma:
            return True
        if d.eng != op.eng:
            return True
        if op.dma:
            return True
        if op.eng == "pe":
            return False
        return is_raw

    def emit(self):
        nc = self.nc
        ops = self.ops
        self.ops = []
        self.phase += 1
        self.esem = self.esem_sets[self.phase]
        self.ecount = {e: 0 for e in self.esem}
        lastc = {}
        for op in ops:
            if not op.dma:
                lastc[op.eng] = op
        for op in lastc.values():
            op.need_sig = True
        for op in ops:
            for d, raw in op.deps.items():
                if not d.dma and self._needs_wait(op, d, raw):
                    d.need_sig = True
        for op in ops:
            if not op.dma and op.need_sig and op.val == 0:
                self.ecount[op.eng] += 1
                op.val = self.ecount[op.eng]
                op.sem = self.esem[op.eng]
        by_eng = {e: [o for o in ops if o.eng == e] for e in ENGS}

        def run(eng_name, eng):
            waited = self.waited[eng_name]
            for op in by_eng[eng_name]:
                need = {}
                for d, raw in op.deps.items():
                    if not self._needs_wait(op, d, raw):
                        continue
                    sem = d.sem
                    assert d.val > 0, (d.id, d.eng, d.dma)
                    k = id(sem)
                    if waited.get(k, 0) >= d.val:
                        continue
                    if k not in need or need[k][1] < d.val:
                        need[k] = (sem, d.val)
                for k, (sem, val) in need.items():
                    eng.wait_ge(sem, val)
                    waited[k] = val
                ins = op.fn(eng)
                if op.dma:
                    ins.then_inc(op.sem, 16)
                elif op.need_sig:
                    ins.then_inc(op.sem, 1)

        with nc.Block() as block:
            @block.tensor
            def _(e):
                run("pe", e)

            @block.scalar
            def _(e):
                run("act", e)

            @block.vector
            def _(e):
                run("dve", e)

            @block.gpsimd
            def _(e):
                run("pool", e)

            @block.sync
            def _(e):
                run("sp", e)

    def finish(self):
        nc = self.nc
        with nc.Block() as block:
            @block.sync
            def _(e):
                for sem, cnt in self.chan_sems["sp"] + self.chan_sems["pool"] + [(c[0], c[1]) for c in self.chan.values()]:
                    if cnt > 0:
                        e.wait_ge(sem, cnt)

import ml_dtypes
from concourse.bass_utils import run_bass_kernel_spmd

D = 1024
DINP = 6928
OQ, OG, OBA, OBB, OCQ, OCKV, OKPE, OKPS, OAB = 0, 3072, 4096, 5120, 6144, 6528, 6784, 6848, 6912
DFF = 2816
EPS = 1e-6
NEG = -30000.0


class K:
    def __init__(self, P):
        self.P = P

    def act(self, out, in_, func, r, w, bias=None, scale=None, accum=None, eng="act"):
        kw = {}
        if bias is not None:
            kw["bias"] = bias
        if scale is not None:
            kw["scale"] = scale
        if accum is not None:
            kw["accum_out"] = accum
        return self.P.op(eng, lambda e: e.activation(out=out, in_=in_, func=func, **kw), r, w)

    def tt(self, eng, out, in0, in1, op, r, w):
        return self.P.op(eng, lambda e: e.tensor_tensor(out=out, in0=in0, in1=in1, op=op), r, w)

    def ts(self, eng, out, in0, s1, s2, op0, op1, r, w):
        if s2 is None:
            return self.P.op(eng, lambda e: e.tensor_scalar(out=out, in0=in0, scalar1=s1, scalar2=None, op0=op0), r, w)
        return self.P.op(eng, lambda e: e.tensor_scalar(out=out, in0=in0, scalar1=s1, scalar2=s2, op0=op0, op1=op1), r, w)

    def stt(self, eng, out, in0, scalar, in1, op0, op1, r, w):
        return self.P.op(eng, lambda e: e.scalar_tensor_tensor(out=out, in0=in0, scalar=scalar, in1=in1, op0=op0, op1=op1), r, w)

    def copy(self, eng, out, in_, r, w):
        if eng == "act":
            return self.P.op(eng, lambda e: e.activation(out=out, in_=in_, func=AF.Copy), r, w)
        return self.P.op(eng, lambda e: e.tensor_copy(out=out, in_=in_), r, w)

    def memset(self, eng, ap, val, w):
        return self.P.op(eng, lambda e: e.memset(ap, val), (), w)

    def mm(self, out, lhsT, rhs, start, stop, r, w):
        return self.P.op("pe", lambda e: e.matmul(out, lhsT=lhsT, rhs=rhs, start=start, stop=stop), r, w)

    def tr(self, out, in_, ident, r, w):
        return self.P.op("pe", lambda e: e.transpose(out, in_, ident), r, w)


class Rot:
    def __init__(self, P, tiles):
        self.t = tiles
        self.b = [P.buf() for _ in tiles]
        self.i = -1

    def next(self):
        self.i = (self.i + 1) % len(self.t)
        return self.t[self.i], self.b[self.i]


def pipelined(items, stages):
    n, ns = len(items), len(stages)
    ctx = [dict() for _ in items]
    for t in range(n + ns - 1):
        for s_ in reversed(range(ns)):
            e = t - s_
            if 0 <= e < n:
                stages[s_](items[e], ctx[e])


def phase0_convert(nc, P, k, pairs):
    with contextlib.ExitStack() as ph:
        CW = 2048
        NBUF = 6
        ft = [ph.enter_context(nc.sbuf_tensor("cv_f%d" % i, [128, CW], F32)) for i in range(NBUF)]
        bt = [ph.enter_context(nc.sbuf_tensor("cv_b%d" % i, [128, CW], BF16)) for i in range(NBUF)]
        fb = [P.buf() for _ in range(NBUF)]
        bb = [P.buf() for _ in range(NBUF)]
        engs = ["dve", "act", "dve", "act", "dve", "pool"]
        n = 0
        for src, dst in pairs:
            R_, C_ = src.shape
            for r0 in range(0, R_, 128):
                for c0 in range(0, C_, CW):
                    cw = min(CW, C_ - c0)
                    i = n % NBUF
                    P.dma("cvl%d" % i, ft[i][:, 0:cw], src[r0:r0 + 128, c0:c0 + cw], w=[fb[i]])
                    k.copy(engs[i], bt[i][:, 0:cw], ft[i][:, 0:cw], [fb[i]], [bb[i]])
                    P.dma("cvs%d" % i, dst[r0:r0 + 128, c0:c0 + cw], bt[i][:, 0:cw], r=[bb[i]])
                    n += 1
        P.emit()
    P.barrier()


def phase_A(nc, P, k, S, NS, T):
    NB = S // 512
    with contextlib.ExitStack() as ph:
        def sb(name, shape, dt):
            return ph.enter_context(nc.sbuf_tensor(name, shape, dt))

        def psum(name, shape, dt):
            return ph.enter_context(nc.psum_tensor(name, shape, dt))

        win_sb = sb("win_sb", [128, 8, DINP], BF16)
        b_win = P.buf()
        wv = T["win_bf"].rearrange("(c p) n -> p c n", p=128)
        splits = [0, 1792, 3584, 5376, DINP]
        for i in range(4):
            P.dma("wl%d" % i, win_sb[:, :, splits[i]:splits[i + 1]], wv[:, :, splits[i]:splits[i + 1]], w=[b_win])
        ident = sb("identA", [128, 128], BF16)
        ones = sb("onesA", [128, 128], BF16)
        gmix = sb("gmix", [128, D], F32)
        cw = sb("cwA", [128, 24, 4], F32)
        dtb = sb("dtb", [128, 8], F32)
        nA = sb("nA", [128, 8], F32)
        gq = sb("gq", [128, 3], F32)
        gkv = sb("gkv", [128, 2], F32)
        epst = sb("epsA", [128, 1], F32)
        onet = sb("oneA", [128, 1], F32)
        lnsc = sb("lnscA", [128, 1], F32)
        b_c = P.buf()
        P.dma("c0", ident[:], T["ident_bf"], w=[b_c])
        P.dma("c0", gmix[:], T["gmix_bc"], w=[b_c])
        P.dma("c0", cw[:], T["cw_qkv"], w=[b_c])
        P.dma("c0", dtb[:], T["dtb_bc"], w=[b_c])
        P.dma("c0", nA[:], T["alog_bc"], w=[b_c])
        P.dma("c0", gq[:], T["gq_p"], w=[b_c])
        P.dma("c0", gkv[:], T["gkv_p"], w=[b_c])
        k.memset("dve", ones[:], 1.0, [b_c])
        k.memset("dve", epst[:], EPS, [b_c])
        k.memset("dve", onet[:], 1.0, [b_c])
        k.memset("dve", lnsc[:], float(np.log(128.0 ** -0.5)), [b_c])
        k.act(nA[:], nA[:], AF.Exp, [b_c], [b_c])
        k.ts("dve", nA[:], nA[:], -1.0, None, ALU.mult, None, [b_c], [b_c])

        halo = sb("haloA", [128, 24, 3], F32)
        b_halo = [P.buf() for _ in range(24)]
        xt = Rot(P, [sb("xtA%d" % i, [128, D], F32) for i in range(2)])
        junk = sb("junkA", [128, D], BF16)
        b_junk = P.buf()
        hb = Rot(P, [sb("hbA%d" % i, [128, D], BF16) for i in range(2)])
        ss = Rot(P, [sb("ssA%d" % i, [128, 1], F32) for i in range(2)])
        hT = Rot(P, [sb("hTA%d" % i, [128, 8, 512], BF16) for i in range(2)])
        ps_t = Rot(P, [psum("pstA%d" % i, [128, 8, 128], BF16) for i in range(1)])
        ps_z = Rot(P, [psum("pszA%d" % i, [128, 512], F32) for i in range(4)])
        ps_s = Rot(P, [psum("pssA%d" % i, [128, 512], F32) for i in range(2)])
        ps_ab = Rot(P, [psum("psabA", [128, 16], F32)])
        abx = Rot(P, [sb("abxA%d" % i, [128, 8], F32) for i in range(2)])
        abt = Rot(P, [sb("abtA%d" % i, [128, 8], F32) for i in range(2)])
        abu = Rot(P, [sb("abuA%d" % i, [128, 8], F32) for i in range(2)])
        gbt = Rot(P, [sb("gbtA%d" % i, [128, 24], F32) for i in range(2)])
        zc = Rot(P, [sb("zcA%d" % i, [128, 515], F32) for i in range(3)])
        yc = Rot(P, [sb("ycA%d" % i, [128, 512], F32) for i in range(2)])
        ys = Rot(P, [sb("ysA%d" % i, [128, 512], F32) for i in range(5)])
        sq = Rot(P, [sb("sqA%d" % i, [128, 512], BF16) for i in range(2)])
        rr = Rot(P, [sb("rrA%d" % i, [128, 512], F32) for i in range(2)])
        st = Rot(P, [sb("stA%d" % i, [128, 512], BF16) for i in range(4)])
        raw = sb("rawA", [128, 3, 512], F32)
        b_raw = P.buf()
        cst = Rot(P, [sb("cosA%d" % i, [64, 512], F32) for i in range(2)])
        snt = Rot(P, [sb("sinA%d" % i, [64, 512], F32) for i in range(2)])
        t1 = sb("t1A", [64, 512], F32)
        t2 = sb("t2A", [64, 512], F32)
        b_t1, b_t2 = P.buf(), P.buf()
        nst = [0]

        def store(dst, tile, btile):
            P.dma("stA%d" % (nst[0] % 4), dst, tile, r=[btile], q="sp")
            nst[0] += 1

        def mm_chunk(pst, bpst, col0, ncols, hTt, bhT):
            for c in range(8):
                k.mm(pst[0:ncols, :], win_sb[:, c, col0:col0 + ncols], hTt[:, c, :], c == 0, c == 7,
                     [b_win, bhT], [bpst])

        def prologue(s, tb):
            t0 = tb * 512
            hTt, bhT = hT.next()
            ct, bct = cst.next()
            sn, bsn = snt.next()
            P.dma("cs0", ct[:], T["cosT"][:, t0:t0 + 512], w=[bct])
            P.dma("cs1", sn[:], T["sinT"][:, t0:t0 + 512], w=[bsn])
            for tt in range(4):
                xtt, bx = xt.next()
                P.dma("xA%d" % (xt.i), xtt[:], T["x"][s, t0 + tt * 128:t0 + (tt + 1) * 128, :], w=[bx])
                sst, bss = ss.next()
                k.act(junk[:], xtt[:], AF.Square, [bx], [b_junk, bss], accum=sst[:])
                k.act(sst[:], sst[:], AF.Ln, [bss, b_c], [bss], bias=epst[:, 0:1], scale=1.0 / D)
                k.act(sst[:], sst[:], AF.Exp, [bss], [bss], scale=-0.5)
                hbt, bhb = hb.next()
                k.stt("dve", hbt[:], xtt[:], sst[:, 0:1], gmix[:], ALU.mult, ALU.mult, [bx, bss, b_c], [bhb])
                pt, bpt = ps_t.next()
                for c in range(8):
                    k.tr(pt[:, c, :], hbt[:, c * 128:(c + 1) * 128], ident[:], [bhb, b_c], [bpt])
                k.copy("dve" if tt % 2 == 0 else "act", hTt[:, :, tt * 128:(tt + 1) * 128], pt[:], [bpt], [bhT])
                pab, bpab = ps_ab.next()
                for c in range(8):
                    k.mm(pab[:], hTt[:, c, tt * 128:(tt + 1) * 128], win_sb[:, c, OAB:OAB + 16], c == 0, c == 7,
                         [b_win, bhT], [bpab])
                ax, bax = abx.next()
                at, bat = abt.next()
                au, bau = abu.next()
                gt, bgt = gbt.next()
                k.tt("dve", ax[:], pab[:, 0:8], dtb[:], ALU.add, [bpab, b_c], [bax])
                k.act(at[:], ax[:], AF.Abs, [bax], [bat])
                k.act(at[:], at[:], AF.Exp, [bat], [bat], scale=-1.0)
                k.act(at[:], at[:], AF.Ln, [bat, b_c], [bat], bias=onet[:, 0:1])
                k.ts("dve", au[:], ax[:], 0.0, None, ALU.max, None, [bax], [bau])
                k.tt("dve", au[:], au[:], at[:], ALU.add, [bau, bat], [bau])
                k.tt("dve", gt[:, 0:8], au[:], nA[:], ALU.mult, [bau, b_c], [bgt])
                k.act(gt[:, 8:16], pab[:, 8:16], AF.Sigmoid, [bpab], [bgt])
                k.act(gt[:, 16:24], gt[:, 8:16], AF.Ln, [bgt], [bgt])
                P.dma("gbtA%d" % gbt.i, T["gbt_s"][s, t0 + tt * 128:t0 + (tt + 1) * 128, :], gt[:], r=[bgt])
            return (hTt, bhT, ct, bct, sn, bsn)

        blocks = [(s, tb) for s in range(NS) for tb in range(NB)]
        pro = {0: prologue(*blocks[0])}
        for bi, (s, tb) in enumerate(blocks):
            t0 = tb * 512
            if bi + 1 < len(blocks):
                pro[bi + 1] = prologue(*blocks[bi + 1])
            hTt, bhT, ct, bct, sn, bsn = pro.pop(bi)

            def q0(e, cx):
                cx["pz"] = ps_z.next()
                mm_chunk(cx["pz"][0], cx["pz"][1], OQ + e * 128, 128, hTt, bhT)

            def q1(e, cx):
                pz, bpz = cx["pz"]
                z, bz = zc.next()
                if tb == 0:
                    k.memset("pool", z[:, 0:3], 0.0, [bz])
                else:
                    k.copy("pool", z[:, 0:3], halo[:, e, :], [b_halo[e]], [bz])
                k.copy("act", z[:, 3:515], pz[:], [bpz], [bz])
                cx["z"] = (z, bz)

            def q2(e, cx):
                z, bz = cx["z"]
                y, by = yc.next()
                k.ts("dve", y[:], z[:, 0:512], cw[:, e, 0:1], None, ALU.mult, None, [bz, b_c], [by])
                for j in range(1, 4):
                    k.stt("dve", y[:], z[:, j:j + 512], cw[:, e, j:j + 1], y[:], ALU.mult, ALU.add, [bz, by, b_c], [by])
                k.copy("pool", halo[:, e, :], z[:, 512:515], [bz], [b_halo[e]])
                cx["y"] = (y, by)

            def q3(e, cx):
                y, by = cx["y"]
                if e >= 16:
                    so, bso = st.next()
                    k.act(so[:], y[:], AF.Silu, [by], [bso])
                    store(T["vT_s"][s, e % 8, :, t0:t0 + 512], so[:], bso)
                else:
                    yy, byy = ys.next()
                    k.act(yy[:], y[:], AF.Silu, [by], [byy])
                    sqt, bsq = sq.next()
                    k.tt("pool", sqt[:], yy[:], yy[:], ALU.mult, [byy], [bsq])
                    cx["yy"] = (yy, byy); cx["sq"] = (sqt, bsq)

            def q4(e, cx):
                if e >= 16:
                    return
                sqt, bsq = cx["sq"]
                pss, bps = ps_s.next()
                k.mm(pss[:], ones[:], sqt[:], True, True, [bsq, b_c], [bps])
                cx["pss"] = (pss, bps)

            def q5(e, cx):
                if e >= 16:
                    return
                pss, bps = cx["pss"]
                rt, brt = rr.next()
                k.act(rt[:], pss[:], AF.Ln, [bps, b_c], [brt], bias=epst[:, 0:1])
                if e < 8:
                    k.act(rt[:], rt[:], AF.Exp, [brt, b_c], [brt], scale=-0.5, bias=lnsc[:, 0:1])
                else:
                    k.act(rt[:], rt[:], AF.Exp, [brt], [brt], scale=-0.5)
                cx["rt"] = (rt, brt)

            def q6(e, cx):
                if e >= 16:
                    return
                yy, byy = cx["yy"]; rt, brt = cx["rt"]
                so, bso = st.next()
                k.tt("dve", so[:], yy[:], rt[:], ALU.mult, [byy, brt], [bso])
                dst = T["qT_s"] if e < 8 else T["kT_s"]
                store(dst[s, e % 8, :, t0:t0 + 512], so[:], bso)

            pipelined(list(range(24)), [q0, q1, q2, q3, q4, q5, q6])

            def g0(e, cx):
                cx["pz"] = ps_z.next()
                mm_chunk(cx["pz"][0], cx["pz"][1], OG + e * 128, 128, hTt, bhT)

            def g1(e, cx):
                pz, bpz = cx["pz"]
                so, bso = st.next()
                k.act(so[:], pz[:], AF.Silu if e < 8 else AF.Sigmoid, [bpz], [bso])
                if e < 8:
                    dst = T["gateT_s"][s, e * 128:(e + 1) * 128, t0:t0 + 512]
                elif e < 16:
                    dst = T["gbaT_s"][s, (e - 8) * 128:(e - 7) * 128, t0:t0 + 512]
                else:
                    dst = T["gbbT_s"][s, (e - 16) * 128:(e - 15) * 128, t0:t0 + 512]
                store(dst, so[:], bso)

            pipelined(list(range(24)), [g0, g1])

            for (col, nch, gg, dstn) in ((OCQ, 3, gq, "cqT_s"), (OCKV, 2, gkv, "ckvT_s")):
                pss, bps = ps_s.next()
                for j in range(nch):
                    pz, bpz = ps_z.next()
                    mm_chunk(pz, bpz, col + j * 128, 128, hTt, bhT)
                    k.copy("act", raw[:, j, :], pz[:], [bpz], [b_raw])
                    sqt, bsq = sq.next()
                    k.tt("pool", sqt[:], raw[:, j, :], raw[:, j, :], ALU.mult, [b_raw], [bsq])
                    k.mm(pss[:], ones[:], sqt[:], j == 0, j == nch - 1, [bsq, b_c], [bps])
                rt, brt = rr.next()
                k.act(rt[:], pss[:], AF.Ln, [bps, b_c], [brt], bias=epst[:, 0:1], scale=1.0 / (128 * nch))
                k.act(rt[:], rt[:], AF.Exp, [brt], [brt], scale=-0.5)
                for j in range(nch):
                    so, bso = st.next()
                    k.stt("dve", so[:], raw[:, j, :], gg[:, j:j + 1], rt[:], ALU.mult, ALU.mult, [b_raw, brt, b_c], [bso])
                    store(T[dstn][s, j * 128:(j + 1) * 128, t0:t0 + 512], so[:], bso)
            pa, bpa = ps_z.next()
            mm_chunk(pa, bpa, OKPE, 64, hTt, bhT)
            pb, bpb = ps_z.next()
            mm_chunk(pb, bpb, OKPS, 64, hTt, bhT)
            k.tt("dve", t1[:], pa[0:64, :], ct[:], ALU.mult, [bpa, bct], [b_t1])
            k.tt("dve", t2[:], pb[0:64, :], sn[:], ALU.mult, [bpb, bsn], [b_t2])
            so, bso = st.next()
            k.tt("pool", so[0:64, :], t1[:], t2[:], ALU.add, [b_t1, b_t2], [bso])
            store(T["kpeT_s"][s, :, t0:t0 + 512], so[0:64, :], bso)
        P.emit()
    P.barrier()


class StopBuild(Exception):
    pass


KSTOP = [0]


def kstop(n):
    if KSTOP[0] == n:
        raise StopBuild()


def phase_G(nc, P, k, S, NS, T):
    NB = S // 512
    with contextlib.ExitStack() as ph:
        def sb(name, shape, dt):
            return ph.enter_context(nc.sbuf_tensor(name, shape, dt))

        def psum(name, shape, dt):
            return ph.enter_context(nc.psum_tensor(name, shape, dt))

        ident = sb("identG", [128, 128], BF16)
        identf = sb("identfG", [128, 128], F32)
        ones = sb("onesG", [128, 128], BF16)
        onesf = sb("onesfG", [128, 128], F32)
        utri = sb("utriG", [128, 128], F32)
        mS = sb("mSG", [128, 128], F32)
        mST = sb("mSTG", [128, 128], F32)
        mIT = sb("mITG", [128, 128], F32)
        ggdn = sb("ggdnG", [128, 1], F32)
        epst = sb("epsG", [128, 1], F32)
        b_c = P.buf()
        for dst, src in ((ident, "ident_bf"), (identf, "ident_f"), (utri, "utri"), (mS, "m_strict"),
                         (mST, "m_strictT"), (mIT, "m_inclT"), (ggdn, "ggdn_p")):
            P.dma("c0", dst[:], T[src], w=[b_c])
        k.memset("dve", ones[:], 1.0, [b_c])
        k.memset("dve", onesf[:], 1.0, [b_c])
        k.memset("dve", epst[:], EPS, [b_c])

        Sf = sb("SfG", [128, 8, 128], F32)
        Sb = sb("SbG", [128, 8, 128], BF16)
        kblk = Rot(P, [sb("kblkG%d" % i, [128, 8, 256], BF16) for i in range(2)])
        qblk = Rot(P, [sb("qblkG%d" % i, [128, 8, 256], BF16) for i in range(2)])
        vblk = Rot(P, [sb("vblkG%d" % i, [128, 8, 256], BF16) for i in range(2)])
        gblk = Rot(P, [sb("gblkG%d" % i, [128, 8, 256], BF16) for i in range(2)])
        oblk = Rot(P, [sb("oblkG%d" % i, [128, 8, 256], BF16) for i in range(2)])
        gbt = Rot(P, [sb("gbtG%d" % i, [128, 24], F32) for i in range(2)])
        dg = Rot(P, [sb("dgG%d" % i, [128, 4, 128], F32) for i in range(2)])
        psG = psum("psGG", [128, 4, 128], F32)
        b_psG = P.buf()
        pT = Rot(P, [psum("psTG", [128, 8, 128], BF16)])
        pwP = Rot(P, [psum("pswPG%d" % i, [128, 4, 128], F32) for i in range(2)])
        pwD = Rot(P, [psum("pswDG%d" % i, [128, 4, 128], F32) for i in range(2)])
        pwA = pwD
        pwR = Rot(P, [psum("pswRG%d" % i, [128, 4, 128], F32) for i in range(2)])
        identf4 = sb("identf4G", [128, 4, 128], F32)
        for j in range(4):
            k.copy("pool", identf4[:, j, :], identf[:], [b_c], [b_c])

        def rot(name, n, dt):
            return Rot(P, [sb("%sG%d" % (name, i), [128, 4, 128], dt) for i in range(n)])

        sc3 = Rot(P, [sb("sc3G%d" % i, [128, 64], F32) for i in range(3)])
        kbg = rot("kbg", 6, BF16); kdec = rot("kdec", 6, BF16); vb = rot("vb", 6, BF16)
        aT = rot("aT", 6, BF16); qd = rot("qd", 6, BF16)
        tmp = rot("tmp", 3, F32); Et = rot("E", 3, F32); Eg = rot("Eg", 2, F32)
        def set4(name, n_):
            t = [[[sb("%sG%d_%d_%d" % (name, p_, h, i), [128, 4, 128], BF16) for i in range(n_)] for h in range(2)] for p_ in range(2)]
            b = [[[P.buf() for i in range(n_)] for h in range(2)] for p_ in range(2)]
            return t, b
        Pk, bPk = set4("Pk", 2); Qk, bQk = set4("Qk", 2); Ak, bAk = set4("Ak", 2); Bk, bBk = set4("Bk", 2)
        NTb = [[sb("NTbG%d_%d" % (p_, h), [128, 4, 128], BF16) for h in range(2)] for p_ in range(2)]
        bNT = [[P.buf() for h in range(2)] for p_ in range(2)]
        Xk = [[sb("XkG%d_%d" % (p_, h), [128, 4, 128], BF16) for h in range(2)] for p_ in range(2)]
        bXk = [[P.buf() for h in range(2)] for p_ in range(2)]
        maskd = sb("maskdG", [128, 4, 128], F32); mX1 = sb("mX1G", [128, 4, 128], F32)
        mX2 = sb("mX2G", [128, 4, 128], F32); mX3 = sb("mX3G", [128, 4, 128], F32)
        for dst_, src_ in ((maskd, "maskd4"), (mX1, "mX1_4"), (mX2, "mX2_4"), (mX3, "mX3_4")):
            P.dma("c0", dst_[:], T[src_], w=[b_c])
        TT = rot("TT", 4, BF16); uf = rot("uf", 2, F32); nwT = rot("nwT", 2, BF16)
        vn = rot("vn", 2, BF16); of = rot("of", 2, F32); sq = rot("sq", 2, BF16)
        rs = rot("rs", 2, F32); on = rot("on", 2, F32)
        b_Sg = [P.buf() for _ in range(2)]
        b_Sbg = [P.buf() for _ in range(2)]
        G, GBc, NG, NBETA, BEG, EDEC, EGL = 0, 16, 8, 24, 32, 40, 48

        tiles = [(s, n) for s in range(NS) for n in range(S // 128)]
        blk = {}
        ctx = {}

        def load_block(s, tb):
            t0 = tb * 256
            kb_, bkb = kblk.next(); qb_, bqb = qblk.next(); vb_, bvb = vblk.next()
            gb_, bgb = gblk.next(); ob_, bob = oblk.next()
            P.dma("gk%d" % kblk.i, kb_[:], T["kT_s"][s].rearrange("h d t -> d h t")[:, :, t0:t0 + 256], w=[bkb])
            P.dma("gq%d" % qblk.i, qb_[:], T["qT_s"][s].rearrange("h d t -> d h t")[:, :, t0:t0 + 256], w=[bqb])
            P.dma("gv%d" % vblk.i, vb_[:], T["vT_s"][s].rearrange("h d t -> d h t")[:, :, t0:t0 + 256], w=[bvb])
            P.dma("gg%d" % gblk.i, gb_[:], T["gateT_s"][s].rearrange("(h d) t -> d h t", d=128)[:, :, t0:t0 + 256], w=[bgb])
            blk[(s, tb)] = dict(k=(kb_, bkb), q=(qb_, bqb), v=(vb_, bvb), g=(gb_, bgb), o=(ob_, bob))

        def stream_P(ti):
            s, n = tiles[ti]
            tb, tt = n // 2, n % 2
            if tt == 0:
                load_block(s, tb)
            B = blk[(s, tb)]
            kb_, bkb = B["k"]; qb_, bqb = B["q"]; vb_, bvb = B["v"]
            c0, c1 = tt * 128, (tt + 1) * 128
            par = ti % 2
            cx = ctx[ti] = dict(par=par, B=B, c0=c0, c1=c1, s=s, n=n)
            gt, bgt = gbt.next()
            P.dma("gt%d" % gbt.i, gt[:], T["gbt_s"][s, n * 128:(n + 1) * 128, :], w=[bgt])
            sct, bsc = sc3.next()
            cx["sc"] = (sct, bsc)
            pg, bpg = pwP.next()
            k.mm(pg[:, 0, 0:8], utri[:], gt[:, 0:8], True, True, [b_c, bgt], [bpg])
            k.mm(pg[:, 1, 0:8], onesf[:], gt[:, 0:8], True, True, [b_c, bgt], [bpg])
            k.copy("dve", sct[:, 0:8], pg[:, 0, 0:8], [bpg], [bsc])
            k.ts("dve", sct[:, 24:32], gt[:, 8:16], -1.0, None, ALU.mult, None, [bgt], [bsc])
            k.act(sct[:, 56:64], pg[:, 0, 0:8], AF.Exp, [bpg], [bsc])
            k.tt("dve", sct[:, 32:40], gt[:, 8:16], sct[:, 56:64], ALU.mult, [bgt, bsc], [bsc])
            k.tt("dve", sct[:, 40:48], pg[:, 1, 0:8], sct[:, 0:8], ALU.subtract, [bpg, bsc], [bsc])
            k.act(sct[:, 40:48], sct[:, 40:48], AF.Exp, [bsc], [bsc])
            k.act(sct[:, 48:56], pg[:, 1, 0:8], AF.Exp, [bpg], [bsc])
            yield
            for hh in range(2):
                H0 = hh * 4
                dgt, bdg = dg.next()
                for j in range(4):
                    h = H0 + j
                    k.ts("dve", dgt[:, j, :], identf[:], sct[:, G + h:G + h + 1], None, ALU.mult, None, [b_c, bsc], [bdg])
                k.mm(psG[:], onesf[:], dgt[:], True, True, [b_c, bdg], [b_psG])
                p1, bp1 = pT.next()
                for j in range(4):
                    k.tr(p1[:, j, :], kb_[:, H0 + j, c0:c1], ident[:], [bkb, b_c], [bp1])
                    k.tr(p1[:, 4 + j, :], vb_[:, H0 + j, c0:c1], ident[:], [bvb, b_c], [bp1])
                kbg_, bkbg = kbg.next(); kdec_, bkdec = kdec.next(); vb2, bvb2 = vb.next()
                for j in range(4):
                    h = H0 + j
                    k.ts("dve", kbg_[:, j, :], p1[:, j, :], sct[:, BEG + h:BEG + h + 1], None, ALU.mult, None, [bp1, bsc], [bkbg])
                    k.ts("dve", kdec_[:, j, :], p1[:, j, :], sct[:, EDEC + h:EDEC + h + 1], None, ALU.mult, None, [bp1, bsc], [bkdec])
                    k.ts("dve", vb2[:, j, :], p1[:, 4 + j, :], gt[:, 8 + h:9 + h], None, ALU.mult, None, [bp1, bgt], [bvb2])
                yield
                pKK, bKK = pwP.next()
                for j in range(4):
                    k.mm(pKK[:, j, :], kb_[:, H0 + j, c0:c1], kb_[:, H0 + j, c0:c1], True, True, [bkb], [bKK])
                pQK, bQK = pwP.next()
                for j in range(4):
                    k.mm(pQK[:, j, :], kb_[:, H0 + j, c0:c1], qb_[:, H0 + j, c0:c1], True, True, [bkb, bqb], [bQK])
                t_, bt_ = tmp.next(); e_, be_ = Et.next()
                for j in range(4):
                    h = H0 + j
                    k.stt("dve", t_[:, j, :], psG[:, j, :], sct[:, G + h:G + h + 1], mS[:], ALU.subtract, ALU.subtract, [b_psG, bsc, b_c], [bt_])
                k.act(e_[:], t_[:], AF.Exp, [bt_], [be_], scale=-1.0)
                for j in range(4):
                    h = H0 + j
                    k.stt("dve", NTb[par][hh][:, j, :], pKK[:, j, :], sct[:, NBETA + h:NBETA + h + 1], e_[:, j, :], ALU.mult, ALU.mult,
                          [bKK, bsc, be_], [bNT[par][hh]])
                yield
                t_, bt_ = tmp.next(); e_, be_ = Et.next()
                for j in range(4):
                    h = H0 + j
                    k.stt("dve", t_[:, j, :], psG[:, j, :], sct[:, G + h:G + h + 1], mIT[:], ALU.subtract, ALU.add, [b_psG, bsc, b_c], [bt_])
                k.act(e_[:], t_[:], AF.Exp, [bt_], [be_])
                aT_, baT = aT.next()
                k.tt("dve", aT_[:], pQK[:], e_[:], ALU.mult, [bQK, be_], [baT])
                eg_, beg = Eg.next()
                k.act(eg_[:], psG[:], AF.Exp, [b_psG], [beg])
                qd_, bqd = qd.next()
                k.tt("pool", qd_[:], qb_[:, H0:H0 + 4, c0:c1], eg_[:], ALU.mult, [bqb, beg], [bqd])
                pN, bpN = pT.next()
                for j in range(4):
                    k.tr(pN[:, j, :], NTb[par][hh][:, j, :], ident[:], [bNT[par][hh], b_c], [bpN])
                k.tt("dve", Pk[par][hh][0][:], pN[:, 0:4, :], maskd[:], ALU.mult, [bpN, b_c], [bPk[par][hh][0]])
                k.tt("pool", Qk[par][hh][0][:], NTb[par][hh][:], maskd[:], ALU.mult, [bNT[par][hh], b_c], [bQk[par][hh][0]])
                k.tt("pool", Ak[par][hh][0][:], Pk[par][hh][0][:], identf4[:], ALU.add, [bPk[par][hh][0], b_c], [bAk[par][hh][0]])
                k.tt("pool", Bk[par][hh][0][:], Qk[par][hh][0][:], identf4[:], ALU.add, [bQk[par][hh][0], b_c], [bBk[par][hh][0]])
                cx[hh] = dict(kbg=(kbg_, bkbg), kdec=(kdec_, bkdec), vb=(vb2, bvb2), aT=(aT_, baT), qd=(qd_, bqd))
                yield

        def stream_D(ti):
            cx = ctx[ti]
            par = cx["par"]

            def grp(hh):
                return (Pk[par][hh], Qk[par][hh], Ak[par][hh], Bk[par][hh], bPk[par][hh], bQk[par][hh], bAk[par][hh], bBk[par][hh])

            def mm4(ps_, bps_, lhs, blhs, rhs, brhs):
                for j in range(4):
                    k.mm(ps_[:, j, :], lhs[:, j, :], rhs[:, j, :], True, True, [blhs, brhs], [bps_])

            for hh in range(2):
                Pc, Qc, Ac, Bc, bPc, bQc, bAc, bBc = grp(hh)
                p_, bp_ = pwD.next(); mm4(p_, bp_, Qc[0], bQc[0], Pc[0], bPc[0])
                q_, bq_ = pwD.next(); mm4(q_, bq_, Pc[0], bPc[0], Qc[0], bQc[0])
                k.copy("act", Pc[1][:], p_[:], [bp_], [bPc[1]])
                k.copy("dve", Qc[1][:], q_[:], [bq_], [bQc[1]])
                yield
            pa, qa, aa = 1, 1, 0
            for lvl in (1, 2, 3):
                for hh in range(2):
                    Pc, Qc, Ac, Bc, bPc, bQc, bAc, bBc = grp(hh)
                    a_, ba_ = pwD.next(); mm4(a_, ba_, Qc[qa], bQc[qa], Ac[aa], bAc[aa])
                    b_, bb_ = pwD.next(); mm4(b_, bb_, Ac[aa], bAc[aa], Qc[qa], bQc[qa])
                    k.tt("dve", Ac[1 - aa][:], a_[:], Ac[aa][:], ALU.add, [ba_, bAc[aa]], [bAc[1 - aa]])
                    k.tt("dve", Bc[1 - aa][:], b_[:], Bc[aa][:], ALU.add, [bb_, bBc[aa]], [bBc[1 - aa]])
                    if lvl < 3:
                        q_, bq_ = pwD.next(); mm4(q_, bq_, Pc[pa], bPc[pa], Qc[qa], bQc[qa])
                        if lvl < 2:
                            p_, bp_ = pwD.next(); mm4(p_, bp_, Qc[qa], bQc[qa], Pc[pa], bPc[pa])
                            k.copy("act", Pc[1 - pa][:], p_[:], [bp_], [bPc[1 - pa]])
                        k.copy("act", Qc[1 - qa][:], q_[:], [bq_], [bQc[1 - qa]])
                    yield
                aa = 1 - aa
                pa, qa = 1 - pa, 1 - qa
            for lvl in (1, 2, 3):
                mX = (mX1, mX2, mX3)[lvl - 1]
                for hh in range(2):
                    Pc, Qc, Ac, Bc, bPc, bQc, bAc, bBc = grp(hh)
                    x_, bx_ = pwD.next(); mm4(x_, bx_, NTb[par][hh], bNT[par][hh], Ac[aa], bAc[aa])
                    k.tt("dve", Xk[par][hh][:], x_[:], mX[:], ALU.mult, [bx_, b_c], [bXk[par][hh]])
                    yield
                for hh in range(2):
                    Pc, Qc, Ac, Bc, bPc, bQc, bAc, bBc = grp(hh)
                    u_, bu_ = pwD.next(); mm4(u_, bu_, Bc[aa], bBc[aa], Xk[par][hh], bXk[par][hh])
                    if lvl < 3:
                        k.tt("dve", Ac[1 - aa][:], u_[:], Ac[aa][:], ALU.add, [bu_, bAc[aa]], [bAc[1 - aa]])
                        v_, bv_ = pwD.next(); mm4(v_, bv_, Xk[par][hh], bXk[par][hh], Bc[aa], bBc[aa])
                        k.tt("dve", Bc[1 - aa][:], v_[:], Bc[aa][:], ALU.add, [bv_, bBc[aa]], [bBc[1 - aa]])
                    else:
                        TT_, bTT = TT.next()
                        k.tt("dve", TT_[:], u_[:], Ac[aa][:], ALU.add, [bu_, bAc[aa]], [bTT])
                        cx[hh]["TT"] = (TT_, bTT)
                    yield
                aa = 1 - aa

        def stream_R(ti):
            cx = ctx[ti]
            sct, bsc = cx["sc"]
            B = cx["B"]; c0, c1 = cx["c0"], cx["c1"]
            gb_, bgb = B["g"]; ob_, bob = B["o"]
            for hh in range(2):
                H0 = hh * 4
                TT_, bTT = cx[hh]["TT"]
                vb2, bvb2 = cx[hh]["vb"]; kbg_, bkbg = cx[hh]["kbg"]
                aT_, baT = cx[hh]["aT"]; qd_, bqd = cx[hh]["qd"]; kdec_, bkdec = cx[hh]["kdec"]
                pu, bpu = pwR.next()
                for j in range(4):
                    k.mm(pu[:, j, :], TT_[:, j, :], vb2[:, j, :], True, True, [bTT, bvb2], [bpu])
                uf_, buf_ = uf.next()
                k.copy("act", uf_[:], pu[:], [bpu], [buf_])
                pW, bpW = pwR.next()
                for j in range(4):
                    k.mm(pW[:, j, :], kbg_[:, j, :], TT_[:, j, :], True, True, [bkbg, bTT], [bpW])
                nw_, bnw = nwT.next()
                k.ts("dve", nw_[:], pW[:], -1.0, None, ALU.mult, None, [bpW], [bnw])
                yield
                pws, bpws = pwR.next()
                for j in range(4):
                    k.mm(pws[:, j, :], nw_[:, j, :], Sb[:, H0 + j, :], True, True, [bnw, b_Sbg[hh]], [bpws])
                vn_, bvn = vn.next()
                k.tt("dve", vn_[:], pws[:], uf_[:], ALU.add, [bpws, buf_], [bvn])
                po, bpo = pwR.next()
                for j in range(4):
                    k.mm(po[:, j, :], Sb[:, H0 + j, :], qd_[:, j, :], True, False, [b_Sbg[hh], bqd], [bpo])
                    k.mm(po[:, j, :], vn_[:, j, :], aT_[:, j, :], False, True, [bvn, baT], [bpo])
                of_, bof = of.next()
                k.copy("act", of_[:], po[:], [bpo], [bof])
                yield
                pS, bpS = pwR.next()
                for j in range(4):
                    k.mm(pS[:, j, :], kdec_[:, j, :], vn_[:, j, :], True, True, [bkdec, bvn], [bpS])
                for j in range(4):
                    h = H0 + j
                    k.stt("dve", Sf[:, h, :], Sf[:, h, :], sct[:, EGL + h:EGL + h + 1], pS[:, j, :], ALU.mult, ALU.add,
                          [b_Sg[hh], bsc, bpS], [b_Sg[hh]])
                k.copy("act", Sb[:, H0:H0 + 4, :], Sf[:, H0:H0 + 4, :], [b_Sg[hh]], [b_Sbg[hh]])
                sq_, bsq = sq.next()
                k.tt("pool", sq_[:], of_[:], of_[:], ALU.mult, [bof], [bsq])
                yield
                pss, bpss = pwR.next()
                k.mm(pss[:], ones[:], sq_[:], True, True, [b_c, bsq], [bpss])
                rs_, brs = rs.next()
                k.act(rs_[:], pss[:], AF.Ln, [bpss, b_c], [brs], bias=epst[:, 0:1], scale=1.0 / 128)
                k.act(rs_[:], rs_[:], AF.Exp, [brs], [brs], scale=-0.5)
                on_, bon = on.next()
                k.stt("dve", on_[:], of_[:], ggdn[:, 0:1], rs_[:], ALU.mult, ALU.mult, [bof, b_c, brs], [bon])
                k.tt("pool", ob_[:, H0:H0 + 4, c0:c1], on_[:], gb_[:, H0:H0 + 4, c0:c1], ALU.mult, [bon, bgb], [bob])
                yield
            s, n = cx["s"], cx["n"]
            if n % 2 == 1:
                t0 = (n // 2) * 256
                P.dma("go%d" % (n // 2 % 2), T["oaT_s"][s].rearrange("(h d) t -> d h t", d=128)[:, :, t0:t0 + 256], ob_[:], r=[bob])

        def drain(g):
            for _ in g:
                pass

        def interleave(main, others):
            others = [o for o in others if o is not None]
            for _ in main:
                for o in list(others):
                    try:
                        next(o)
                    except StopIteration:
                        others.remove(o)
            for o in others:
                drain(o)

        NTt = len(tiles)
        for ti in range(NTt):
            s, n = tiles[ti]
            if n == 0:
                if ti > 0:
                    drain(stream_R(ti - 1))
                for hh in range(2):
                    k.memset("dve", Sf[:, hh * 4:hh * 4 + 4, :], 0.0, [b_Sg[hh]])
                    k.memset("pool", Sb[:, hh * 4:hh * 4 + 4, :], 0.0, [b_Sbg[hh]])
                drain(stream_P(ti))
            nxtP = stream_P(ti + 1) if (ti + 1 < NTt and tiles[ti + 1][1] != 0) else None
            prvR = stream_R(ti - 1) if (ti > 0 and n != 0) else None
            interleave(stream_D(ti), [nxtP, prvR])
        drain(stream_R(NTt - 1))
        P.emit()
    P.barrier()


def phase_M(nc, P, k, S, NS, T):
    NB = S // 512
    NT = S // 128
    scale = float(192 ** -0.5)
    with contextlib.ExitStack() as ph:
        def sb(name, shape, dt):
            return ph.enter_context(nc.sbuf_tensor(name, shape, dt))

        def psum(name, shape, dt):
            return ph.enter_context(nc.psum_tensor(name, shape, dt))

        ident = sb("identM", [128, 128], BF16)
        tri = sb("triM", [128, 128], BF16)
        wuq = sb("wuqM", [128, 3, 2048], BF16)
        wuk = sb("wukM", [128, 2, 1024], BF16)
        wuv = sb("wuvM", [128, 2, 1024], BF16)
        cosT = sb("cosM", [64, S], F32)
        sinT = sb("sinM", [64, S], F32)
        b_c = P.buf()
        P.dma("c0", ident[:], T["ident_bf"], w=[b_c])
        P.dma("c0", tri[:], T["tri_bf"], w=[b_c])
        P.dma("c1", wuq[:], T["wuq_bf"].rearrange("(c p) n -> p c n", p=128), w=[b_c])
        P.dma("c2", wuk[:], T["wuk_bf"].rearrange("(c p) n -> p c n", p=128), w=[b_c])
        P.dma("c3", wuv[:], T["wuv_bf"].rearrange("(c p) n -> p c n", p=128), w=[b_c])
        P.dma("c1", cosT[:], T["cosT"], w=[b_c])
        P.dma("c2", sinT[:], T["sinT"], w=[b_c])
        cq = sb("cqM", [128, 3, S], BF16)
        ckv = sb("ckvM", [128, 2, S], BF16)
        kpe = sb("kpeM", [128, S], BF16)
        b_in = P.buf()
        qn = sb("qnM", [128, S], BF16)
        qr = sb("qrM", [128, S], BF16)
        kn = sb("knM", [128, S], BF16)
        vv = sb("vvM", [128, NT, 128], BF16)
        b_qn, b_qr, b_kn, b_vv = P.buf(), P.buf(), P.buf(), P.buf()
        b_pad = P.buf()
        k.memset("pool", kpe[64:128, :], 0.0, [b_pad])
        k.memset("pool", qr[64:128, :], 0.0, [b_pad])
        t1 = Rot(P, [sb("t1M%d" % i, [64, 512], F32) for i in range(2)])
        t2 = Rot(P, [sb("t2M%d" % i, [64, 512], F32) for i in range(2)])
        pT = Rot(P, [sb("pTM%d" % i, [128, 512], BF16) for i in range(4)])
        rec = Rot(P, [sb("recM%d" % i, [128, 512], F32) for i in range(2)])
        acc = Rot(P, [sb("accM%d" % i, [128, 512], F32) for i in range(2)])
        onesf = sb("onesfM", [128, 128], F32)
        k.memset("dve", onesf[:], 1.0, [b_c])
        obT = Rot(P, [sb("obTM%d" % i, [128, 512], BF16) for i in range(2)])
        ps_s = Rot(P, [psum("pssM%d" % i, [128, 512], F32) for i in range(4)])
        po = Rot(P, [psum("poM%d" % i, [128, 512], F32) for i in range(2)])
        prs_r = Rot(P, [psum("prsM", [128, 512], F32)])
        ppv = Rot(P, [psum("ppvM", [128, 4, 128], F32)])

        for s in range(NS):
            P.dma("mi0", cq[:], T["cqT_s"][s].rearrange("(c p) t -> p c t", p=128), w=[b_in])
            P.dma("mi1", ckv[:], T["ckvT_s"][s].rearrange("(c p) t -> p c t", p=128), w=[b_in])
            P.dma("mi2", kpe[0:64, :], T["kpeT_s"][s], w=[b_in])
            for h in range(8):
                for tb in range(NB):
                    t0 = tb * 512
                    pz, bpz = ps_s.next()
                    for r in range(3):
                        k.mm(pz[:], wuq[:, r, h * 256:h * 256 + 128], cq[:, r, t0:t0 + 512], r == 0, r == 2, [b_c, b_in], [bpz])
                    k.copy("act", qn[:, t0:t0 + 512], pz[:], [bpz], [b_qn])
                    pa, bpa = ps_s.next()
                    for r in range(3):
                        k.mm(pa[0:64, :], wuq[:, r, h * 256 + 128:h * 256 + 192], cq[:, r, t0:t0 + 512], r == 0, r == 2, [b_c, b_in], [bpa])
                    a1, ba1 = t1.next()
                    k.tt("dve", a1[:], pa[0:64, :], cosT[:, t0:t0 + 512], ALU.mult, [bpa, b_c], [ba1])
                    pb, bpb = ps_s.next()
                    for r in range(3):
                        k.mm(pb[0:64, :], wuq[:, r, h * 256 + 192:h * 256 + 256], cq[:, r, t0:t0 + 512], r == 0, r == 2, [b_c, b_in], [bpb])
                    a2, ba2 = t2.next()
                    k.tt("dve", a2[:], pb[0:64, :], sinT[:, t0:t0 + 512], ALU.mult, [bpb, b_c], [ba2])
                    k.tt("pool", qr[0:64, t0:t0 + 512], a1[:], a2[:], ALU.add, [ba1, ba2], [b_qr])
                    pk, bpk = ps_s.next()
                    for r in range(2):
                        k.mm(pk[:], wuk[:, r, h * 128:(h + 1) * 128], ckv[:, r, t0:t0 + 512], r == 0, r == 1, [b_c, b_in], [bpk])
                    k.copy("act", kn[:, t0:t0 + 512], pk[:], [bpk], [b_kn])
                    pv, bpv = ppv.next()
                    for tt in range(4):
                        for r in range(2):
                            k.mm(pv[:, tt, :], ckv[:, r, t0 + tt * 128:t0 + (tt + 1) * 128], wuv[:, r, h * 128:(h + 1) * 128],
                                 r == 0, r == 1, [b_c, b_in], [bpv])
                    k.copy("dve", vv[:, tb * 4:(tb + 1) * 4, :], pv[:], [bpv], [b_vv])
                items = [(qg, kb) for qg in range(NB) for kb in range(4 * (qg + 1))]

                def st_S(it, cx):
                    qg, kb = it
                    q0 = qg * 512
                    r_ = kb - 4 * qg
                    qlo = 128 * r_ if r_ > 0 else 0
                    pss, bps = ps_s.next()
                    k.mm(pss[:, qlo:512], kn[:, kb * 128:(kb + 1) * 128], qn[:, q0 + qlo:q0 + 512], True, False, [b_kn, b_qn], [bps])
                    k.mm(pss[:, qlo:512], kpe[:, kb * 128:(kb + 1) * 128], qr[:, q0 + qlo:q0 + 512], False, True, [b_in, b_qr, b_pad], [bps])
                    cx["pss"] = (pss, bps); cx["r"] = r_; cx["qlo"] = qlo

                def st_E(it, cx):
                    qg, kb = it
                    pss, bps = cx["pss"]; r_ = cx["r"]; qlo = cx["qlo"]
                    pt_, bpt = pT.next()
                    k.act(pt_[:, qlo:512], pss[:, qlo:512], AF.Exp, [bps], [bpt], scale=scale)
                    if r_ >= 0:
                        k.tt("pool", pt_[:, 128 * r_:128 * (r_ + 1)], pt_[:, 128 * r_:128 * (r_ + 1)], tri[:], ALU.mult, [bpt, b_c], [bpt])
                    if kb == 0:
                        st_E.acc = acc.next()
                        k.copy("dve", st_E.acc[0][:], pt_[:], [bpt], [st_E.acc[1]])
                    else:
                        a_, ba_ = st_E.acc
                        k.tt("dve", a_[:, qlo:512], a_[:, qlo:512], pt_[:, qlo:512], ALU.add, [bpt, ba_], [ba_])
                    cx["pt"] = (pt_, bpt); cx["acc"] = st_E.acc

                def st_V(it, cx, s=s, h=h):
                    qg, kb = it
                    q0 = qg * 512
                    pt_, bpt = cx["pt"]; qlo = cx["qlo"]
                    if kb == 0:
                        st_V.po = po.next()
                    po_, bpo = st_V.po
                    lastk = (kb == 4 * qg + 3)
                    k.mm(po_[:, qlo:512], vv[:, kb, :], pt_[:, qlo:512], kb == 0, lastk, [bpt, b_vv], [bpo])
                    if lastk:
                        a_, ba_ = cx["acc"]
                        prs, bprs = prs_r.next()
                        k.mm(prs[:], onesf[:], a_[:], True, True, [ba_, b_c], [bprs])
                        rc, brc = rec.next()
                        k.act(rc[:], prs[:], AF.Ln, [bprs], [brc])
                        k.act(rc[:], rc[:], AF.Exp, [brc], [brc], scale=-1.0)
                        oT, boT = obT.next()
                        k.tt("dve", oT[:], po_[:], rc[:], ALU.mult, [bpo, brc], [boT])
                        P.dma("mo%d" % obT.i, T["obT_s"][s, h * 128:(h + 1) * 128, q0:q0 + 512], oT[:], r=[boT], q="sp")

                pipelined(items, [st_S, st_E, st_V])
        P.emit()
    P.barrier()


def phase_C1(nc, P, k, S, NS, T):
    NB = S // 512
    with contextlib.ExitStack() as ph:
        def sb(name, shape, dt):
            return ph.enter_context(nc.sbuf_tensor(name, shape, dt))

        def psum(name, shape, dt):
            return ph.enter_context(nc.psum_tensor(name, shape, dt))

        ident = sb("identC", [128, 128], BF16)
        wog = sb("wogC", [128, 8, D], BF16)
        wom = sb("womC", [128, 8, D], BF16)
        wout = sb("woutC", [128, 8, D], BF16)
        gffn = sb("gffnC", [128, D], F32)
        epst = sb("epsC", [128, 1], F32)
        b_c = P.buf()
        P.dma("c0", ident[:], T["ident_bf"], w=[b_c])
        P.dma("c1", wog[:], T["wog_bf"].rearrange("(c p) n -> p c n", p=128), w=[b_c])
        P.dma("c2", wom[:], T["wom_bf"].rearrange("(c p) n -> p c n", p=128), w=[b_c])
        P.dma("c3", wout[:], T["wout_bf"].rearrange("(c p) n -> p c n", p=128), w=[b_c])
        P.dma("c0", gffn[:], T["gffn_bc"], w=[b_c])
        k.memset("dve", epst[:], EPS, [b_c])
        oa = Rot(P, [sb("oaC%d" % i, [128, 8, 512], BF16) for i in range(2)])
        ob = Rot(P, [sb("obC%d" % i, [128, 8, 512], BF16) for i in range(2)])
        ga = Rot(P, [sb("gaC%d" % i, [128, 8, 512], BF16) for i in range(2)])
        gb = Rot(P, [sb("gbC%d" % i, [128, 8, 512], BF16) for i in range(2)])
        mg = Rot(P, [sb("mgC%d" % i, [128, 8, 512], BF16) for i in range(2)])
        h2T = Rot(P, [sb("h2TC%d" % i, [128, 8, 512], BF16) for i in range(2)])
        ta = Rot(P, [sb("taC%d" % i, [128, 512], F32) for i in range(2)])
        tb_ = Rot(P, [sb("tbC%d" % i, [128, 512], F32) for i in range(2)])
        xt = Rot(P, [sb("xtC%d" % i, [128, D], F32) for i in range(2)])
        x1 = Rot(P, [sb("x1C%d" % i, [128, D], F32) for i in range(2)])
        hb = Rot(P, [sb("hbC%d" % i, [128, D], BF16) for i in range(2)])
        ss = Rot(P, [sb("ssC%d" % i, [128, 1], F32) for i in range(2)])
        junk = sb("junkC", [128, D], BF16)
        b_junk = P.buf()
        ps = Rot(P, [psum("psC%d" % i, [128, 512], F32) for i in range(6)])
        ps_t = Rot(P, [psum("pstC%d" % i, [128, 8, 128], BF16) for i in range(2)])
        for s in range(NS):
            for tb in range(NB):
                t0 = tb * 512
                oa_, boa = oa.next(); ob_, bob = ob.next(); ga_, bga = ga.next(); gb_, bgb = gb.next()
                vw = lambda nm: T[nm][s].rearrange("(c p) t -> p c t", p=128)[:, :, t0:t0 + 512]
                P.dma("ca%d" % oa.i, oa_[:], vw("oaT_s"), w=[boa])
                P.dma("cb%d" % ob.i, ob_[:], vw("obT_s"), w=[bob])
                P.dma("cc%d" % ga.i, ga_[:], vw("gbaT_s"), w=[bga])
                P.dma("cd%d" % gb.i, gb_[:], vw("gbbT_s"), w=[bgb])
                mg_, bmg = mg.next()
                for dch in range(8):
                    pa, bpa = ps.next()
                    for e in range(8):
                        k.mm(pa[:], wog[:, e, dch * 128:(dch + 1) * 128], oa_[:, e, :], e == 0, e == 7, [b_c, boa], [bpa])
                    pb, bpb = ps.next()
                    for e in range(8):
                        k.mm(pb[:], wom[:, e, dch * 128:(dch + 1) * 128], ob_[:, e, :], e == 0, e == 7, [b_c, bob], [bpb])
                    a_, ba_ = ta.next(); b2, bb2 = tb_.next()
                    k.tt("dve", a_[:], pa[:], ga_[:, dch, :], ALU.mult, [bpa, bga], [ba_])
                    k.tt("dve", b2[:], pb[:], gb_[:, dch, :], ALU.mult, [bpb, bgb], [bb2])
                    k.tt("pool", mg_[:, dch, :], a_[:], b2[:], ALU.add, [ba_, bb2], [bmg])
                h2_, bh2 = h2T.next()
                for tt in range(4):
                    c0, c1 = tt * 128, (tt + 1) * 128
                    x_, bx = xt.next()
                    P.dma("cx%d" % xt.i, x_[:], T["x"][s, t0 + c0:t0 + c1, :], w=[bx])
                    x1_, bx1 = x1.next()
                    for dh in range(2):
                        po_, bpo = ps.next()
                        for c in range(8):
                            k.mm(po_[:], mg_[:, c, c0:c1], wout[:, c, dh * 512:(dh + 1) * 512], c == 0, c == 7, [bmg, b_c], [bpo])
                        k.tt("dve", x1_[:, dh * 512:(dh + 1) * 512], po_[:], x_[:, dh * 512:(dh + 1) * 512], ALU.add, [bpo, bx], [bx1])
                    P.dma("cs%d" % x1.i, T["x1_s"][s, t0 + c0:t0 + c1, :], x1_[:], r=[bx1])
                    ss_, bss = ss.next()
                    k.act(junk[:], x1_[:], AF.Square, [bx1], [b_junk, bss], accum=ss_[:])
                    k.act(ss_[:], ss_[:], AF.Ln, [bss, b_c], [bss], bias=epst[:, 0:1], scale=1.0 / D)
                    k.act(ss_[:], ss_[:], AF.Exp, [bss], [bss], scale=-0.5)
                    hb_, bhb = hb.next()
                    k.stt("dve", hb_[:], x1_[:], ss_[:, 0:1], gffn[:], ALU.mult, ALU.mult, [bx1, bss, b_c], [bhb])
                    pt, bpt = ps_t.next()
                    for c in range(8):
                        k.tr(pt[:, c, :], hb_[:, c * 128:(c + 1) * 128], ident[:], [bhb, b_c], [bpt])
                    k.copy("act", h2_[:, :, c0:c1], pt[:], [bpt], [bh2])
                P.dma("ch%d" % h2T.i, T["h2T_s"][s].rearrange("(c p) t -> p c t", p=128)[:, :, t0:t0 + 512], h2_[:], r=[bh2])
        P.emit()
    P.barrier()


def phase_C2(nc, P, k, S, NS, T):
    NB = S // 512
    with contextlib.ExitStack() as ph:
        def sb(name, shape, dt):
            return ph.enter_context(nc.sbuf_tensor(name, shape, dt))

        def psum(name, shape, dt):
            return ph.enter_context(nc.psum_tensor(name, shape, dt))

        wup = sb("wupF", [128, 8, 2 * DFF], BF16)
        wdn = sb("wdnF", [128, 22, D], BF16)
        cwf = sb("cwF", [128, 44, 3], F32)
        gfin = sb("gfinF", [128, D], F32)
        epst = sb("epsF", [128, 1], F32)
        b_c = P.buf()
        wv = T["wup_bf"].rearrange("(c p) n -> p c n", p=128)
        for i in range(4):
            P.dma("c%d" % i, wup[:, :, i * 1408:(i + 1) * 1408], wv[:, :, i * 1408:(i + 1) * 1408], w=[b_c])
        P.dma("c0", wdn[:], T["wdn_bf"].rearrange("(c p) n -> p c n", p=128), w=[b_c])
        P.dma("c0", cwf[:], T["cw_ffn"], w=[b_c])
        P.dma("c2", gfin[:], T["gfin_bc"], w=[b_c])
        k.memset("dve", epst[:], EPS, [b_c])
        halo = sb("haloF", [128, 44, 2], F32)
        b_halo = [P.buf() for _ in range(44)]
        h2T = Rot(P, [sb("h2TF", [128, 8, 512], BF16)])
        aT = Rot(P, [sb("aTF", [128, 22, 512], BF16)])
        zc = Rot(P, [sb("zcF%d" % i, [128, 514], F32) for i in range(4)])
        yc = Rot(P, [sb("ycF%d" % i, [128, 512], F32) for i in range(4)])
        sg = Rot(P, [sb("sgF%d" % i, [128, 512], F32) for i in range(2)])
        x1 = Rot(P, [sb("x1F%d" % i, [128, D], F32) for i in range(2)])
        ot = Rot(P, [sb("otF", [128, D], F32)])
        ss = Rot(P, [sb("ssF%d" % i, [128, 1], F32) for i in range(2)])
        junk = sb("junkF", [128, D], BF16)
        b_junk = P.buf()
        ps = Rot(P, [psum("psF%d" % i, [128, 512], F32) for i in range(8)])
        ncv = 0
        for s in range(NS):
            for tb in range(NB):
                t0 = tb * 512
                h2_, bh2 = h2T.next()
                P.dma("fh", h2_[:], T["h2T_s"][s].rearrange("(c p) t -> p c t", p=128)[:, :, t0:t0 + 512], w=[bh2])
                aT_, baT = aT.next()
                def f0(i, cx):
                    cx["pz"] = []
                    for e in (i, 22 + i):
                        pz, bpz = ps.next()
                        for c in range(8):
                            k.mm(pz[:], wup[:, c, e * 128:(e + 1) * 128], h2_[:, c, :], c == 0, c == 7, [b_c, bh2], [bpz])
                        cx["pz"].append((pz, bpz))

                def f1(i, cx):
                    cx["z"] = []
                    for (pz, bpz), e in zip(cx["pz"], (i, 22 + i)):
                        z, bz = zc.next()
                        if tb == 0:
                            k.memset("pool", z[:, 0:2], 0.0, [bz])
                        else:
                            k.copy("pool", z[:, 0:2], halo[:, e, :], [b_halo[e]], [bz])
                        k.copy("act", z[:, 2:514], pz[:], [bpz], [bz])
                        cx["z"].append((z, bz))

                def f2(i, cx):
                    cx["y"] = []
                    for (z, bz), e in zip(cx["z"], (i, 22 + i)):
                        y, by = yc.next()
                        k.ts("dve", y[:], z[:, 0:512], cwf[:, e, 0:1], None, ALU.mult, None, [bz, b_c], [by])
                        for j in range(1, 3):
                            k.stt("dve", y[:], z[:, j:j + 512], cwf[:, e, j:j + 1], y[:], ALU.mult, ALU.add, [bz, by, b_c], [by])
                        k.copy("pool", halo[:, e, :], z[:, 512:514], [bz], [b_halo[e]])
                        cx["y"].append((y, by))

                def f3(i, cx):
                    ys = cx["y"]
                    g_, bg_ = sg.next()
                    k.act(g_[:], ys[0][0][:], AF.Silu, [ys[0][1]], [bg_])
                    k.tt("pool", aT_[:, i, :], g_[:], ys[1][0][:], ALU.mult, [bg_, ys[1][1]], [baT])

                pipelined(list(range(22)), [f0, f1, f2, f3])
                for tt in range(4):
                    c0, c1 = tt * 128, (tt + 1) * 128
                    x1_, bx1 = x1.next()
                    P.dma("fx%d" % x1.i, x1_[:], T["x1_s"][s, t0 + c0:t0 + c1, :], w=[bx1])
                    for dh in range(2):
                        po_, bpo = ps.next()
                        for i in range(22):
                            k.mm(po_[:], aT_[:, i, c0:c1], wdn[:, i, dh * 512:(dh + 1) * 512], i == 0, i == 21, [baT, b_c], [bpo])
                        k.tt("dve", x1_[:, dh * 512:(dh + 1) * 512], po_[:], x1_[:, dh * 512:(dh + 1) * 512], ALU.add, [bpo, bx1], [bx1])
                    ss_, bss = ss.next()
                    k.act(junk[:], x1_[:], AF.Square, [bx1], [b_junk, bss], accum=ss_[:])
                    k.act(ss_[:], ss_[:], AF.Ln, [bss, b_c], [bss], bias=epst[:, 0:1], scale=1.0 / D)
                    k.act(ss_[:], ss_[:], AF.Exp, [bss], [bss], scale=-0.5)
                    o_, bo = ot.next()
                    k.stt("dve", o_[:], x1_[:], ss_[:, 0:1], gfin[:], ALU.mult, ALU.mult, [bx1, bss, b_c], [bo])
                    P.dma("fo", T["out"][s, t0 + c0:t0 + c1, :], o_[:], r=[bo])
        P.emit()
    P.barrier()


def build(S, NS, upto="all", dbg=False):
    nc = bass.Bass("TRN2", target_bir_lowering=False)
    T = {}

    def din(name, shape, dt=F32):
        T[name] = nc.dram_tensor(name, list(shape), dt, kind="ExternalInput").ap()

    def dscr(name, shape, dt):
        kind = "ExternalOutput" if dbg else "Internal"
        T[name] = nc.dram_tensor(name, list(shape), dt, kind=kind).ap()

    din("x", [NS, S, D])
    din("w_in_p", [D, DINP]); din("w_uq_p", [384, 2048]); din("w_uk", [256, 1024]); din("w_uv", [256, 1024])
    din("w_og", [D, D]); din("w_om", [D, D]); din("w_out", [D, D]); din("w_up", [D, 2 * DFF]); din("w_dn", [DFF, D])
    din("ident_bf", [128, 128], BF16); din("ident_f", [128, 128])
    din("gmix_bc", [128, D]); din("gffn_bc", [128, D]); din("gfin_bc", [128, D])
    din("cw_qkv", [128, 24, 4]); din("cw_ffn", [128, 44, 3])
    din("dtb_bc", [128, 8]); din("alog_bc", [128, 8])
    din("gq_p", [128, 3]); din("gkv_p", [128, 2]); din("ggdn_p", [128, 1])
    din("cosT", [64, S]); din("sinT", [64, S])
    din("m_strict", [128, 128]); din("m_strictT", [128, 128]); din("m_inclT", [128, 128]); din("utri", [128, 128])
    din("tri_bf", [128, 128], BF16)
    din("maskd4", [128, 4, 128]); din("mX1_4", [128, 4, 128]); din("mX2_4", [128, 4, 128]); din("mX3_4", [128, 4, 128])
    for nm, shp in (("win_bf", [D, DINP]), ("wuq_bf", [384, 2048]), ("wuk_bf", [256, 1024]), ("wuv_bf", [256, 1024]),
                    ("wog_bf", [D, D]), ("wom_bf", [D, D]), ("wout_bf", [D, D]), ("wup_bf", [D, 2 * DFF]),
                    ("wdn_bf", [DFF, D])):
        T[nm] = nc.dram_tensor(nm, shp, BF16, kind="Internal").ap()
    dscr("qT_s", [NS, 8, 128, S], BF16); dscr("kT_s", [NS, 8, 128, S], BF16); dscr("vT_s", [NS, 8, 128, S], BF16)
    dscr("gateT_s", [NS, D, S], BF16); dscr("gbaT_s", [NS, D, S], BF16); dscr("gbbT_s", [NS, D, S], BF16)
    dscr("cqT_s", [NS, 384, S], BF16); dscr("ckvT_s", [NS, 256, S], BF16); dscr("kpeT_s", [NS, 64, S], BF16)
    dscr("gbt_s", [NS, S, 24], F32)
    dscr("oaT_s", [NS, D, S], BF16); dscr("obT_s", [NS, D, S], BF16)
    dscr("x1_s", [NS, S, D], F32); dscr("h2T_s", [NS, D, S], BF16)
    T["out"] = nc.dram_tensor("out", [NS, S, D], F32, kind="ExternalOutput").ap()

    with contextlib.ExitStack() as es:
        P = Prog(nc, es)
        k = K(P)
        phase0_convert(nc, P, k, [(T["w_in_p"], T["win_bf"]), (T["w_uq_p"], T["wuq_bf"]), (T["w_uk"], T["wuk_bf"]),
                                  (T["w_uv"], T["wuv_bf"]), (T["w_og"], T["wog_bf"]), (T["w_om"], T["wom_bf"]),
                                  (T["w_out"], T["wout_bf"]), (T["w_up"], T["wup_bf"]), (T["w_dn"], T["wdn_bf"])])
        phase_A(nc, P, k, S, NS, T)
        if upto != "A":
            phase_G(nc, P, k, S, NS, T)
        if upto not in ("A", "G"):
            phase_M(nc, P, k, S, NS, T)
        if upto not in ("A", "G", "M"):
            phase_C1(nc, P, k, S, NS, T)
            phase_C2(nc, P, k, S, NS, T)
        P.finish()
    return nc


def host_consts(inp, S):
    f = np.float32
    bf = ml_dtypes.bfloat16
    w_in = np.asarray(inp["w_in"][0], f)
    offs = np.cumsum([0, 3072, 1024, 8, 8, 384, 256, 64, 1024, 1024])
    qkv, gate, a_, b_, cq, ckv, kpe, gba, gbb = [w_in[:, offs[i]:offs[i + 1]] for i in range(9)]
    kps = np.concatenate([kpe[:, 32:], kpe[:, :32]], axis=1)
    c = {}
    c["w_in_p"] = np.ascontiguousarray(np.concatenate([qkv, gate, gba, gbb, cq, ckv, kpe, kps, a_, b_], axis=1))
    wuq = np.asarray(inp["w_uq"][0], f).reshape(384, 8, 192)
    c["w_uq_p"] = np.ascontiguousarray(np.concatenate(
        [wuq[:, :, :128], wuq[:, :, 128:], wuq[:, :, 160:], wuq[:, :, 128:160]], axis=2).reshape(384, 2048))
    wukv = np.asarray(inp["w_ukv"][0], f).reshape(256, 8, 256)
    c["w_uk"] = np.ascontiguousarray(wukv[:, :, :128].reshape(256, 1024))
    c["w_uv"] = np.ascontiguousarray(wukv[:, :, 128:].reshape(256, 1024))
    c["w_og"] = np.ascontiguousarray(inp["w_o_gdn"][0], f)
    c["w_om"] = np.ascontiguousarray(inp["w_o_mla"][0], f)
    c["w_out"] = np.ascontiguousarray(inp["w_out"][0], f)
    c["w_up"] = np.ascontiguousarray(inp["w_up"][0], f)
    c["w_dn"] = np.ascontiguousarray(inp["w_down"][0], f)
    c["ident_bf"] = np.eye(128, dtype=f).astype(bf)
    c["ident_f"] = np.eye(128, dtype=f)
    bc = lambda v: np.ascontiguousarray(np.broadcast_to(np.asarray(v, f).reshape(1, -1), (128, np.asarray(v).size)))
    c["gmix_bc"] = bc(inp["norm_mix_g"][0]); c["gffn_bc"] = bc(inp["norm_ffn_g"][0]); c["gfin_bc"] = bc(inp["norm_final_g"])
    c["cw_qkv"] = np.ascontiguousarray(np.asarray(inp["conv_qkv_w"][0], f).reshape(4, 24, 128).transpose(2, 1, 0))
    c["cw_ffn"] = np.ascontiguousarray(np.asarray(inp["conv_ffn_w"][0], f).reshape(3, 44, 128).transpose(2, 1, 0))
    c["dtb_bc"] = bc(inp["gdn_dt_bias"][0]); c["alog_bc"] = bc(inp["gdn_a_log"][0])
    c["gq_p"] = np.ascontiguousarray(np.asarray(inp["mla_q_norm_g"][0], f).reshape(3, 128).T)
    c["gkv_p"] = np.ascontiguousarray(np.asarray(inp["mla_kv_norm_g"][0], f).reshape(2, 128).T)
    c["ggdn_p"] = np.ascontiguousarray(np.asarray(inp["gdn_norm_g"][0], f).reshape(128, 1))
    inv = (np.float32(10000.0) ** (-(np.arange(32, dtype=f) / np.float32(32)))).astype(f)
    ang = (np.arange(S, dtype=f)[None, :] * inv[:, None]).astype(f)
    cs, sn = np.cos(ang.astype(np.float64)).astype(f), np.sin(ang.astype(np.float64)).astype(f)
    c["cosT"] = np.ascontiguousarray(np.concatenate([cs, cs], 0))
    c["sinT"] = np.ascontiguousarray(np.concatenate([-sn, sn], 0))
    i = np.arange(128)[:, None]; j = np.arange(128)[None, :]
    c["m_strict"] = np.where(i > j, 0.0, NEG).astype(f)
    c["m_strictT"] = np.where(j > i, 0.0, NEG).astype(f)
    c["m_inclT"] = np.where(j >= i, 0.0, NEG).astype(f)
    c["utri"] = (i <= j).astype(f)
    c["tri_bf"] = (j >= i).astype(f).astype(bf)
    rep4 = lambda m_: np.ascontiguousarray(np.broadcast_to(m_.astype(f)[:, None, :], (128, 4, 128)))
    c["maskd4"] = rep4((i // 16) == (j // 16))
    for l_, b_ in ((1, 16), (2, 32), (3, 64)):
        c["mX%d_4" % l_] = rep4(((i // b_) % 2 == 0) & ((j // b_) == (i // b_) + 1))
    return c


_NC_CACHE = {}


def kernel(**inputs):
    x = np.asarray(inputs["x"], np.float32)
    B, S, _ = x.shape
    n = 8
    NS = B // n
    key = (S, NS)
    if key not in _NC_CACHE:
        _NC_CACHE[key] = build(S, NS)
    nc = _NC_CACHE[key]
    c = host_consts(inputs, S)
    in_maps = []
    for i in range(n):
        m = dict(c)
        m["x"] = np.ascontiguousarray(x[i * NS:(i + 1) * NS])
        in_maps.append(m)
    res = run_bass_kernel_spmd(nc, in_maps, core_ids=list(range(n)))
    return np.concatenate([np.asarray(r["out"], np.float32) for r in res.results], axis=0)
```

```python
import contextlib
import numpy as np
import concourse.bass as bass
import concourse.mybir as mybir

F32 = mybir.dt.float32
BF16 = mybir.dt.bfloat16
AF = mybir.ActivationFunctionType
ALU = mybir.AluOpType
AX = mybir.AxisListType


class Buf:
    __slots__ = ("name", "w", "rs")

    def __init__(self, name):
        self.name = name
        self.w = None
        self.rs = []


class Op:
    __slots__ = ("id", "eng", "fn", "deps", "dma", "chan", "val", "sem", "need_sig")

    def __init__(self, id, eng, fn, dma=False, chan=None):
        self.id = id
        self.eng = eng
        self.fn = fn
        self.deps = {}
        self.dma = dma
        self.chan = chan
        self.val = 0
        self.need_sig = False


ENGS = ("pe", "act", "dve", "pool", "sp")


class Prog:
    def __init__(self, nc, es, n_chan=30):
        self.nc = nc
        self.ops = []
        self.nid = 0
        self.esem_sets = [{e: es.enter_context(nc.semaphore("s%d_%s" % (i, e))) for e in ENGS if e != "sp"}
                          for i in range(7)]
        self.phase = -1
        self.esem = self.esem_sets[0]
        self.ecount = {e: 0 for e in self.esem}
        self.chan_sems = {q: [(es.enter_context(nc.semaphore("d%s%d" % (q, i))), 0) for i in range(n_chan if q == "sp" else 18)]
                          for q in ("sp", "pool")}
        self.chan = {}
        self.waited = {e: {} for e in ENGS}
        self.last = {e: None for e in ENGS}
        self.barrier_deps = []
        self.bufs = []

    def buf(self, name="b"):
        b = Buf(name)
        self.bufs.append(b)
        return b

    def bufs_n(self, n, name="b"):
        return [self.buf(name + str(i)) for i in range(n)]

    def _mk(self, eng, fn, r, w, dma=False, chan=None):
        op = Op(self.nid, eng, fn, dma, chan)
        self.nid += 1
        for d in self.barrier_deps:
            op.deps[d] = True
        for b in r:
            if b.w is not None:
                op.deps[b.w] = True
        for b in w:
            if b.w is not None and b.w not in op.deps:
                op.deps[b.w] = False
            for rd in b.rs:
                if rd is not op and rd not in op.deps:
                    op.deps[rd] = False
        for b in r:
            b.rs.append(op)
        for b in w:
            b.w = op
            b.rs = []
        self.ops.append(op)
        self.last[eng] = op
        return op

    def op(self, eng, fn, r=(), w=()):
        return self._mk(eng, fn, r, w)

    def dma(self, chan, out, in_, r=(), w=(), q=None):
        if q is None:
            q = "sp" if len(w) > 0 else "pool"
        if chan not in self.chan:
            if not self.chan_sems[q]:
                raise RuntimeError("out of dma channels")
            sem, cnt = self.chan_sems[q].pop()
            self.chan[chan] = [sem, cnt, None, q]
        c = self.chan[chan]
        op = self._mk(q, lambda e, o=out, i=in_: e.dma_start(out=o, in_=i), r, w, dma=True, chan=chan)
        if c[2] is not None and c[2] not in op.deps:
            op.deps[c[2]] = True
        c[1] += 16
        op.val = c[1]
        op.sem = c[0]
        c[2] = op
        return op

    def barrier(self):
        deps = [o for o in self.last.values() if o is not None]
        deps += [c[2] for c in self.chan.values() if c[2] is not None]
        self.barrier_deps = deps
        for key, c in self.chan.items():
            self.chan_sems[c[3]].append((c[0], c[1]))
        self.chan = {}
        for b in self.bufs:
            b.w = None
            b.rs = []
        self.bufs = []

    def _needs_wait(self, op, d, is_raw):
        if d.dma:
            return True
        if d.eng != op.eng:
            return True
        if op.dma:
            return True
        if op.eng == "pe":
            return False
        return is_raw

    def emit(self):
        nc = self.nc
        ops = self.ops
        self.ops = []
        self.phase += 1
        self.esem = self.esem_sets[self.phase]
        self.ecount = {e: 0 for e in self.esem}
        lastc = {}
        for op in ops:
            if not op.dma:
                lastc[op.eng] = op
        for op in lastc.values():
            op.need_sig = True
        for op in ops:
            for d, raw in op.deps.items():
                if not d.dma and self._needs_wait(op, d, raw):
                    d.need_sig = True
        for op in ops:
            if not op.dma and op.need_sig and op.val == 0:
                self.ecount[op.eng] += 1
                op.val = self.ecount[op.eng]
                op.sem = self.esem[op.eng]
        by_eng = {e: [o for o in ops if o.eng == e] for e in ENGS}

        def run(eng_name, eng):
            waited = self.waited[eng_name]
            for op in by_eng[eng_name]:
                need = {}
                for d, raw in op.deps.items():
                    if not self._needs_wait(op, d, raw):
                        continue
                    sem = d.sem
                    assert d.val > 0, (d.id, d.eng, d.dma)
                    k = id(sem)
                    if waited.get(k, 0) >= d.val:
                        continue
                    if k not in need or need[k][1] < d.val:
                        need[k] = (sem, d.val)
                for k, (sem, val) in need.items():
                    eng.wait_ge(sem, val)
                    waited[k] = val
                ins = op.fn(eng)
                if op.dma:
                    ins.then_inc(op.sem, 16)
                elif op.need_sig:
                    ins.then_inc(op.sem, 1)

        with nc.Block() as block:
            @block.tensor
            def _(e):
                run("pe", e)

            @block.scalar
            def _(e):
                run("act", e)

            @block.vector
            def _(e):
                run("dve", e)

            @block.gpsimd
            def _(e):
                run("pool", e)

            @block.sync
            def _(e):
                run("sp", e)

    def finish(self):
        nc = self.nc
        with nc.Block() as block:
            @block.sync
            def _(e):
                for sem, cnt in self.chan_sems["sp"] + self.chan_sems["pool"] + [(c[0], c[1]) for c in self.chan.values()]:
                    if cnt > 0:
                        e.wait_ge(sem, cnt)

import ml_dtypes
from concourse.bass_utils import run_bass_kernel_spmd

D = 1024
DINP = 6928
OQ, OG, OBA, OBB, OCQ, OCKV, OKPE, OKPS, OAB = 0, 3072, 4096, 5120, 6144, 6528, 6784, 6848, 6912
DFF = 2816
EPS = 1e-6
NEG = -30000.0


class K:
    def __init__(self, P):
        self.P = P

    def act(self, out, in_, func, r, w, bias=None, scale=None, accum=None, eng="act"):
        kw = {}
        if bias is not None:
            kw["bias"] = bias
        if scale is not None:
            kw["scale"] = scale
        if accum is not None:
            kw["accum_out"] = accum
        return self.P.op(eng, lambda e: e.activation(out=out, in_=in_, func=func, **kw), r, w)

    def tt(self, eng, out, in0, in1, op, r, w):
        return self.P.op(eng, lambda e: e.tensor_tensor(out=out, in0=in0, in1=in1, op=op), r, w)

    def ts(self, eng, out, in0, s1, s2, op0, op1, r, w):
        if s2 is None:
            return self.P.op(eng, lambda e: e.tensor_scalar(out=out, in0=in0, scalar1=s1, scalar2=None, op0=op0), r, w)
        return self.P.op(eng, lambda e: e.tensor_scalar(out=out, in0=in0, scalar1=s1, scalar2=s2, op0=op0, op1=op1), r, w)

    def stt(self, eng, out, in0, scalar, in1, op0, op1, r, w):
        return self.P.op(eng, lambda e: e.scalar_tensor_tensor(out=out, in0=in0, scalar=scalar, in1=in1, op0=op0, op1=op1), r, w)

    def copy(self, eng, out, in_, r, w):
        if eng == "act":
            return self.P.op(eng, lambda e: e.activation(out=out, in_=in_, func=AF.Copy), r, w)
        return self.P.op(eng, lambda e: e.tensor_copy(out=out, in_=in_), r, w)

    def memset(self, eng, ap, val, w):
        return self.P.op(eng, lambda e: e.memset(ap, val), (), w)

    def mm(self, out, lhsT, rhs, start, stop, r, w):
        return self.P.op("pe", lambda e: e.matmul(out, lhsT=lhsT, rhs=rhs, start=start, stop=stop), r, w)

    def tr(self, out, in_, ident, r, w):
        return self.P.op("pe", lambda e: e.transpose(out, in_, ident), r, w)


class Rot:
    def __init__(self, P, tiles):
        self.t = tiles
        self.b = [P.buf() for _ in tiles]
        self.i = -1

    def next(self):
        self.i = (self.i + 1) % len(self.t)
        return self.t[self.i], self.b[self.i]


def pipelined(items, stages):
    n, ns = len(items), len(stages)
    ctx = [dict() for _ in items]
    for t in range(n + ns - 1):
        for s_ in reversed(range(ns)):
            e = t - s_
            if 0 <= e < n:
                stages[s_](items[e], ctx[e])


def phase0_convert(nc, P, k, pairs):
    with contextlib.ExitStack() as ph:
        CW = 2048
        NBUF = 6
        ft = [ph.enter_context(nc.sbuf_tensor("cv_f%d" % i, [128, CW], F32)) for i in range(NBUF)]
        bt = [ph.enter_context(nc.sbuf_tensor("cv_b%d" % i, [128, CW], BF16)) for i in range(NBUF)]
        fb = [P.buf() for _ in range(NBUF)]
        bb = [P.buf() for _ in range(NBUF)]
        engs = ["dve", "act", "dve", "act", "dve", "pool"]
        n = 0
        for src, dst in pairs:
            R_, C_ = src.shape
            for r0 in range(0, R_, 128):
                for c0 in range(0, C_, CW):
                    cw = min(CW, C_ - c0)
                    i = n % NBUF
                    P.dma("cvl%d" % i, ft[i][:, 0:cw], src[r0:r0 + 128, c0:c0 + cw], w=[fb[i]])
                    k.copy(engs[i], bt[i][:, 0:cw], ft[i][:, 0:cw], [fb[i]], [bb[i]])
                    P.dma("cvs%d" % i, dst[r0:r0 + 128, c0:c0 + cw], bt[i][:, 0:cw], r=[bb[i]])
                    n += 1
        P.emit()
    P.barrier()


def phase_A(nc, P, k, S, NS, T):
    NB = S // 512
    with contextlib.ExitStack() as ph:
        def sb(name, shape, dt):
            return ph.enter_context(nc.sbuf_tensor(name, shape, dt))

        def psum(name, shape, dt):
            return ph.enter_context(nc.psum_tensor(name, shape, dt))

        win_sb = sb("win_sb", [128, 8, DINP], BF16)
        b_win = P.buf()
        wv = T["win_bf"].rearrange("(c p) n -> p c n", p=128)
        splits = [0, 1792, 3584, 5376, DINP]
        for i in range(4):
            P.dma("wl%d" % i, win_sb[:, :, splits[i]:splits[i + 1]], wv[:, :, splits[i]:splits[i + 1]], w=[b_win])
        ident = sb("identA", [128, 128], BF16)
        ones = sb("onesA", [128, 128], BF16)
        gmix = sb("gmix", [128, D], F32)
        cw = sb("cwA", [128, 24, 4], F32)
        dtb = sb("dtb", [128, 8], F32)
        nA = sb("nA", [128, 8], F32)
        gq = sb("gq", [128, 3], F32)
        gkv = sb("gkv", [128, 2], F32)
        epst = sb("epsA", [128, 1], F32)
        onet = sb("oneA", [128, 1], F32)
        lnsc = sb("lnscA", [128, 1], F32)
        b_c = P.buf()
        P.dma("c0", ident[:], T["ident_bf"], w=[b_c])
        P.dma("c0", gmix[:], T["gmix_bc"], w=[b_c])
        P.dma("c0", cw[:], T["cw_qkv"], w=[b_c])
        P.dma("c0", dtb[:], T["dtb_bc"], w=[b_c])
        P.dma("c0", nA[:], T["alog_bc"], w=[b_c])
        P.dma("c0", gq[:], T["gq_p"], w=[b_c])
        P.dma("c0", gkv[:], T["gkv_p"], w=[b_c])
        k.memset("dve", ones[:], 1.0, [b_c])
        k.memset("dve", epst[:], EPS, [b_c])
        k.memset("dve", onet[:], 1.0, [b_c])
        k.memset("dve", lnsc[:], float(np.log(128.0 ** -0.5)), [b_c])
        k.act(nA[:], nA[:], AF.Exp, [b_c], [b_c])
        k.ts("dve", nA[:], nA[:], -1.0, None, ALU.mult, None, [b_c], [b_c])

        halo = sb("haloA", [128, 24, 3], F32)
        b_halo = [P.buf() for _ in range(24)]
        xt = Rot(P, [sb("xtA%d" % i, [128, D], F32) for i in range(2)])
        junk = sb("junkA", [128, D], BF16)
        b_junk = P.buf()
        hb = Rot(P, [sb("hbA%d" % i, [128, D], BF16) for i in range(2)])
        ss = Rot(P, [sb("ssA%d" % i, [128, 1], F32) for i in range(2)])
        hT = Rot(P, [sb("hTA%d" % i, [128, 8, 512], BF16) for i in range(2)])
        ps_t = Rot(P, [psum("pstA%d" % i, [128, 8, 128], BF16) for i in range(1)])
        ps_z = Rot(P, [psum("pszA%d" % i, [128, 512], F32) for i in range(4)])
        ps_s = Rot(P, [psum("pssA%d" % i, [128, 512], F32) for i in range(2)])
        ps_ab = Rot(P, [psum("psabA", [128, 16], F32)])
        abx = Rot(P, [sb("abxA%d" % i, [128, 8], F32) for i in range(2)])
        abt = Rot(P, [sb("abtA%d" % i, [128, 8], F32) for i in range(2)])
        abu = Rot(P, [sb("abuA%d" % i, [128, 8], F32) for i in range(2)])
        gbt = Rot(P, [sb("gbtA%d" % i, [128, 24], F32) for i in range(2)])
        zc = Rot(P, [sb("zcA%d" % i, [128, 515], F32) for i in range(3)])
        yc = Rot(P, [sb("ycA%d" % i, [128, 512], F32) for i in range(2)])
        ys = Rot(P, [sb("ysA%d" % i, [128, 512], F32) for i in range(5)])
        sq = Rot(P, [sb("sqA%d" % i, [128, 512], BF16) for i in range(2)])
        rr = Rot(P, [sb("rrA%d" % i, [128, 512], F32) for i in range(2)])
        st = Rot(P, [sb("stA%d" % i, [128, 512], BF16) for i in range(4)])
        raw = sb("rawA", [128, 3, 512], F32)
        b_raw = P.buf()
        cst = Rot(P, [sb("cosA%d" % i, [64, 512], F32) for i in range(2)])
        snt = Rot(P, [sb("sinA%d" % i, [64, 512], F32) for i in range(2)])
        t1 = sb("t1A", [64, 512], F32)
        t2 = sb("t2A", [64, 512], F32)
        b_t1, b_t2 = P.buf(), P.buf()
        nst = [0]

        def store(dst, tile, btile):
            P.dma("stA%d" % (nst[0] % 4), dst, tile, r=[btile], q="sp")
            nst[0] += 1

        def mm_chunk(pst, bpst, col0, ncols, hTt, bhT):
            for c in range(8):
                k.mm(pst[0:ncols, :], win_sb[:, c, col0:col0 + ncols], hTt[:, c, :], c == 0, c == 7,
                     [b_win, bhT], [bpst])

        def prologue(s, tb):
            t0 = tb * 512
            hTt, bhT = hT.next()
            ct, bct = cst.next()
            sn, bsn = snt.next()
            P.dma("cs0", ct[:], T["cosT"][:, t0:t0 + 512], w=[bct])
            P.dma("cs1", sn[:], T["sinT"][:, t0:t0 + 512], w=[bsn])
            for tt in range(4):
                xtt, bx = xt.next()
                P.dma("xA%d" % (xt.i), xtt[:], T["x"][s, t0 + tt * 128:t0 + (tt + 1) * 128, :], w=[bx])
                sst, bss = ss.next()
                k.act(junk[:], xtt[:], AF.Square, [bx], [b_junk, bss], accum=sst[:])
                k.act(sst[:], sst[:], AF.Ln, [bss, b_c], [bss], bias=epst[:, 0:1], scale=1.0 / D)
                k.act(sst[:], sst[:], AF.Exp, [bss], [bss], scale=-0.5)
                hbt, bhb = hb.next()
                k.stt("dve", hbt[:], xtt[:], sst[:, 0:1], gmix[:], ALU.mult, ALU.mult, [bx, bss, b_c], [bhb])
                pt, bpt = ps_t.next()
                for c in range(8):
                    k.tr(pt[:, c, :], hbt[:, c * 128:(c + 1) * 128], ident[:], [bhb, b_c], [bpt])
                k.copy("dve" if tt % 2 == 0 else "act", hTt[:, :, tt * 128:(tt + 1) * 128], pt[:], [bpt], [bhT])
                pab, bpab = ps_ab.next()
                for c in range(8):
                    k.mm(pab[:], hTt[:, c, tt * 128:(tt + 1) * 128], win_sb[:, c, OAB:OAB + 16], c == 0, c == 7,
                         [b_win, bhT], [bpab])
                ax, bax = abx.next()
                at, bat = abt.next()
                au, bau = abu.next()
                gt, bgt = gbt.next()
                k.tt("dve", ax[:], pab[:, 0:8], dtb[:], ALU.add, [bpab, b_c], [bax])
                k.act(at[:], ax[:], AF.Abs, [bax], [bat])
                k.act(at[:], at[:], AF.Exp, [bat], [bat], scale=-1.0)
                k.act(at[:], at[:], AF.Ln, [bat, b_c], [bat], bias=onet[:, 0:1])
                k.ts("dve", au[:], ax[:], 0.0, None, ALU.max, None, [bax], [bau])
                k.tt("dve", au[:], au[:], at[:], ALU.add, [bau, bat], [bau])
                k.tt("dve", gt[:, 0:8], au[:], nA[:], ALU.mult, [bau, b_c], [bgt])
                k.act(gt[:, 8:16], pab[:, 8:16], AF.Sigmoid, [bpab], [bgt])
                k.act(gt[:, 16:24], gt[:, 8:16], AF.Ln, [bgt], [bgt])
                P.dma("gbtA%d" % gbt.i, T["gbt_s"][s, t0 + tt * 128:t0 + (tt + 1) * 128, :], gt[:], r=[bgt])
            return (hTt, bhT, ct, bct, sn, bsn)

        blocks = [(s, tb) for s in range(NS) for tb in range(NB)]
        pro = {0: prologue(*blocks[0])}
        for bi, (s, tb) in enumerate(blocks):
            t0 = tb * 512
            if bi + 1 < len(blocks):
                pro[bi + 1] = prologue(*blocks[bi + 1])
            hTt, bhT, ct, bct, sn, bsn = pro.pop(bi)

            def q0(e, cx):
                cx["pz"] = ps_z.next()
                mm_chunk(cx["pz"][0], cx["pz"][1], OQ + e * 128, 128, hTt, bhT)

            def q1(e, cx):
                pz, bpz = cx["pz"]
                z, bz = zc.next()
                if tb == 0:
                    k.memset("pool", z[:, 0:3], 0.0, [bz])
                else:
                    k.copy("pool", z[:, 0:3], halo[:, e, :], [b_halo[e]], [bz])
                k.copy("act", z[:, 3:515], pz[:], [bpz], [bz])
                cx["z"] = (z, bz)

            def q2(e, cx):
                z, bz = cx["z"]
                y, by = yc.next()
                k.ts("dve", y[:], z[:, 0:512], cw[:, e, 0:1], None, ALU.mult, None, [bz, b_c], [by])
                for j in range(1, 4):
                    k.stt("dve", y[:], z[:, j:j + 512], cw[:, e, j:j + 1], y[:], ALU.mult, ALU.add, [bz, by, b_c], [by])
                k.copy("pool", halo[:, e, :], z[:, 512:515], [bz], [b_halo[e]])
                cx["y"] = (y, by)

            def q3(e, cx):
                y, by = cx["y"]
                if e >= 16:
                    so, bso = st.next()
                    k.act(so[:], y[:], AF.Silu, [by], [bso])
                    store(T["vT_s"][s, e % 8, :, t0:t0 + 512], so[:], bso)
                else:
                    yy, byy = ys.next()
                    k.act(yy[:], y[:], AF.Silu, [by], [byy])
                    sqt, bsq = sq.next()
                    k.tt("pool", sqt[:], yy[:], yy[:], ALU.mult, [byy], [bsq])
                    cx["yy"] = (yy, byy); cx["sq"] = (sqt, bsq)

            def q4(e, cx):
                if e >= 16:
                    return
                sqt, bsq = cx["sq"]
                pss, bps = ps_s.next()
                k.mm(pss[:], ones[:], sqt[:], True, True, [bsq, b_c], [bps])
                cx["pss"] = (pss, bps)

            def q5(e, cx):
                if e >= 16:
                    return
                pss, bps = cx["pss"]
                rt, brt = rr.next()
                k.act(rt[:], pss[:], AF.Ln, [bps, b_c], [brt], bias=epst[:, 0:1])
                if e < 8:
                    k.act(rt[:], rt[:], AF.Exp, [brt, b_c], [brt], scale=-0.5, bias=lnsc[:, 0:1])
                else:
                    k.act(rt[:], rt[:], AF.Exp, [brt], [brt], scale=-0.5)
                cx["rt"] = (rt, brt)

            def q6(e, cx):
                if e >= 16:
                    return
                yy, byy = cx["yy"]; rt, brt = cx["rt"]
                so, bso = st.next()
                k.tt("dve", so[:], yy[:], rt[:], ALU.mult, [byy, brt], [bso])
                dst = T["qT_s"] if e < 8 else T["kT_s"]
                store(dst[s, e % 8, :, t0:t0 + 512], so[:], bso)

            pipelined(list(range(24)), [q0, q1, q2, q3, q4, q5, q6])

            def g0(e, cx):
                cx["pz"] = ps_z.next()
                mm_chunk(cx["pz"][0], cx["pz"][1], OG + e * 128, 128, hTt, bhT)

            def g1(e, cx):
                pz, bpz = cx["pz"]
                so, bso = st.next()
                k.act(so[:], pz[:], AF.Silu if e < 8 else AF.Sigmoid, [bpz], [bso])
                if e < 8:
                    dst = T["gateT_s"][s, e * 128:(e + 1) * 128, t0:t0 + 512]
                elif e < 16:
                    dst = T["gbaT_s"][s, (e - 8) * 128:(e - 7) * 128, t0:t0 + 512]
                else:
                    dst = T["gbbT_s"][s, (e - 16) * 128:(e - 15) * 128, t0:t0 + 512]
                store(dst, so[:], bso)

            pipelined(list(range(24)), [g0, g1])

            for (col, nch, gg, dstn) in ((OCQ, 3, gq, "cqT_s"), (OCKV, 2, gkv, "ckvT_s")):
                pss, bps = ps_s.next()
                for j in range(nch):
                    pz, bpz = ps_z.next()
                    mm_chunk(pz, bpz, col + j * 128, 128, hTt, bhT)
                    k.copy("act", raw[:, j, :], pz[:], [bpz], [b_raw])
                    sqt, bsq = sq.next()
                    k.tt("pool", sqt[:], raw[:, j, :], raw[:, j, :], ALU.mult, [b_raw], [bsq])
                    k.mm(pss[:], ones[:], sqt[:], j == 0, j == nch - 1, [bsq, b_c], [bps])
                rt, brt = rr.next()
                k.act(rt[:], pss[:], AF.Ln, [bps, b_c], [brt], bias=epst[:, 0:1], scale=1.0 / (128 * nch))
                k.act(rt[:], rt[:], AF.Exp, [brt], [brt], scale=-0.5)
                for j in range(nch):
                    so, bso = st.next()
                    k.stt("dve", so[:], raw[:, j, :], gg[:, j:j + 1], rt[:], ALU.mult, ALU.mult, [b_raw, brt, b_c], [bso])
                    store(T[dstn][s, j * 128:(j + 1) * 128, t0:t0 + 512], so[:], bso)
            pa, bpa = ps_z.next()
            mm_chunk(pa, bpa, OKPE, 64, hTt, bhT)
            pb, bpb = ps_z.next()
            mm_chunk(pb, bpb, OKPS, 64, hTt, bhT)
            k.tt("dve", t1[:], pa[0:64, :], ct[:], ALU.mult, [bpa, bct], [b_t1])
            k.tt("dve", t2[:], pb[0:64, :], sn[:], ALU.mult, [bpb, bsn], [b_t2])
            so, bso = st.next()
            k.tt("pool", so[0:64, :], t1[:], t2[:], ALU.add, [b_t1, b_t2], [bso])
            store(T["kpeT_s"][s, :, t0:t0 + 512], so[0:64, :], bso)
        P.emit()
    P.barrier()


class StopBuild(Exception):
    pass


KSTOP = [0]


def kstop(n):
    if KSTOP[0] == n:
        raise StopBuild()


def phase_G(nc, P, k, S, NS, T):
    NB = S // 512
    with contextlib.ExitStack() as ph:
        def sb(name, shape, dt):
            return ph.enter_context(nc.sbuf_tensor(name, shape, dt))

        def psum(name, shape, dt):
            return ph.enter_context(nc.psum_tensor(name, shape, dt))

        ident = sb("identG", [128, 128], BF16)
        identf = sb("identfG", [128, 128], F32)
        ones = sb("onesG", [128, 128], BF16)
        onesf = sb("onesfG", [128, 128], F32)
        utri = sb("utriG", [128, 128], F32)
        mS = sb("mSG", [128, 128], F32)
        mST = sb("mSTG", [128, 128], F32)
        mIT = sb("mITG", [128, 128], F32)
        ggdn = sb("ggdnG", [128, 1], F32)
        epst = sb("epsG", [128, 1], F32)
        b_c = P.buf()
        for dst, src in ((ident, "ident_bf"), (identf, "ident_f"), (utri, "utri"), (mS, "m_strict"),
                         (mST, "m_strictT"), (mIT, "m_inclT"), (ggdn, "ggdn_p")):
            P.dma("c0", dst[:], T[src], w=[b_c])
        k.memset("dve", ones[:], 1.0, [b_c])
        k.memset("dve", onesf[:], 1.0, [b_c])
        k.memset("dve", epst[:], EPS, [b_c])

        Sf = sb("SfG", [128, 8, 128], F32)
        Sb = sb("SbG", [128, 8, 128], BF16)
        kblk = Rot(P, [sb("kblkG%d" % i, [128, 8, 256], BF16) for i in range(2)])
        qblk = Rot(P, [sb("qblkG%d" % i, [128, 8, 256], BF16) for i in range(2)])
        vblk = Rot(P, [sb("vblkG%d" % i, [128, 8, 256], BF16) for i in range(2)])
        gblk = Rot(P, [sb("gblkG%d" % i, [128, 8, 256], BF16) for i in range(2)])
        oblk = Rot(P, [sb("oblkG%d" % i, [128, 8, 256], BF16) for i in range(2)])
        gbt = Rot(P, [sb("gbtG%d" % i, [128, 24], F32) for i in range(2)])
        dg = Rot(P, [sb("dgG%d" % i, [128, 4, 128], F32) for i in range(2)])
        psG = psum("psGG", [128, 4, 128], F32)
        b_psG = P.buf()
        pT = Rot(P, [psum("psTG", [128, 8, 128], BF16)])
        pwP = Rot(P, [psum("pswPG%d" % i, [128, 4, 128], F32) for i in range(2)])
        pwD = Rot(P, [psum("pswDG%d" % i, [128, 4, 128], F32) for i in range(2)])
        pwA = pwD
        pwR = Rot(P, [psum("pswRG%d" % i, [128, 4, 128], F32) for i in range(2)])
        identf4 = sb("identf4G", [128, 4, 128], F32)
        for j in range(4):
            k.copy("pool", identf4[:, j, :], identf[:], [b_c], [b_c])

        def rot(name, n, dt):
            return Rot(P, [sb("%sG%d" % (name, i), [128, 4, 128], dt) for i in range(n)])

        sc3 = Rot(P, [sb("sc3G%d" % i, [128, 64], F32) for i in range(3)])
        kbg = rot("kbg", 6, BF16); kdec = rot("kdec", 6, BF16); vb = rot("vb", 6, BF16)
        aT = rot("aT", 6, BF16); qd = rot("qd", 6, BF16)
        tmp = rot("tmp", 3, F32); Et = rot("E", 3, F32); Eg = rot("Eg", 2, F32)
        def set4(name, n_):
            t = [[[sb("%sG%d_%d_%d" % (name, p_, h, i), [128, 4, 128], BF16) for i in range(n_)] for h in range(2)] for p_ in range(2)]
            b = [[[P.buf() for i in range(n_)] for h in range(2)] for p_ in range(2)]
            return t, b
        Pk, bPk = set4("Pk", 2); Qk, bQk = set4("Qk", 2); Ak, bAk = set4("Ak", 2); Bk, bBk = set4("Bk", 2)
        NTb = [[sb("NTbG%d_%d" % (p_, h), [128, 4, 128], BF16) for h in range(2)] for p_ in range(2)]
        bNT = [[P.buf() for h in range(2)] for p_ in range(2)]
        Xk = [[sb("XkG%d_%d" % (p_, h), [128, 4, 128], BF16) for h in range(2)] for p_ in range(2)]
        bXk = [[P.buf() for h in range(2)] for p_ in range(2)]
        maskd = sb("maskdG", [128, 4, 128], F32); mX1 = sb("mX1G", [128, 4, 128], F32)
        mX2 = sb("mX2G", [128, 4, 128], F32); mX3 = sb("mX3G", [128, 4, 128], F32)
        for dst_, src_ in ((maskd, "maskd4"), (mX1, "mX1_4"), (mX2, "mX2_4"), (mX3, "mX3_4")):
            P.dma("c0", dst_[:], T[src_], w=[b_c])
        TT = rot("TT", 4, BF16); uf = rot("uf", 2, F32); nwT = rot("nwT", 2, BF16)
        vn = rot("vn", 2, BF16); of = rot("of", 2, F32); sq = rot("sq", 2, BF16)
        rs = rot("rs", 2, F32); on = rot("on", 2, F32)
        b_Sg = [P.buf() for _ in range(2)]
        b_Sbg = [P.buf() for _ in range(2)]
        G, GBc, NG, NBETA, BEG, EDEC, EGL = 0, 16, 8, 24, 32, 40, 48

        tiles = [(s, n) for s in range(NS) for n in range(S // 128)]
        blk = {}
        ctx = {}

        def load_block(s, tb):
            t0 = tb * 256
            kb_, bkb = kblk.next(); qb_, bqb = qblk.next(); vb_, bvb = vblk.next()
            gb_, bgb = gblk.next(); ob_, bob = oblk.next()
            P.dma("gk%d" % kblk.i, kb_[:], T["kT_s"][s].rearrange("h d t -> d h t")[:, :, t0:t0 + 256], w=[bkb])
            P.dma("gq%d" % qblk.i, qb_[:], T["qT_s"][s].rearrange("h d t -> d h t")[:, :, t0:t0 + 256], w=[bqb])
            P.dma("gv%d" % vblk.i, vb_[:], T["vT_s"][s].rearrange("h d t -> d h t")[:, :, t0:t0 + 256], w=[bvb])
            P.dma("gg%d" % gblk.i, gb_[:], T["gateT_s"][s].rearrange("(h d) t -> d h t", d=128)[:, :, t0:t0 + 256], w=[bgb])
            blk[(s, tb)] = dict(k=(kb_, bkb), q=(qb_, bqb), v=(vb_, bvb), g=(gb_, bgb), o=(ob_, bob))

        def stream_P(ti):
            s, n = tiles[ti]
            tb, tt = n // 2, n % 2
            if tt == 0:
                load_block(s, tb)
            B = blk[(s, tb)]
            kb_, bkb = B["k"]; qb_, bqb = B["q"]; vb_, bvb = B["v"]
            c0, c1 = tt * 128, (tt + 1) * 128
            par = ti % 2
            cx = ctx[ti] = dict(par=par, B=B, c0=c0, c1=c1, s=s, n=n)
            gt, bgt = gbt.next()
            P.dma("gt%d" % gbt.i, gt[:], T["gbt_s"][s, n * 128:(n + 1) * 128, :], w=[bgt])
            sct, bsc = sc3.next()
            cx["sc"] = (sct, bsc)
            pg, bpg = pwP.next()
            k.mm(pg[:, 0, 0:8], utri[:], gt[:, 0:8], True, True, [b_c, bgt], [bpg])
            k.mm(pg[:, 1, 0:8], onesf[:], gt[:, 0:8], True, True, [b_c, bgt], [bpg])
            k.copy("dve", sct[:, 0:8], pg[:, 0, 0:8], [bpg], [bsc])
            k.ts("dve", sct[:, 24:32], gt[:, 8:16], -1.0, None, ALU.mult, None, [bgt], [bsc])
            k.act(sct[:, 56:64], pg[:, 0, 0:8], AF.Exp, [bpg], [bsc])
            k.tt("dve", sct[:, 32:40], gt[:, 8:16], sct[:, 56:64], ALU.mult, [bgt, bsc], [bsc])
            k.tt("dve", sct[:, 40:48], pg[:, 1, 0:8], sct[:, 0:8], ALU.subtract, [bpg, bsc], [bsc])
            k.act(sct[:, 40:48], sct[:, 40:48], AF.Exp, [bsc], [bsc])
            k.act(sct[:, 48:56], pg[:, 1, 0:8], AF.Exp, [bpg], [bsc])
            yield
            for hh in range(2):
                H0 = hh * 4
                dgt, bdg = dg.next()
                for j in range(4):
                    h = H0 + j
                    k.ts("dve", dgt[:, j, :], identf[:], sct[:, G + h:G + h + 1], None, ALU.mult, None, [b_c, bsc], [bdg])
                k.mm(psG[:], onesf[:], dgt[:], True, True, [b_c, bdg], [b_psG])
                p1, bp1 = pT.next()
                for j in range(4):
                    k.tr(p1[:, j, :], kb_[:, H0 + j, c0:c1], ident[:], [bkb, b_c], [bp1])
                    k.tr(p1[:, 4 + j, :], vb_[:, H0 + j, c0:c1], ident[:], [bvb, b_c], [bp1])
                kbg_, bkbg = kbg.next(); kdec_, bkdec = kdec.next(); vb2, bvb2 = vb.next()
                for j in range(4):
                    h = H0 + j
                    k.ts("dve", kbg_[:, j, :], p1[:, j, :], sct[:, BEG + h:BEG + h + 1], None, ALU.mult, None, [bp1, bsc], [bkbg])
                    k.ts("dve", kdec_[:, j, :], p1[:, j, :], sct[:, EDEC + h:EDEC + h + 1], None, ALU.mult, None, [bp1, bsc], [bkdec])
                    k.ts("dve", vb2[:, j, :], p1[:, 4 + j, :], gt[:, 8 + h:9 + h], None, ALU.mult, None, [bp1, bgt], [bvb2])
                yield
                pKK, bKK = pwP.next()
                for j in range(4):
                    k.mm(pKK[:, j, :], kb_[:, H0 + j, c0:c1], kb_[:, H0 + j, c0:c1], True, True, [bkb], [bKK])
                pQK, bQK = pwP.next()
                for j in range(4):
                    k.mm(pQK[:, j, :], kb_[:, H0 + j, c0:c1], qb_[:, H0 + j, c0:c1], True, True, [bkb, bqb], [bQK])
                t_, bt_ = tmp.next(); e_, be_ = Et.next()
                for j in range(4):
                    h = H0 + j
                    k.stt("dve", t_[:, j, :], psG[:, j, :], sct[:, G + h:G + h + 1], mS[:], ALU.subtract, ALU.subtract, [b_psG, bsc, b_c], [bt_])
                k.act(e_[:], t_[:], AF.Exp, [bt_], [be_], scale=-1.0)
                for j in range(4):
                    h = H0 + j
                    k.stt("dve", NTb[par][hh][:, j, :], pKK[:, j, :], sct[:, NBETA + h:NBETA + h + 1], e_[:, j, :], ALU.mult, ALU.mult,
                          [bKK, bsc, be_], [bNT[par][hh]])
                yield
                t_, bt_ = tmp.next(); e_, be_ = Et.next()
                for j in range(4):
                    h = H0 + j
                    k.stt("dve", t_[:, j, :], psG[:, j, :], sct[:, G + h:G + h + 1], mIT[:], ALU.subtract, ALU.add, [b_psG, bsc, b_c], [bt_])
                k.act(e_[:], t_[:], AF.Exp, [bt_], [be_])
                aT_, baT = aT.next()
                k.tt("dve", aT_[:], pQK[:], e_[:], ALU.mult, [bQK, be_], [baT])
                eg_, beg = Eg.next()
                k.act(eg_[:], psG[:], AF.Exp, [b_psG], [beg])
                qd_, bqd = qd.next()
                k.tt("pool", qd_[:], qb_[:, H0:H0 + 4, c0:c1], eg_[:], ALU.mult, [bqb, beg], [bqd])
                pN, bpN = pT.next()
                for j in range(4):
                    k.tr(pN[:, j, :], NTb[par][hh][:, j, :], ident[:], [bNT[par][hh], b_c], [bpN])
                k.tt("dve", Pk[par][hh][0][:], pN[:, 0:4, :], maskd[:], ALU.mult, [bpN, b_c], [bPk[par][hh][0]])
                k.tt("pool", Qk[par][hh][0][:], NTb[par][hh][:], maskd[:], ALU.mult, [bNT[par][hh], b_c], [bQk[par][hh][0]])
                k.tt("pool", Ak[par][hh][0][:], Pk[par][hh][0][:], identf4[:], ALU.add, [bPk[par][hh][0], b_c], [bAk[par][hh][0]])
                k.tt("pool", Bk[par][hh][0][:], Qk[par][hh][0][:], identf4[:], ALU.add, [bQk[par][hh][0], b_c], [bBk[par][hh][0]])
                cx[hh] = dict(kbg=(kbg_, bkbg), kdec=(kdec_, bkdec), vb=(vb2, bvb2), aT=(aT_, baT), qd=(qd_, bqd))
                yield

        def stream_D(ti):
            cx = ctx[ti]
            par = cx["par"]

            def grp(hh):
                return (Pk[par][hh], Qk[par][hh], Ak[par][hh], Bk[par][hh], bPk[par][hh], bQk[par][hh], bAk[par][hh], bBk[par][hh])

            def mm4(ps_, bps_, lhs, blhs, rhs, brhs):
                for j in range(4):
                    k.mm(ps_[:, j, :], lhs[:, j, :], rhs[:, j, :], True, True, [blhs, brhs], [bps_])

            for hh in range(2):
                Pc, Qc, Ac, Bc, bPc, bQc, bAc, bBc = grp(hh)
                p_, bp_ = pwD.next(); mm4(p_, bp_, Qc[0], bQc[0], Pc[0], bPc[0])
                q_, bq_ = pwD.next(); mm4(q_, bq_, Pc[0], bPc[0], Qc[0], bQc[0])
                k.copy("act", Pc[1][:], p_[:], [bp_], [bPc[1]])
                k.copy("dve", Qc[1][:], q_[:], [bq_], [bQc[1]])
                yield
            pa, qa, aa = 1, 1, 0
            for lvl in (1, 2, 3):
                for hh in range(2):
                    Pc, Qc, Ac, Bc, bPc, bQc, bAc, bBc = grp(hh)
                    a_, ba_ = pwD.next(); mm4(a_, ba_, Qc[qa], bQc[qa], Ac[aa], bAc[aa])
                    b_, bb_ = pwD.next(); mm4(b_, bb_, Ac[aa], bAc[aa], Qc[qa], bQc[qa])
                    k.tt("dve", Ac[1 - aa][:], a_[:], Ac[aa][:], ALU.add, [ba_, bAc[aa]], [bAc[1 - aa]])
                    k.tt("dve", Bc[1 - aa][:], b_[:], Bc[aa][:], ALU.add, [bb_, bBc[aa]], [bBc[1 - aa]])
                    if lvl < 3:
                        q_, bq_ = pwD.next(); mm4(q_, bq_, Pc[pa], bPc[pa], Qc[qa], bQc[qa])
                        if lvl < 2:
                            p_, bp_ = pwD.next(); mm4(p_, bp_, Qc[qa], bQc[qa], Pc[pa], bPc[pa])
                            k.copy("act", Pc[1 - pa][:], p_[:], [bp_], [bPc[1 - pa]])
                        k.copy("act", Qc[1 - qa][:], q_[:], [bq_], [bQc[1 - qa]])
                    yield
                aa = 1 - aa
                pa, qa = 1 - pa, 1 - qa
            for lvl in (1, 2, 3):
                mX = (mX1, mX2, mX3)[lvl - 1]
                for hh in range(2):
                    Pc, Qc, Ac, Bc, bPc, bQc, bAc, bBc = grp(hh)
                    x_, bx_ = pwD.next(); mm4(x_, bx_, NTb[par][hh], bNT[par][hh], Ac[aa], bAc[aa])
                    k.tt("dve", Xk[par][hh][:], x_[:], mX[:], ALU.mult, [bx_, b_c], [bXk[par][hh]])
                    yield
                for hh in range(2):
                    Pc, Qc, Ac, Bc, bPc, bQc, bAc, bBc = grp(hh)
                    u_, bu_ = pwD.next(); mm4(u_, bu_, Bc[aa], bBc[aa], Xk[par][hh], bXk[par][hh])
                    if lvl < 3:
                        k.tt("dve", Ac[1 - aa][:], u_[:], Ac[aa][:], ALU.add, [bu_, bAc[aa]], [bAc[1 - aa]])
                        v_, bv_ = pwD.next(); mm4(v_, bv_, Xk[par][hh], bXk[par][hh], Bc[aa], bBc[aa])
                        k.tt("dve", Bc[1 - aa][:], v_[:], Bc[aa][:], ALU.add, [bv_, bBc[aa]], [bBc[1 - aa]])
                    else:
                        TT_, bTT = TT.next()
                        k.tt("dve", TT_[:], u_[:], Ac[aa][:], ALU.add, [bu_, bAc[aa]], [bTT])
                        cx[hh]["TT"] = (TT_, bTT)
                    yield
                aa = 1 - aa

        def stream_R(ti):
            cx = ctx[ti]
            sct, bsc = cx["sc"]
            B = cx["B"]; c0, c1 = cx["c0"], cx["c1"]
            gb_, bgb = B["g"]; ob_, bob = B["o"]
            for hh in range(2):
                H0 = hh * 4
                TT_, bTT = cx[hh]["TT"]
                vb2, bvb2 = cx[hh]["vb"]; kbg_, bkbg = cx[hh]["kbg"]
                aT_, baT = cx[hh]["aT"]; qd_, bqd = cx[hh]["qd"]; kdec_, bkdec = cx[hh]["kdec"]
                pu, bpu = pwR.next()
                for j in range(4):
                    k.mm(pu[:, j, :], TT_[:, j, :], vb2[:, j, :], True, True, [bTT, bvb2], [bpu])
                uf_, buf_ = uf.next()
                k.copy("act", uf_[:], pu[:], [bpu], [buf_])
                pW, bpW = pwR.next()
                for j in range(4):
                    k.mm(pW[:, j, :], kbg_[:, j, :], TT_[:, j, :], True, True, [bkbg, bTT], [bpW])
                nw_, bnw = nwT.next()
                k.ts("dve", nw_[:], pW[:], -1.0, None, ALU.mult, None, [bpW], [bnw])
                yield
                pws, bpws = pwR.next()
                for j in range(4):
                    k.mm(pws[:, j, :], nw_[:, j, :], Sb[:, H0 + j, :], True, True, [bnw, b_Sbg[hh]], [bpws])
                vn_, bvn = vn.next()
                k.tt("dve", vn_[:], pws[:], uf_[:], ALU.add, [bpws, buf_], [bvn])
                po, bpo = pwR.next()
                for j in range(4):
                    k.mm(po[:, j, :], Sb[:, H0 + j, :], qd_[:, j, :], True, False, [b_Sbg[hh], bqd], [bpo])
                    k.mm(po[:, j, :], vn_[:, j, :], aT_[:, j, :], False, True, [bvn, baT], [bpo])
                of_, bof = of.next()
                k.copy("act", of_[:], po[:], [bpo], [bof])
                yield
                pS, bpS = pwR.next()
                for j in range(4):
                    k.mm(pS[:, j, :], kdec_[:, j, :], vn_[:, j, :], True, True, [bkdec, bvn], [bpS])
                for j in range(4):
                    h = H0 + j
                    k.stt("dve", Sf[:, h, :], Sf[:, h, :], sct[:, EGL + h:EGL + h + 1], pS[:, j, :], ALU.mult, ALU.add,
                          [b_Sg[hh], bsc, bpS], [b_Sg[hh]])
                k.copy("act", Sb[:, H0:H0 + 4, :], Sf[:, H0:H0 + 4, :], [b_Sg[hh]], [b_Sbg[hh]])
                sq_, bsq = sq.next()
                k.tt("pool", sq_[:], of_[:], of_[:], ALU.mult, [bof], [bsq])
                yield
                pss, bpss = pwR.next()
                k.mm(pss[:], ones[:], sq_[:], True, True, [b_c, bsq], [bpss])
                rs_, brs = rs.next()
                k.act(rs_[:], pss[:], AF.Ln, [bpss, b_c], [brs], bias=epst[:, 0:1], scale=1.0 / 128)
                k.act(rs_[:], rs_[:], AF.Exp, [brs], [brs], scale=-0.5)
                on_, bon = on.next()
                k.stt("dve", on_[:], of_[:], ggdn[:, 0:1], rs_[:], ALU.mult, ALU.mult, [bof, b_c, brs], [bon])
                k.tt("pool", ob_[:, H0:H0 + 4, c0:c1], on_[:], gb_[:, H0:H0 + 4, c0:c1], ALU.mult, [bon, bgb], [bob])
                yield
            s, n = cx["s"], cx["n"]
            if n % 2 == 1:
                t0 = (n // 2) * 256
                P.dma("go%d" % (n // 2 % 2), T["oaT_s"][s].rearrange("(h d) t -> d h t", d=128)[:, :, t0:t0 + 256], ob_[:], r=[bob])

        def drain(g):
            for _ in g:
                pass

        def interleave(main, others):
            others = [o for o in others if o is not None]
            for _ in main:
                for o in list(others):
                    try:
                        next(o)
                    except StopIteration:
                        others.remove(o)
            for o in others:
                drain(o)

        NTt = len(tiles)
        for ti in range(NTt):
            s, n = tiles[ti]
            if n == 0:
                if ti > 0:
                    drain(stream_R(ti - 1))
                for hh in range(2):
                    k.memset("dve", Sf[:, hh * 4:hh * 4 + 4, :], 0.0, [b_Sg[hh]])
                    k.memset("pool", Sb[:, hh * 4:hh * 4 + 4, :], 0.0, [b_Sbg[hh]])
                drain(stream_P(ti))
            nxtP = stream_P(ti + 1) if (ti + 1 < NTt and tiles[ti + 1][1] != 0) else None
            prvR = stream_R(ti - 1) if (ti > 0 and n != 0) else None
            interleave(stream_D(ti), [nxtP, prvR])
        drain(stream_R(NTt - 1))
        P.emit()
    P.barrier()


def phase_M(nc, P, k, S, NS, T):
    NB = S // 512
    NT = S // 128
    scale = float(192 ** -0.5)
    with contextlib.ExitStack() as ph:
        def sb(name, shape, dt):
            return ph.enter_context(nc.sbuf_tensor(name, shape, dt))

        def psum(name, shape, dt):
            return ph.enter_context(nc.psum_tensor(name, shape, dt))

        ident = sb("identM", [128, 128], BF16)
        tri = sb("triM", [128, 128], BF16)
        wuq = sb("wuqM", [128, 3, 2048], BF16)
        wuk = sb("wukM", [128, 2, 1024], BF16)
        wuv = sb("wuvM", [128, 2, 1024], BF16)
        cosT = sb("cosM", [64, S], F32)
        sinT = sb("sinM", [64, S], F32)
        b_c = P.buf()
        P.dma("c0", ident[:], T["ident_bf"], w=[b_c])
        P.dma("c0", tri[:], T["tri_bf"], w=[b_c])
        P.dma("c1", wuq[:], T["wuq_bf"].rearrange("(c p) n -> p c n", p=128), w=[b_c])
        P.dma("c2", wuk[:], T["wuk_bf"].rearrange("(c p) n -> p c n", p=128), w=[b_c])
        P.dma("c3", wuv[:], T["wuv_bf"].rearrange("(c p) n -> p c n", p=128), w=[b_c])
        P.dma("c1", cosT[:], T["cosT"], w=[b_c])
        P.dma("c2", sinT[:], T["sinT"], w=[b_c])
        cq = sb("cqM", [128, 3, S], BF16)
        ckv = sb("ckvM", [128, 2, S], BF16)
        kpe = sb("kpeM", [128, S], BF16)
        b_in = P.buf()
        qn = sb("qnM", [128, S], BF16)
        qr = sb("qrM", [128, S], BF16)
        kn = sb("knM", [128, S], BF16)
        vv = sb("vvM", [128, NT, 128], BF16)
        b_qn, b_qr, b_kn, b_vv = P.buf(), P.buf(), P.buf(), P.buf()
        b_pad = P.buf()
        k.memset("pool", kpe[64:128, :], 0.0, [b_pad])
        k.memset("pool", qr[64:128, :], 0.0, [b_pad])
        t1 = Rot(P, [sb("t1M%d" % i, [64, 512], F32) for i in range(2)])
        t2 = Rot(P, [sb("t2M%d" % i, [64, 512], F32) for i in range(2)])
        pT = Rot(P, [sb("pTM%d" % i, [128, 512], BF16) for i in range(4)])
        rec = Rot(P, [sb("recM%d" % i, [128, 512], F32) for i in range(2)])
        acc = Rot(P, [sb("accM%d" % i, [128, 512], F32) for i in range(2)])
        onesb = sb("onesbM", [128, 128], BF16)
        k.memset("dve", onesb[:], 1.0, [b_c])
        obT = Rot(P, [sb("obTM%d" % i, [128, 512], BF16) for i in range(2)])
        ps_s = Rot(P, [psum("pssM%d" % i, [128, 512], F32) for i in range(3)])
        po = Rot(P, [psum("poM%d" % i, [128, 512], F32) for i in range(2)])
        prs_r = Rot(P, [psum("prsM%d" % i, [128, 512], F32) for i in range(2)])
        ppv = Rot(P, [psum("ppvM", [128, 4, 128], F32)])

        for s in range(NS):
            P.dma("mi0", cq[:], T["cqT_s"][s].rearrange("(c p) t -> p c t", p=128), w=[b_in])
            P.dma("mi1", ckv[:], T["ckvT_s"][s].rearrange("(c p) t -> p c t", p=128), w=[b_in])
            P.dma("mi2", kpe[0:64, :], T["kpeT_s"][s], w=[b_in])
            for h in range(8):
                for tb in range(NB):
                    t0 = tb * 512
                    pz, bpz = ps_s.next()
                    for r in range(3):
                        k.mm(pz[:], wuq[:, r, h * 256:h * 256 + 128], cq[:, r, t0:t0 + 512], r == 0, r == 2, [b_c, b_in], [bpz])
                    k.copy("act", qn[:, t0:t0 + 512], pz[:], [bpz], [b_qn])
                    pa, bpa = ps_s.next()
                    for r in range(3):
                        k.mm(pa[0:64, :], wuq[:, r, h * 256 + 128:h * 256 + 192], cq[:, r, t0:t0 + 512], r == 0, r == 2, [b_c, b_in], [bpa])
                    a1, ba1 = t1.next()
                    k.tt("dve", a1[:], pa[0:64, :], cosT[:, t0:t0 + 512], ALU.mult, [bpa, b_c], [ba1])
                    pb, bpb = ps_s.next()
                    for r in range(3):
                        k.mm(pb[0:64, :], wuq[:, r, h * 256 + 192:h * 256 + 256], cq[:, r, t0:t0 + 512], r == 0, r == 2, [b_c, b_in], [bpb])
                    a2, ba2 = t2.next()
                    k.tt("dve", a2[:], pb[0:64, :], sinT[:, t0:t0 + 512], ALU.mult, [bpb, b_c], [ba2])
                    k.tt("pool", qr[0:64, t0:t0 + 512], a1[:], a2[:], ALU.add, [ba1, ba2], [b_qr])
                    pk, bpk = ps_s.next()
                    for r in range(2):
                        k.mm(pk[:], wuk[:, r, h * 128:(h + 1) * 128], ckv[:, r, t0:t0 + 512], r == 0, r == 1, [b_c, b_in], [bpk])
                    k.copy("act", kn[:, t0:t0 + 512], pk[:], [bpk], [b_kn])
                    pv, bpv = ppv.next()
                    for tt in range(4):
                        for r in range(2):
                            k.mm(pv[:, tt, :], ckv[:, r, t0 + tt * 128:t0 + (tt + 1) * 128], wuv[:, r, h * 128:(h + 1) * 128],
                                 r == 0, r == 1, [b_c, b_in], [bpv])
                    k.copy("dve", vv[:, tb * 4:(tb + 1) * 4, :], pv[:], [bpv], [b_vv])
                items = [(qg, kb) for qg in range(NB) for kb in range(4 * (qg + 1))]

                def st_S(it, cx):
                    qg, kb = it
                    q0 = qg * 512
                    r_ = kb - 4 * qg
                    qlo = 128 * r_ if r_ > 0 else 0
                    pss, bps = ps_s.next()
                    k.mm(pss[:, qlo:512], kn[:, kb * 128:(kb + 1) * 128], qn[:, q0 + qlo:q0 + 512], True, False, [b_kn, b_qn], [bps])
                    k.mm(pss[:, qlo:512], kpe[:, kb * 128:(kb + 1) * 128], qr[:, q0 + qlo:q0 + 512], False, True, [b_in, b_qr, b_pad], [bps])
                    cx["pss"] = (pss, bps); cx["r"] = r_; cx["qlo"] = qlo

                def st_E(it, cx):
                    qg, kb = it
                    pss, bps = cx["pss"]; r_ = cx["r"]; qlo = cx["qlo"]
                    pt_, bpt = pT.next()
                    k.act(pt_[:, qlo:512], pss[:, qlo:512], AF.Exp, [bps], [bpt], scale=scale)
                    if r_ >= 0:
                        k.tt("pool", pt_[:, 128 * r_:128 * (r_ + 1)], pt_[:, 128 * r_:128 * (r_ + 1)], tri[:], ALU.mult, [bpt, b_c], [bpt])
                    cx["pt"] = (pt_, bpt)

                def st_V(it, cx, s=s, h=h):
                    qg, kb = it
                    q0 = qg * 512
                    pt_, bpt = cx["pt"]; qlo = cx["qlo"]
                    if kb == 0:
                        st_V.po = po.next()
                        st_V.prs = prs_r.next()
                    po_, bpo = st_V.po
                    prs, bprs = st_V.prs
                    lastk = (kb == 4 * qg + 3)
                    k.mm(po_[:, qlo:512], vv[:, kb, :], pt_[:, qlo:512], kb == 0, lastk, [bpt, b_vv], [bpo])
                    k.mm(prs[:, qlo:512], onesb[:], pt_[:, qlo:512], kb == 0, lastk, [bpt, b_c], [bprs])
                    if lastk:
                        rc, brc = rec.next()
                        k.act(rc[:], prs[:], AF.Ln, [bprs], [brc])
                        k.act(rc[:], rc[:], AF.Exp, [brc], [brc], scale=-1.0)
                        oT, boT = obT.next()
                        k.tt("dve", oT[:], po_[:], rc[:], ALU.mult, [bpo, brc], [boT])
                        P.dma("mo%d" % obT.i, T["obT_s"][s, h * 128:(h + 1) * 128, q0:q0 + 512], oT[:], r=[boT], q="sp")

                pipelined(items, [st_S, st_E, st_V])
        P.emit()
    P.barrier()


def phase_C1(nc, P, k, S, NS, T):
    NB = S // 512
    with contextlib.ExitStack() as ph:
        def sb(name, shape, dt):
            return ph.enter_context(nc.sbuf_tensor(name, shape, dt))

        def psum(name, shape, dt):
            return ph.enter_context(nc.psum_tensor(name, shape, dt))

        ident = sb("identC", [128, 128], BF16)
        wog = sb("wogC", [128, 8, D], BF16)
        wom = sb("womC", [128, 8, D], BF16)
        wout = sb("woutC", [128, 8, D], BF16)
        gffn = sb("gffnC", [128, D], F32)
        epst = sb("epsC", [128, 1], F32)
        b_c = P.buf()
        P.dma("c0", ident[:], T["ident_bf"], w=[b_c])
        P.dma("c1", wog[:], T["wog_bf"].rearrange("(c p) n -> p c n", p=128), w=[b_c])
        P.dma("c2", wom[:], T["wom_bf"].rearrange("(c p) n -> p c n", p=128), w=[b_c])
        P.dma("c3", wout[:], T["wout_bf"].rearrange("(c p) n -> p c n", p=128), w=[b_c])
        P.dma("c0", gffn[:], T["gffn_bc"], w=[b_c])
        k.memset("dve", epst[:], EPS, [b_c])
        oa = Rot(P, [sb("oaC%d" % i, [128, 8, 512], BF16) for i in range(2)])
        ob = Rot(P, [sb("obC%d" % i, [128, 8, 512], BF16) for i in range(2)])
        ga = Rot(P, [sb("gaC%d" % i, [128, 8, 512], BF16) for i in range(2)])
        gb = Rot(P, [sb("gbC%d" % i, [128, 8, 512], BF16) for i in range(2)])
        mg = Rot(P, [sb("mgC%d" % i, [128, 8, 512], BF16) for i in range(2)])
        h2T = Rot(P, [sb("h2TC%d" % i, [128, 8, 512], BF16) for i in range(2)])
        ta = Rot(P, [sb("taC%d" % i, [128, 512], F32) for i in range(2)])
        tb_ = Rot(P, [sb("tbC%d" % i, [128, 512], F32) for i in range(2)])
        xt = Rot(P, [sb("xtC%d" % i, [128, D], F32) for i in range(2)])
        x1 = Rot(P, [sb("x1C%d" % i, [128, D], F32) for i in range(2)])
        hb = Rot(P, [sb("hbC%d" % i, [128, D], BF16) for i in range(2)])
        ss = Rot(P, [sb("ssC%d" % i, [128, 1], F32) for i in range(2)])
        junk = sb("junkC", [128, D], BF16)
        b_junk = P.buf()
        ps = Rot(P, [psum("psC%d" % i, [128, 512], F32) for i in range(6)])
        ps_t = Rot(P, [psum("pstC%d" % i, [128, 8, 128], BF16) for i in range(2)])
        for s in range(NS):
            for tb in range(NB):
                t0 = tb * 512
                oa_, boa = oa.next(); ob_, bob = ob.next(); ga_, bga = ga.next(); gb_, bgb = gb.next()
                vw = lambda nm: T[nm][s].rearrange("(c p) t -> p c t", p=128)[:, :, t0:t0 + 512]
                P.dma("ca%d" % oa.i, oa_[:], vw("oaT_s"), w=[boa])
                P.dma("cb%d" % ob.i, ob_[:], vw("obT_s"), w=[bob])
                P.dma("cc%d" % ga.i, ga_[:], vw("gbaT_s"), w=[bga])
                P.dma("cd%d" % gb.i, gb_[:], vw("gbbT_s"), w=[bgb])
                mg_, bmg = mg.next()
                for dch in range(8):
                    pa, bpa = ps.next()
                    for e in range(8):
                        k.mm(pa[:], wog[:, e, dch * 128:(dch + 1) * 128], oa_[:, e, :], e == 0, e == 7, [b_c, boa], [bpa])
                    pb, bpb = ps.next()
                    for e in range(8):
                        k.mm(pb[:], wom[:, e, dch * 128:(dch + 1) * 128], ob_[:, e, :], e == 0, e == 7, [b_c, bob], [bpb])
                    a_, ba_ = ta.next(); b2, bb2 = tb_.next()
                    k.tt("dve", a_[:], pa[:], ga_[:, dch, :], ALU.mult, [bpa, bga], [ba_])
                    k.tt("dve", b2[:], pb[:], gb_[:, dch, :], ALU.mult, [bpb, bgb], [bb2])
                    k.tt("pool", mg_[:, dch, :], a_[:], b2[:], ALU.add, [ba_, bb2], [bmg])
                h2_, bh2 = h2T.next()
                for tt in range(4):
                    c0, c1 = tt * 128, (tt + 1) * 128
                    x_, bx = xt.next()
                    P.dma("cx%d" % xt.i, x_[:], T["x"][s, t0 + c0:t0 + c1, :], w=[bx])
                    x1_, bx1 = x1.next()
                    for dh in range(2):
                        po_, bpo = ps.next()
                        for c in range(8):
                            k.mm(po_[:], mg_[:, c, c0:c1], wout[:, c, dh * 512:(dh + 1) * 512], c == 0, c == 7, [bmg, b_c], [bpo])
                        k.tt("dve", x1_[:, dh * 512:(dh + 1) * 512], po_[:], x_[:, dh * 512:(dh + 1) * 512], ALU.add, [bpo, bx], [bx1])
                    P.dma("cs%d" % x1.i, T["x1_s"][s, t0 + c0:t0 + c1, :], x1_[:], r=[bx1])
                    ss_, bss = ss.next()
                    k.act(junk[:], x1_[:], AF.Square, [bx1], [b_junk, bss], accum=ss_[:])
                    k.act(ss_[:], ss_[:], AF.Ln, [bss, b_c], [bss], bias=epst[:, 0:1], scale=1.0 / D)
                    k.act(ss_[:], ss_[:], AF.Exp, [bss], [bss], scale=-0.5)
                    hb_, bhb = hb.next()
                    k.stt("dve", hb_[:], x1_[:], ss_[:, 0:1], gffn[:], ALU.mult, ALU.mult, [bx1, bss, b_c], [bhb])
                    pt, bpt = ps_t.next()
                    for c in range(8):
                        k.tr(pt[:, c, :], hb_[:, c * 128:(c + 1) * 128], ident[:], [bhb, b_c], [bpt])
                    k.copy("act", h2_[:, :, c0:c1], pt[:], [bpt], [bh2])
                P.dma("ch%d" % h2T.i, T["h2T_s"][s].rearrange("(c p) t -> p c t", p=128)[:, :, t0:t0 + 512], h2_[:], r=[bh2])
        P.emit()
    P.barrier()


def phase_C2(nc, P, k, S, NS, T):
    NB = S // 512
    with contextlib.ExitStack() as ph:
        def sb(name, shape, dt):
            return ph.enter_context(nc.sbuf_tensor(name, shape, dt))

        def psum(name, shape, dt):
            return ph.enter_context(nc.psum_tensor(name, shape, dt))

        wup = sb("wupF", [128, 8, 2 * DFF], BF16)
        wdn = sb("wdnF", [128, 22, D], BF16)
        cwf = sb("cwF", [128, 44, 3], F32)
        gfin = sb("gfinF", [128, D], F32)
        epst = sb("epsF", [128, 1], F32)
        b_c = P.buf()
        wv = T["wup_bf"].rearrange("(c p) n -> p c n", p=128)
        for i in range(4):
            P.dma("c%d" % i, wup[:, :, i * 1408:(i + 1) * 1408], wv[:, :, i * 1408:(i + 1) * 1408], w=[b_c])
        P.dma("c0", wdn[:], T["wdn_bf"].rearrange("(c p) n -> p c n", p=128), w=[b_c])
        P.dma("c0", cwf[:], T["cw_ffn"], w=[b_c])
        P.dma("c2", gfin[:], T["gfin_bc"], w=[b_c])
        k.memset("dve", epst[:], EPS, [b_c])
        halo = sb("haloF", [128, 44, 2], F32)
        b_halo = [P.buf() for _ in range(44)]
        h2T = Rot(P, [sb("h2TF", [128, 8, 512], BF16)])
        aT = Rot(P, [sb("aTF", [128, 22, 512], BF16)])
        zc = Rot(P, [sb("zcF%d" % i, [128, 514], F32) for i in range(4)])
        yc = Rot(P, [sb("ycF%d" % i, [128, 512], F32) for i in range(4)])
        sg = Rot(P, [sb("sgF%d" % i, [128, 512], F32) for i in range(2)])
        x1 = Rot(P, [sb("x1F%d" % i, [128, D], F32) for i in range(2)])
        ot = Rot(P, [sb("otF", [128, D], F32)])
        ss = Rot(P, [sb("ssF%d" % i, [128, 1], F32) for i in range(2)])
        junk = sb("junkF", [128, D], BF16)
        b_junk = P.buf()
        ps = Rot(P, [psum("psF%d" % i, [128, 512], F32) for i in range(8)])
        ncv = 0
        for s in range(NS):
            for tb in range(NB):
                t0 = tb * 512
                h2_, bh2 = h2T.next()
                P.dma("fh", h2_[:], T["h2T_s"][s].rearrange("(c p) t -> p c t", p=128)[:, :, t0:t0 + 512], w=[bh2])
                aT_, baT = aT.next()
                def f0(i, cx):
                    cx["pz"] = []
                    for e in (i, 22 + i):
                        pz, bpz = ps.next()
                        for c in range(8):
                            k.mm(pz[:], wup[:, c, e * 128:(e + 1) * 128], h2_[:, c, :], c == 0, c == 7, [b_c, bh2], [bpz])
                        cx["pz"].append((pz, bpz))

                def f1(i, cx):
                    cx["z"] = []
                    for (pz, bpz), e in zip(cx["pz"], (i, 22 + i)):
                        z, bz = zc.next()
                        if tb == 0:
                            k.memset("pool", z[:, 0:2], 0.0, [bz])
                        else:
                            k.copy("pool", z[:, 0:2], halo[:, e, :], [b_halo[e]], [bz])
                        k.copy("act", z[:, 2:514], pz[:], [bpz], [bz])
                        cx["z"].append((z, bz))

                def f2(i, cx):
                    cx["y"] = []
                    for (z, bz), e in zip(cx["z"], (i, 22 + i)):
                        y, by = yc.next()
                        k.ts("dve", y[:], z[:, 0:512], cwf[:, e, 0:1], None, ALU.mult, None, [bz, b_c], [by])
                        for j in range(1, 3):
                            k.stt("dve", y[:], z[:, j:j + 512], cwf[:, e, j:j + 1], y[:], ALU.mult, ALU.add, [bz, by, b_c], [by])
                        k.copy("pool", halo[:, e, :], z[:, 512:514], [bz], [b_halo[e]])
                        cx["y"].append((y, by))

                def f3(i, cx):
                    ys = cx["y"]
                    g_, bg_ = sg.next()
                    k.act(g_[:], ys[0][0][:], AF.Silu, [ys[0][1]], [bg_])
                    k.tt("pool", aT_[:, i, :], g_[:], ys[1][0][:], ALU.mult, [bg_, ys[1][1]], [baT])

                pipelined(list(range(22)), [f0, f1, f2, f3])
                for tt in range(4):
                    c0, c1 = tt * 128, (tt + 1) * 128
                    x1_, bx1 = x1.next()
                    P.dma("fx%d" % x1.i, x1_[:], T["x1_s"][s, t0 + c0:t0 + c1, :], w=[bx1])
                    for dh in range(2):
                        po_, bpo = ps.next()
                        for i in range(22):
                            k.mm(po_[:], aT_[:, i, c0:c1], wdn[:, i, dh * 512:(dh + 1) * 512], i == 0, i == 21, [baT, b_c], [bpo])
                        k.tt("dve", x1_[:, dh * 512:(dh + 1) * 512], po_[:], x1_[:, dh * 512:(dh + 1) * 512], ALU.add, [bpo, bx1], [bx1])
                    ss_, bss = ss.next()
                    k.act(junk[:], x1_[:], AF.Square, [bx1], [b_junk, bss], accum=ss_[:])
                    k.act(ss_[:], ss_[:], AF.Ln, [bss, b_c], [bss], bias=epst[:, 0:1], scale=1.0 / D)
                    k.act(ss_[:], ss_[:], AF.Exp, [bss], [bss], scale=-0.5)
                    o_, bo = ot.next()
                    k.stt("dve", o_[:], x1_[:], ss_[:, 0:1], gfin[:], ALU.mult, ALU.mult, [bx1, bss, b_c], [bo])
                    P.dma("fo", T["out"][s, t0 + c0:t0 + c1, :], o_[:], r=[bo])
        P.emit()
    P.barrier()


def build(S, NS, upto="all", dbg=False):
    nc = bass.Bass("TRN2", target_bir_lowering=False)
    T = {}

    def din(name, shape, dt=F32):
        T[name] = nc.dram_tensor(name, list(shape), dt, kind="ExternalInput").ap()

    def dscr(name, shape, dt):
        kind = "ExternalOutput" if dbg else "Internal"
        T[name] = nc.dram_tensor(name, list(shape), dt, kind=kind).ap()

    din("x", [NS, S, D])
    din("w_in_p", [D, DINP]); din("w_uq_p", [384, 2048]); din("w_uk", [256, 1024]); din("w_uv", [256, 1024])
    din("w_og", [D, D]); din("w_om", [D, D]); din("w_out", [D, D]); din("w_up", [D, 2 * DFF]); din("w_dn", [DFF, D])
    din("ident_bf", [128, 128], BF16); din("ident_f", [128, 128])
    din("gmix_bc", [128, D]); din("gffn_bc", [128, D]); din("gfin_bc", [128, D])
    din("cw_qkv", [128, 24, 4]); din("cw_ffn", [128, 44, 3])
    din("dtb_bc", [128, 8]); din("alog_bc", [128, 8])
    din("gq_p", [128, 3]); din("gkv_p", [128, 2]); din("ggdn_p", [128, 1])
    din("cosT", [64, S]); din("sinT", [64, S])
    din("m_strict", [128, 128]); din("m_strictT", [128, 128]); din("m_inclT", [128, 128]); din("utri", [128, 128])
    din("tri_bf", [128, 128], BF16)
    din("maskd4", [128, 4, 128]); din("mX1_4", [128, 4, 128]); din("mX2_4", [128, 4, 128]); din("mX3_4", [128, 4, 128])
    for nm, shp in (("win_bf", [D, DINP]), ("wuq_bf", [384, 2048]), ("wuk_bf", [256, 1024]), ("wuv_bf", [256, 1024]),
                    ("wog_bf", [D, D]), ("wom_bf", [D, D]), ("wout_bf", [D, D]), ("wup_bf", [D, 2 * DFF]),
                    ("wdn_bf", [DFF, D])):
        T[nm] = nc.dram_tensor(nm, shp, BF16, kind="Internal").ap()
    dscr("qT_s", [NS, 8, 128, S], BF16); dscr("kT_s", [NS, 8, 128, S], BF16); dscr("vT_s", [NS, 8, 128, S], BF16)
    dscr("gateT_s", [NS, D, S], BF16); dscr("gbaT_s", [NS, D, S], BF16); dscr("gbbT_s", [NS, D, S], BF16)
    dscr("cqT_s", [NS, 384, S], BF16); dscr("ckvT_s", [NS, 256, S], BF16); dscr("kpeT_s", [NS, 64, S], BF16)
    dscr("gbt_s", [NS, S, 24], F32)
    dscr("oaT_s", [NS, D, S], BF16); dscr("obT_s", [NS, D, S], BF16)
    dscr("x1_s", [NS, S, D], F32); dscr("h2T_s", [NS, D, S], BF16)
    T["out"] = nc.dram_tensor("out", [NS, S, D], F32, kind="ExternalOutput").ap()

    with contextlib.ExitStack() as es:
        P = Prog(nc, es)
        k = K(P)
        phase0_convert(nc, P, k, [(T["w_in_p"], T["win_bf"]), (T["w_uq_p"], T["wuq_bf"]), (T["w_uk"], T["wuk_bf"]),
                                  (T["w_uv"], T["wuv_bf"]), (T["w_og"], T["wog_bf"]), (T["w_om"], T["wom_bf"]),
                                  (T["w_out"], T["wout_bf"]), (T["w_up"], T["wup_bf"]), (T["w_dn"], T["wdn_bf"])])
        phase_A(nc, P, k, S, NS, T)
        if upto != "A":
            phase_G(nc, P, k, S, NS, T)
        if upto not in ("A", "G"):
            phase_M(nc, P, k, S, NS, T)
        if upto not in ("A", "G", "M"):
            phase_C1(nc, P, k, S, NS, T)
            phase_C2(nc, P, k, S, NS, T)
        P.finish()
    return nc


def host_consts(inp, S):
    f = np.float32
    bf = ml_dtypes.bfloat16
    w_in = np.asarray(inp["w_in"][0], f)
    offs = np.cumsum([0, 3072, 1024, 8, 8, 384, 256, 64, 1024, 1024])
    qkv, gate, a_, b_, cq, ckv, kpe, gba, gbb = [w_in[:, offs[i]:offs[i + 1]] for i in range(9)]
    kps = np.concatenate([kpe[:, 32:], kpe[:, :32]], axis=1)
    c = {}
    c["w_in_p"] = np.ascontiguousarray(np.concatenate([qkv, gate, gba, gbb, cq, ckv, kpe, kps, a_, b_], axis=1))
    wuq = np.asarray(inp["w_uq"][0], f).reshape(384, 8, 192)
    c["w_uq_p"] = np.ascontiguousarray(np.concatenate(
        [wuq[:, :, :128], wuq[:, :, 128:], wuq[:, :, 160:], wuq[:, :, 128:160]], axis=2).reshape(384, 2048))
    wukv = np.asarray(inp["w_ukv"][0], f).reshape(256, 8, 256)
    c["w_uk"] = np.ascontiguousarray(wukv[:, :, :128].reshape(256, 1024))
    c["w_uv"] = np.ascontiguousarray(wukv[:, :, 128:].reshape(256, 1024))
    c["w_og"] = np.ascontiguousarray(inp["w_o_gdn"][0], f)
    c["w_om"] = np.ascontiguousarray(inp["w_o_mla"][0], f)
    c["w_out"] = np.ascontiguousarray(inp["w_out"][0], f)
    c["w_up"] = np.ascontiguousarray(inp["w_up"][0], f)
    c["w_dn"] = np.ascontiguousarray(inp["w_down"][0], f)
    c["ident_bf"] = np.eye(128, dtype=f).astype(bf)
    c["ident_f"] = np.eye(128, dtype=f)
    bc = lambda v: np.ascontiguousarray(np.broadcast_to(np.asarray(v, f).reshape(1, -1), (128, np.asarray(v).size)))
    c["gmix_bc"] = bc(inp["norm_mix_g"][0]); c["gffn_bc"] = bc(inp["norm_ffn_g"][0]); c["gfin_bc"] = bc(inp["norm_final_g"])
    c["cw_qkv"] = np.ascontiguousarray(np.asarray(inp["conv_qkv_w"][0], f).reshape(4, 24, 128).transpose(2, 1, 0))
    c["cw_ffn"] = np.ascontiguousarray(np.asarray(inp["conv_ffn_w"][0], f).reshape(3, 44, 128).transpose(2, 1, 0))
    c["dtb_bc"] = bc(inp["gdn_dt_bias"][0]); c["alog_bc"] = bc(inp["gdn_a_log"][0])
    c["gq_p"] = np.ascontiguousarray(np.asarray(inp["mla_q_norm_g"][0], f).reshape(3, 128).T)
    c["gkv_p"] = np.ascontiguousarray(np.asarray(inp["mla_kv_norm_g"][0], f).reshape(2, 128).T)
    c["ggdn_p"] = np.ascontiguousarray(np.asarray(inp["gdn_norm_g"][0], f).reshape(128, 1))
    inv = (np.float32(10000.0) ** (-(np.arange(32, dtype=f) / np.float32(32)))).astype(f)
    ang = (np.arange(S, dtype=f)[None, :] * inv[:, None]).astype(f)
    cs, sn = np.cos(ang.astype(np.float64)).astype(f), np.sin(ang.astype(np.float64)).astype(f)
    c["cosT"] = np.ascontiguousarray(np.concatenate([cs, cs], 0))
    c["sinT"] = np.ascontiguousarray(np.concatenate([-sn, sn], 0))
    i = np.arange(128)[:, None]; j = np.arange(128)[None, :]
    c["m_strict"] = np.where(i > j, 0.0, NEG).astype(f)
    c["m_strictT"] = np.where(j > i, 0.0, NEG).astype(f)
    c["m_inclT"] = np.where(j >= i, 0.0, NEG).astype(f)
    c["utri"] = (i <= j).astype(f)
    c["tri_bf"] = (j >= i).astype(f).astype(bf)
    rep4 = lambda m_: np.ascontiguousarray(np.broadcast_to(m_.astype(f)[:, None, :], (128, 4, 128)))
    c["maskd4"] = rep4((i // 16) == (j // 16))
    for l_, b_ in ((1, 16), (2, 32), (3, 64)):
        c["mX%d_4" % l_] = rep4(((i // b_) % 2 == 0) & ((j // b_) == (i // b_) + 1))
    return c


_NC_CACHE = {}


def kernel(**inputs):
    x = np.asarray(inputs["x"], np.float32)
    B, S, _ = x.shape
    n = 8
    NS = B // n
    key = (S, NS)
    if key not in _NC_CACHE:
        _NC_CACHE[key] = build(S, NS)
    nc = _NC_CACHE[key]
    c = host_consts(inputs, S)
    in_maps = []
    for i in range(n):
        m = dict(c)
        m["x"] = np.ascontiguousarray(x[i * NS:(i + 1) * NS])
        in_maps.append(m)
    res = run_bass_kernel_spmd(nc, in_maps, core_ids=list(range(n)))
    return np.concatenate([np.asarray(r["out"], np.float32) for r in res.results], axis=0)
```

```python
import contextlib
import numpy as np
import concourse.bass as bass
import concourse.mybir as mybir

F32 = mybir.dt.float32
BF16 = mybir.dt.bfloat16
AF = mybir.ActivationFunctionType
ALU = mybir.AluOpType
AX = mybir.AxisListType


class Buf:
    __slots__ = ("name", "w", "rs")

    def __init__(self, name):
        self.name = name
        self.w = None
        self.rs = []


class Op:
    __slots__ = ("id", "eng", "fn", "deps", "dma", "chan", "val", "sem", "need_sig")

    def __init__(self, id, eng, fn, dma=False, chan=None):
        self.id = id
        self.eng = eng
        self.fn = fn
        self.deps = {}
        self.dma = dma
        self.chan = chan
        self.val = 0
        self.need_sig = False


ENGS = ("pe", "act", "dve", "pool", "sp")


class Prog:
    def __init__(self, nc, es, n_chan=30):
        self.nc = nc
        self.ops = []
        self.nid = 0
        self.esem_sets = [{e: es.enter_context(nc.semaphore("s%d_%s" % (i, e))) for e in ENGS if e != "sp"}
                          for i in range(7)]
        self.phase = -1
        self.esem = self.esem_sets[0]
        self.ecount = {e: 0 for e in self.esem}
        self.chan_sems = {q: [(es.enter_context(nc.semaphore("d%s%d" % (q, i))), 0) for i in range(n_chan if q == "sp" else 18)]
                          for q in ("sp", "pool")}
        self.chan = {}
        self.waited = {e: {} for e in ENGS}
        self.last = {e: None for e in ENGS}
        self.barrier_deps = []
        self.bufs = []

    def buf(self, name="b"):
        b = Buf(name)
        self.bufs.append(b)
        return b

    def bufs_n(self, n, name="b"):
        return [self.buf(name + str(i)) for i in range(n)]

    def _mk(self, eng, fn, r, w, dma=False, chan=None):
        op = Op(self.nid, eng, fn, dma, chan)
        self.nid += 1
        for d in self.barrier_deps:
            op.deps[d] = True
        for b in r:
            if b.w is not None:
                op.deps[b.w] = True
        for b in w:
            if b.w is not None and b.w not in op.deps:
                op.deps[b.w] = False
            for rd in b.rs:
                if rd is not op and rd not in op.deps:
                    op.deps[rd] = False
        for b in r:
            b.rs.append(op)
        for b in w:
            b.w = op
            b.rs = []
        self.ops.append(op)
        self.last[eng] = op
        return op

    def op(self, eng, fn, r=(), w=()):
        return self._mk(eng, fn, r, w)

    def dma(self, chan, out, in_, r=(), w=(), q=None):
        if q is None:
            q = "sp" if len(w) > 0 else "pool"
        if chan not in self.chan:
            if not self.chan_sems[q]:
                raise RuntimeError("out of dma channels")
            sem, cnt = self.chan_sems[q].pop()
            self.chan[chan] = [sem, cnt, None, q]
        c = self.chan[chan]
        op = self._mk(q, lambda e, o=out, i=in_: e.dma_start(out=o, in_=i), r, w, dma=True, chan=chan)
        if c[2] is not None and c[2] not in op.deps:
            op.deps[c[2]] = True
        c[1] += 16
        op.val = c[1]
        op.sem = c[0]
        c[2] = op
        return op

    def barrier(self):
        deps = [o for o in self.last.values() if o is not None]
        deps += [c[2] for c in self.chan.values() if c[2] is not None]
        self.barrier_deps = deps
        for key, c in self.chan.items():
            self.chan_sems[c[3]].append((c[0], c[1]))
        self.chan = {}
        for b in self.bufs:
            b.w = None
            b.rs = []
        self.bufs = []

    def _needs_wait(self, op, d, is_raw):
        if d.dma:
            return True
        if d.eng != op.eng:
            return True
        if op.dma:
            return True
        if op.eng == "pe":
            return False
        return is_raw

    def emit(self):
        nc = self.nc
        ops = self.ops
        self.ops = []
        self.phase += 1
        self.esem = self.esem_sets[self.phase]
        self.ecount = {e: 0 for e in self.esem}
        lastc = {}
        for op in ops:
            if not op.dma:
                lastc[op.eng] = op
        for op in lastc.values():
            op.need_sig = True
        for op in ops:
            for d, raw in op.deps.items():
                if not d.dma and self._needs_wait(op, d, raw):
                    d.need_sig = True
        for op in ops:
            if not op.dma and op.need_sig and op.val == 0:
                self.ecount[op.eng] += 1
                op.val = self.ecount[op.eng]
                op.sem = self.esem[op.eng]
        by_eng = {e: [o for o in ops if o.eng == e] for e in ENGS}

        def run(eng_name, eng):
            waited = self.waited[eng_name]
            for op in by_eng[eng_name]:
                need = {}
                for d, raw in op.deps.items():
                    if not self._needs_wait(op, d, raw):
                        continue
                    sem = d.sem
                    assert d.val > 0, (d.id, d.eng, d.dma)
                    k = id(sem)
                    if waited.get(k, 0) >= d.val:
                        continue
                    if k not in need or need[k][1] < d.val:
                        need[k] = (sem, d.val)
                for k, (sem, val) in need.items():
                    eng.wait_ge(sem, val)
                    waited[k] = val
                ins = op.fn(eng)
                if op.dma:
                    ins.then_inc(op.sem, 16)
                elif op.need_sig:
                    ins.then_inc(op.sem, 1)

        with nc.Block() as block:
            @block.tensor
            def _(e):
                run("pe", e)

            @block.scalar
            def _(e):
                run("act", e)

            @block.vector
            def _(e):
                run("dve", e)

            @block.gpsimd
            def _(e):
                run("pool", e)

            @block.sync
            def _(e):
                run("sp", e)

    def finish(self):
        nc = self.nc
        with nc.Block() as block:
            @block.sync
            def _(e):
                for sem, cnt in self.chan_sems["sp"] + self.chan_sems["pool"] + [(c[0], c[1]) for c in self.chan.values()]:
                    if cnt > 0:
                        e.wait_ge(sem, cnt)

import ml_dtypes
from concourse.bass_utils import run_bass_kernel_spmd

D = 1024
DINP = 6928
OQ, OG, OBA, OBB, OCQ, OCKV, OKPE, OKPS, OAB = 0, 3072, 4096, 5120, 6144, 6528, 6784, 6848, 6912
DFF = 2816
EPS = 1e-6
NEG = -30000.0


class K:
    def __init__(self, P):
        self.P = P

    def act(self, out, in_, func, r, w, bias=None, scale=None, accum=None, eng="act"):
        kw = {}
        if bias is not None:
            kw["bias"] = bias
        if scale is not None:
            kw["scale"] = scale
        if accum is not None:
            kw["accum_out"] = accum
        return self.P.op(eng, lambda e: e.activation(out=out, in_=in_, func=func, **kw), r, w)

    def tt(self, eng, out, in0, in1, op, r, w):
        return self.P.op(eng, lambda e: e.tensor_tensor(out=out, in0=in0, in1=in1, op=op), r, w)

    def ts(self, eng, out, in0, s1, s2, op0, op1, r, w):
        if s2 is None:
            return self.P.op(eng, lambda e: e.tensor_scalar(out=out, in0=in0, scalar1=s1, scalar2=None, op0=op0), r, w)
        return self.P.op(eng, lambda e: e.tensor_scalar(out=out, in0=in0, scalar1=s1, scalar2=s2, op0=op0, op1=op1), r, w)

    def stt(self, eng, out, in0, scalar, in1, op0, op1, r, w):
        return self.P.op(eng, lambda e: e.scalar_tensor_tensor(out=out, in0=in0, scalar=scalar, in1=in1, op0=op0, op1=op1), r, w)

    def copy(self, eng, out, in_, r, w):
        if eng == "act":
            return self.P.op(eng, lambda e: e.activation(out=out, in_=in_, func=AF.Copy), r, w)
        return self.P.op(eng, lambda e: e.tensor_copy(out=out, in_=in_), r, w)

    def memset(self, eng, ap, val, w):
        return self.P.op(eng, lambda e: e.memset(ap, val), (), w)

    def mm(self, out, lhsT, rhs, start, stop, r, w):
        return self.P.op("pe", lambda e: e.matmul(out, lhsT=lhsT, rhs=rhs, start=start, stop=stop), r, w)

    def tr(self, out, in_, ident, r, w):
        return self.P.op("pe", lambda e: e.transpose(out, in_, ident), r, w)


class Rot:
    def __init__(self, P, tiles):
        self.t = tiles
        self.b = [P.buf() for _ in tiles]
        self.i = -1

    def next(self):
        self.i = (self.i + 1) % len(self.t)
        return self.t[self.i], self.b[self.i]


def pipelined(items, stages):
    n, ns = len(items), len(stages)
    ctx = [dict() for _ in items]
    for t in range(n + ns - 1):
        for s_ in reversed(range(ns)):
            e = t - s_
            if 0 <= e < n:
                stages[s_](items[e], ctx[e])


def phase0_convert(nc, P, k, pairs):
    with contextlib.ExitStack() as ph:
        CW = 2048
        NBUF = 6
        ft = [ph.enter_context(nc.sbuf_tensor("cv_f%d" % i, [128, CW], F32)) for i in range(NBUF)]
        bt = [ph.enter_context(nc.sbuf_tensor("cv_b%d" % i, [128, CW], BF16)) for i in range(NBUF)]
        fb = [P.buf() for _ in range(NBUF)]
        bb = [P.buf() for _ in range(NBUF)]
        engs = ["dve", "act", "dve", "act", "dve", "pool"]
        n = 0
        for src, dst in pairs:
            R_, C_ = src.shape
            for r0 in range(0, R_, 128):
                for c0 in range(0, C_, CW):
                    cw = min(CW, C_ - c0)
                    i = n % NBUF
                    P.dma("cvl%d" % i, ft[i][:, 0:cw], src[r0:r0 + 128, c0:c0 + cw], w=[fb[i]])
                    k.copy(engs[i], bt[i][:, 0:cw], ft[i][:, 0:cw], [fb[i]], [bb[i]])
                    P.dma("cvs%d" % i, dst[r0:r0 + 128, c0:c0 + cw], bt[i][:, 0:cw], r=[bb[i]])
                    n += 1
        P.emit()
    P.barrier()


def phase_A(nc, P, k, S, NS, T):
    NB = S // 512
    with contextlib.ExitStack() as ph:
        def sb(name, shape, dt):
            return ph.enter_context(nc.sbuf_tensor(name, shape, dt))

        def psum(name, shape, dt):
            return ph.enter_context(nc.psum_tensor(name, shape, dt))

        win_sb = sb("win_sb", [128, 8, DINP], BF16)
        b_win = P.buf()
        wv = T["win_bf"].rearrange("(c p) n -> p c n", p=128)
        splits = [0, 1792, 3584, 5376, DINP]
        for i in range(4):
            P.dma("wl%d" % i, win_sb[:, :, splits[i]:splits[i + 1]], wv[:, :, splits[i]:splits[i + 1]], w=[b_win])
        ident = sb("identA", [128, 128], BF16)
        ones = sb("onesA", [128, 128], BF16)
        gmix = sb("gmix", [128, D], F32)
        cw = sb("cwA", [128, 24, 4], F32)
        dtb = sb("dtb", [128, 8], F32)
        nA = sb("nA", [128, 8], F32)
        gq = sb("gq", [128, 3], F32)
        gkv = sb("gkv", [128, 2], F32)
        epst = sb("epsA", [128, 1], F32)
        onet = sb("oneA", [128, 1], F32)
        lnsc = sb("lnscA", [128, 1], F32)
        b_c = P.buf()
        P.dma("c0", ident[:], T["ident_bf"], w=[b_c])
        P.dma("c0", gmix[:], T["gmix_bc"], w=[b_c])
        P.dma("c0", cw[:], T["cw_qkv"], w=[b_c])
        P.dma("c0", dtb[:], T["dtb_bc"], w=[b_c])
        P.dma("c0", nA[:], T["alog_bc"], w=[b_c])
        P.dma("c0", gq[:], T["gq_p"], w=[b_c])
        P.dma("c0", gkv[:], T["gkv_p"], w=[b_c])
        k.memset("dve", ones[:], 1.0, [b_c])
        k.memset("dve", epst[:], EPS, [b_c])
        k.memset("dve", onet[:], 1.0, [b_c])
        k.memset("dve", lnsc[:], float(np.log(128.0 ** -0.5)), [b_c])
        k.act(nA[:], nA[:], AF.Exp, [b_c], [b_c])
        k.ts("dve", nA[:], nA[:], -1.0, None, ALU.mult, None, [b_c], [b_c])

        halo = sb("haloA", [128, 24, 3], F32)
        b_halo = [P.buf() for _ in range(24)]
        xt = Rot(P, [sb("xtA%d" % i, [128, D], F32) for i in range(2)])
        junk = sb("junkA", [128, D], BF16)
        b_junk = P.buf()
        hb = Rot(P, [sb("hbA%d" % i, [128, D], BF16) for i in range(2)])
        ss = Rot(P, [sb("ssA%d" % i, [128, 1], F32) for i in range(2)])
        hT = Rot(P, [sb("hTA%d" % i, [128, 8, 512], BF16) for i in range(2)])
        ps_t = Rot(P, [psum("pstA%d" % i, [128, 8, 128], BF16) for i in range(1)])
        ps_z = Rot(P, [psum("pszA%d" % i, [128, 512], F32) for i in range(4)])
        ps_s = Rot(P, [psum("pssA%d" % i, [128, 512], F32) for i in range(2)])
        ps_ab = Rot(P, [psum("psabA", [128, 16], F32)])
        abx = Rot(P, [sb("abxA%d" % i, [128, 8], F32) for i in range(2)])
        abt = Rot(P, [sb("abtA%d" % i, [128, 8], F32) for i in range(2)])
        abu = Rot(P, [sb("abuA%d" % i, [128, 8], F32) for i in range(2)])
        gbt = Rot(P, [sb("gbtA%d" % i, [128, 24], F32) for i in range(2)])
        zc = Rot(P, [sb("zcA%d" % i, [128, 515], F32) for i in range(3)])
        yc = Rot(P, [sb("ycA%d" % i, [128, 512], F32) for i in range(2)])
        ys = Rot(P, [sb("ysA%d" % i, [128, 512], F32) for i in range(5)])
        sq = Rot(P, [sb("sqA%d" % i, [128, 512], BF16) for i in range(2)])
        rr = Rot(P, [sb("rrA%d" % i, [128, 512], F32) for i in range(2)])
        st = Rot(P, [sb("stA%d" % i, [128, 512], BF16) for i in range(4)])
        raw = sb("rawA", [128, 3, 512], F32)
        b_raw = P.buf()
        cst = Rot(P, [sb("cosA%d" % i, [64, 512], F32) for i in range(2)])
        snt = Rot(P, [sb("sinA%d" % i, [64, 512], F32) for i in range(2)])
        t1 = sb("t1A", [64, 512], F32)
        t2 = sb("t2A", [64, 512], F32)
        b_t1, b_t2 = P.buf(), P.buf()
        nst = [0]

        def store(dst, tile, btile):
            P.dma("stA%d" % (nst[0] % 4), dst, tile, r=[btile], q="sp")
            nst[0] += 1

        def mm_chunk(pst, bpst, col0, ncols, hTt, bhT):
            for c in range(8):
                k.mm(pst[0:ncols, :], win_sb[:, c, col0:col0 + ncols], hTt[:, c, :], c == 0, c == 7,
                     [b_win, bhT], [bpst])

        def prologue(s, tb):
            t0 = tb * 512
            hTt, bhT = hT.next()
            ct, bct = cst.next()
            sn, bsn = snt.next()
            P.dma("cs0", ct[:], T["cosT"][:, t0:t0 + 512], w=[bct])
            P.dma("cs1", sn[:], T["sinT"][:, t0:t0 + 512], w=[bsn])
            for tt in range(4):
                xtt, bx = xt.next()
                P.dma("xA%d" % (xt.i), xtt[:], T["x"][s, t0 + tt * 128:t0 + (tt + 1) * 128, :], w=[bx])
                sst, bss = ss.next()
                k.act(junk[:], xtt[:], AF.Square, [bx], [b_junk, bss], accum=sst[:])
                k.act(sst[:], sst[:], AF.Ln, [bss, b_c], [bss], bias=epst[:, 0:1], scale=1.0 / D)
                k.act(sst[:], sst[:], AF.Exp, [bss], [bss], scale=-0.5)
                hbt, bhb = hb.next()
                k.stt("dve", hbt[:], xtt[:], sst[:, 0:1], gmix[:], ALU.mult, ALU.mult, [bx, bss, b_c], [bhb])
                pt, bpt = ps_t.next()
                for c in range(8):
                    k.tr(pt[:, c, :], hbt[:, c * 128:(c + 1) * 128], ident[:], [bhb, b_c], [bpt])
                k.copy("dve" if tt % 2 == 0 else "act", hTt[:, :, tt * 128:(tt + 1) * 128], pt[:], [bpt], [bhT])
                pab, bpab = ps_ab.next()
                for c in range(8):
                    k.mm(pab[:], hTt[:, c, tt * 128:(tt + 1) * 128], win_sb[:, c, OAB:OAB + 16], c == 0, c == 7,
                         [b_win, bhT], [bpab])
                ax, bax = abx.next()
                at, bat = abt.next()
                au, bau = abu.next()
                gt, bgt = gbt.next()
                k.tt("dve", ax[:], pab[:, 0:8], dtb[:], ALU.add, [bpab, b_c], [bax])
                k.act(at[:], ax[:], AF.Abs, [bax], [bat])
                k.act(at[:], at[:], AF.Exp, [bat], [bat], scale=-1.0)
                k.act(at[:], at[:], AF.Ln, [bat, b_c], [bat], bias=onet[:, 0:1])
                k.ts("dve", au[:], ax[:], 0.0, None, ALU.max, None, [bax], [bau])
                k.tt("dve", au[:], au[:], at[:], ALU.add, [bau, bat], [bau])
                k.tt("dve", gt[:, 0:8], au[:], nA[:], ALU.mult, [bau, b_c], [bgt])
                k.act(gt[:, 8:16], pab[:, 8:16], AF.Sigmoid, [bpab], [bgt])
                k.act(gt[:, 16:24], gt[:, 8:16], AF.Ln, [bgt], [bgt])
                P.dma("gbtA%d" % gbt.i, T["gbt_s"][s, t0 + tt * 128:t0 + (tt + 1) * 128, :], gt[:], r=[bgt])
            return (hTt, bhT, ct, bct, sn, bsn)

        blocks = [(s, tb) for s in range(NS) for tb in range(NB)]
        pro = {0: prologue(*blocks[0])}
        for bi, (s, tb) in enumerate(blocks):
            t0 = tb * 512
            if bi + 1 < len(blocks):
                pro[bi + 1] = prologue(*blocks[bi + 1])
            hTt, bhT, ct, bct, sn, bsn = pro.pop(bi)

            def q0(e, cx):
                cx["pz"] = ps_z.next()
                mm_chunk(cx["pz"][0], cx["pz"][1], OQ + e * 128, 128, hTt, bhT)

            def q1(e, cx):
                pz, bpz = cx["pz"]
                z, bz = zc.next()
                if tb == 0:
                    k.memset("pool", z[:, 0:3], 0.0, [bz])
                else:
                    k.copy("pool", z[:, 0:3], halo[:, e, :], [b_halo[e]], [bz])
                k.copy("act", z[:, 3:515], pz[:], [bpz], [bz])
                cx["z"] = (z, bz)

            def q2(e, cx):
                z, bz = cx["z"]
                y, by = yc.next()
                k.ts("dve", y[:], z[:, 0:512], cw[:, e, 0:1], None, ALU.mult, None, [bz, b_c], [by])
                for j in range(1, 4):
                    k.stt("dve", y[:], z[:, j:j + 512], cw[:, e, j:j + 1], y[:], ALU.mult, ALU.add, [bz, by, b_c], [by])
                k.copy("pool", halo[:, e, :], z[:, 512:515], [bz], [b_halo[e]])
                cx["y"] = (y, by)

            def q3(e, cx):
                y, by = cx["y"]
                if e >= 16:
                    so, bso = st.next()
                    k.act(so[:], y[:], AF.Silu, [by], [bso])
                    store(T["vT_s"][s, e % 8, :, t0:t0 + 512], so[:], bso)
                else:
                    yy, byy = ys.next()
                    k.act(yy[:], y[:], AF.Silu, [by], [byy])
                    sqt, bsq = sq.next()
                    k.tt("pool", sqt[:], yy[:], yy[:], ALU.mult, [byy], [bsq])
                    cx["yy"] = (yy, byy); cx["sq"] = (sqt, bsq)

            def q4(e, cx):
                if e >= 16:
                    return
                sqt, bsq = cx["sq"]
                pss, bps = ps_s.next()
                k.mm(pss[:], ones[:], sqt[:], True, True, [bsq, b_c], [bps])
                cx["pss"] = (pss, bps)

            def q5(e, cx):
                if e >= 16:
                    return
                pss, bps = cx["pss"]
                rt, brt = rr.next()
                k.act(rt[:], pss[:], AF.Ln, [bps, b_c], [brt], bias=epst[:, 0:1])
                if e < 8:
                    k.act(rt[:], rt[:], AF.Exp, [brt, b_c], [brt], scale=-0.5, bias=lnsc[:, 0:1])
                else:
                    k.act(rt[:], rt[:], AF.Exp, [brt], [brt], scale=-0.5)
                cx["rt"] = (rt, brt)

            def q6(e, cx):
                if e >= 16:
                    return
                yy, byy = cx["yy"]; rt, brt = cx["rt"]
                so, bso = st.next()
                k.tt("dve", so[:], yy[:], rt[:], ALU.mult, [byy, brt], [bso])
                dst = T["qT_s"] if e < 8 else T["kT_s"]
                store(dst[s, e % 8, :, t0:t0 + 512], so[:], bso)

            pipelined(list(range(24)), [q0, q1, q2, q3, q4, q5, q6])

            def g0(e, cx):
                cx["pz"] = ps_z.next()
                mm_chunk(cx["pz"][0], cx["pz"][1], OG + e * 128, 128, hTt, bhT)

            def g1(e, cx):
                pz, bpz = cx["pz"]
                so, bso = st.next()
                k.act(so[:], pz[:], AF.Silu if e < 8 else AF.Sigmoid, [bpz], [bso])
                if e < 8:
                    dst = T["gateT_s"][s, e * 128:(e + 1) * 128, t0:t0 + 512]
                elif e < 16:
                    dst = T["gbaT_s"][s, (e - 8) * 128:(e - 7) * 128, t0:t0 + 512]
                else:
                    dst = T["gbbT_s"][s, (e - 16) * 128:(e - 15) * 128, t0:t0 + 512]
                store(dst, so[:], bso)

            pipelined(list(range(24)), [g0, g1])

            for (col, nch, gg, dstn) in ((OCQ, 3, gq, "cqT_s"), (OCKV, 2, gkv, "ckvT_s")):
                pss, bps = ps_s.next()
                for j in range(nch):
                    pz, bpz = ps_z.next()
                    mm_chunk(pz, bpz, col + j * 128, 128, hTt, bhT)
                    k.copy("act", raw[:, j, :], pz[:], [bpz], [b_raw])
                    sqt, bsq = sq.next()
                    k.tt("pool", sqt[:], raw[:, j, :], raw[:, j, :], ALU.mult, [b_raw], [bsq])
                    k.mm(pss[:], ones[:], sqt[:], j == 0, j == nch - 1, [bsq, b_c], [bps])
                rt, brt = rr.next()
                k.act(rt[:], pss[:], AF.Ln, [bps, b_c], [brt], bias=epst[:, 0:1], scale=1.0 / (128 * nch))
                k.act(rt[:], rt[:], AF.Exp, [brt], [brt], scale=-0.5)
                for j in range(nch):
                    so, bso = st.next()
                    k.stt("dve", so[:], raw[:, j, :], gg[:, j:j + 1], rt[:], ALU.mult, ALU.mult, [b_raw, brt, b_c], [bso])
                    store(T[dstn][s, j * 128:(j + 1) * 128, t0:t0 + 512], so[:], bso)
            pa, bpa = ps_z.next()
            mm_chunk(pa, bpa, OKPE, 64, hTt, bhT)
            pb, bpb = ps_z.next()
            mm_chunk(pb, bpb, OKPS, 64, hTt, bhT)
            k.tt("dve", t1[:], pa[0:64, :], ct[:], ALU.mult, [bpa, bct], [b_t1])
            k.tt("dve", t2[:], pb[0:64, :], sn[:], ALU.mult, [bpb, bsn], [b_t2])
            so, bso = st.next()
            k.tt("pool", so[0:64, :], t1[:], t2[:], ALU.add, [b_t1, b_t2], [bso])
            store(T["kpeT_s"][s, :, t0:t0 + 512], so[0:64, :], bso)
        P.emit()
    P.barrier()


class StopBuild(Exception):
    pass


KSTOP = [0]


def kstop(n):
    if KSTOP[0] == n:
        raise StopBuild()


def phase_G(nc, P, k, S, NS, T):
    NB = S // 512
    with contextlib.ExitStack() as ph:
        def sb(name, shape, dt):
            return ph.enter_context(nc.sbuf_tensor(name, shape, dt))

        def psum(name, shape, dt):
            return ph.enter_context(nc.psum_tensor(name, shape, dt))

        ident = sb("identG", [128, 128], BF16)
        identf = sb("identfG", [128, 128], F32)
        ones = sb("onesG", [128, 128], BF16)
        onesf = sb("onesfG", [128, 128], F32)
        utri = sb("utriG", [128, 128], F32)
        mS = sb("mSG", [128, 128], F32)
        mST = sb("mSTG", [128, 128], F32)
        mIT = sb("mITG", [128, 128], F32)
        ggdn = sb("ggdnG", [128, 1], F32)
        epst = sb("epsG", [128, 1], F32)
        b_c = P.buf()
        for dst, src in ((ident, "ident_bf"), (identf, "ident_f"), (utri, "utri"), (mS, "m_strict"),
                         (mST, "m_strictT"), (mIT, "m_inclT"), (ggdn, "ggdn_p")):
            P.dma("c0", dst[:], T[src], w=[b_c])
        k.memset("dve", ones[:], 1.0, [b_c])
        k.memset("dve", onesf[:], 1.0, [b_c])
        k.memset("dve", epst[:], EPS, [b_c])

        Sf = sb("SfG", [128, 8, 128], F32)
        Sb = sb("SbG", [128, 8, 128], BF16)
        kblk = Rot(P, [sb("kblkG%d" % i, [128, 8, 256], BF16) for i in range(2)])
        qblk = Rot(P, [sb("qblkG%d" % i, [128, 8, 256], BF16) for i in range(2)])
        vblk = Rot(P, [sb("vblkG%d" % i, [128, 8, 256], BF16) for i in range(2)])
        gblk = Rot(P, [sb("gblkG%d" % i, [128, 8, 256], BF16) for i in range(2)])
        oblk = Rot(P, [sb("oblkG%d" % i, [128, 8, 256], BF16) for i in range(2)])
        gbt = Rot(P, [sb("gbtG%d" % i, [128, 24], F32) for i in range(2)])
        dg = Rot(P, [sb("dgG%d" % i, [128, 4, 128], F32) for i in range(2)])
        psG = psum("psGG", [128, 4, 128], F32)
        b_psG = P.buf()
        pT = Rot(P, [psum("psTG", [128, 8, 128], BF16)])
        pwP = Rot(P, [psum("pswPG%d" % i, [128, 4, 128], F32) for i in range(2)])
        pwD = Rot(P, [psum("pswDG%d" % i, [128, 4, 128], F32) for i in range(2)])
        pwA = pwD
        pwR = Rot(P, [psum("pswRG%d" % i, [128, 4, 128], F32) for i in range(2)])
        identf4 = sb("identf4G", [128, 4, 128], F32)
        for j in range(4):
            k.copy("pool", identf4[:, j, :], identf[:], [b_c], [b_c])

        def rot(name, n, dt):
            return Rot(P, [sb("%sG%d" % (name, i), [128, 4, 128], dt) for i in range(n)])

        sc3 = Rot(P, [sb("sc3G%d" % i, [128, 64], F32) for i in range(3)])
        kbg = rot("kbg", 6, BF16); kdec = rot("kdec", 6, BF16); vb = rot("vb", 6, BF16)
        aT = rot("aT", 6, BF16); qd = rot("qd", 6, BF16)
        tmp = rot("tmp", 3, F32); Et = rot("E", 3, F32); Eg = rot("Eg", 2, F32)
        def set4(name, n_):
            t = [[[sb("%sG%d_%d_%d" % (name, p_, h, i), [128, 4, 128], BF16) for i in range(n_)] for h in range(2)] for p_ in range(2)]
            b = [[[P.buf() for i in range(n_)] for h in range(2)] for p_ in range(2)]
            return t, b
        Pk, bPk = set4("Pk", 2); Qk, bQk = set4("Qk", 2); Ak, bAk = set4("Ak", 2); Bk, bBk = set4("Bk", 2)
        NTb = [[sb("NTbG%d_%d" % (p_, h), [128, 4, 128], BF16) for h in range(2)] for p_ in range(2)]
        bNT = [[P.buf() for h in range(2)] for p_ in range(2)]
        Xk = [[sb("XkG%d_%d" % (p_, h), [128, 4, 128], BF16) for h in range(2)] for p_ in range(2)]
        bXk = [[P.buf() for h in range(2)] for p_ in range(2)]
        maskd = sb("maskdG", [128, 4, 128], F32); mX1 = sb("mX1G", [128, 4, 128], F32)
        mX2 = sb("mX2G", [128, 4, 128], F32); mX3 = sb("mX3G", [128, 4, 128], F32)
        for dst_, src_ in ((maskd, "maskd4"), (mX1, "mX1_4"), (mX2, "mX2_4"), (mX3, "mX3_4")):
            P.dma("c0", dst_[:], T[src_], w=[b_c])
        TT = rot("TT", 4, BF16); uf = rot("uf", 2, F32); nwT = rot("nwT", 2, BF16)
        vn = rot("vn", 2, BF16); of = rot("of", 2, F32); sq = rot("sq", 2, BF16)
        rs = rot("rs", 2, F32); on = rot("on", 2, F32)
        b_Sg = [P.buf() for _ in range(2)]
        b_Sbg = [P.buf() for _ in range(2)]
        G, GBc, NG, NBETA, BEG, EDEC, EGL = 0, 16, 8, 24, 32, 40, 48

        tiles = [(s, n) for s in range(NS) for n in range(S // 128)]
        blk = {}
        ctx = {}

        def load_block(s, tb):
            t0 = tb * 256
            kb_, bkb = kblk.next(); qb_, bqb = qblk.next(); vb_, bvb = vblk.next()
            gb_, bgb = gblk.next(); ob_, bob = oblk.next()
            P.dma("gk%d" % kblk.i, kb_[:], T["kT_s"][s].rearrange("h d t -> d h t")[:, :, t0:t0 + 256], w=[bkb])
            P.dma("gq%d" % qblk.i, qb_[:], T["qT_s"][s].rearrange("h d t -> d h t")[:, :, t0:t0 + 256], w=[bqb])
            P.dma("gv%d" % vblk.i, vb_[:], T["vT_s"][s].rearrange("h d t -> d h t")[:, :, t0:t0 + 256], w=[bvb])
            P.dma("gg%d" % gblk.i, gb_[:], T["gateT_s"][s].rearrange("(h d) t -> d h t", d=128)[:, :, t0:t0 + 256], w=[bgb])
            blk[(s, tb)] = dict(k=(kb_, bkb), q=(qb_, bqb), v=(vb_, bvb), g=(gb_, bgb), o=(ob_, bob))

        def stream_P(ti):
            s, n = tiles[ti]
            tb, tt = n // 2, n % 2
            if tt == 0:
                load_block(s, tb)
            B = blk[(s, tb)]
            kb_, bkb = B["k"]; qb_, bqb = B["q"]; vb_, bvb = B["v"]
            c0, c1 = tt * 128, (tt + 1) * 128
            par = ti % 2
            cx = ctx[ti] = dict(par=par, B=B, c0=c0, c1=c1, s=s, n=n)
            gt, bgt = gbt.next()
            P.dma("gt%d" % gbt.i, gt[:], T["gbt_s"][s, n * 128:(n + 1) * 128, :], w=[bgt])
            sct, bsc = sc3.next()
            cx["sc"] = (sct, bsc)
            pg, bpg = pwP.next()
            k.mm(pg[:, 0, 0:8], utri[:], gt[:, 0:8], True, True, [b_c, bgt], [bpg])
            k.mm(pg[:, 1, 0:8], onesf[:], gt[:, 0:8], True, True, [b_c, bgt], [bpg])
            k.copy("dve", sct[:, 0:8], pg[:, 0, 0:8], [bpg], [bsc])
            k.ts("dve", sct[:, 24:32], gt[:, 8:16], -1.0, None, ALU.mult, None, [bgt], [bsc])
            k.act(sct[:, 56:64], pg[:, 0, 0:8], AF.Exp, [bpg], [bsc])
            k.tt("dve", sct[:, 32:40], gt[:, 8:16], sct[:, 56:64], ALU.mult, [bgt, bsc], [bsc])
            k.tt("dve", sct[:, 40:48], pg[:, 1, 0:8], sct[:, 0:8], ALU.subtract, [bpg, bsc], [bsc])
            k.act(sct[:, 40:48], sct[:, 40:48], AF.Exp, [bsc], [bsc])
            k.act(sct[:, 48:56], pg[:, 1, 0:8], AF.Exp, [bpg], [bsc])
            yield
            for hh in range(2):
                H0 = hh * 4
                dgt, bdg = dg.next()
                for j in range(4):
                    h = H0 + j
                    k.ts("dve", dgt[:, j, :], identf[:], sct[:, G + h:G + h + 1], None, ALU.mult, None, [b_c, bsc], [bdg])
                k.mm(psG[:], onesf[:], dgt[:], True, True, [b_c, bdg], [b_psG])
                p1, bp1 = pT.next()
                for j in range(4):
                    k.tr(p1[:, j, :], kb_[:, H0 + j, c0:c1], ident[:], [bkb, b_c], [bp1])
                    k.tr(p1[:, 4 + j, :], vb_[:, H0 + j, c0:c1], ident[:], [bvb, b_c], [bp1])
                kbg_, bkbg = kbg.next(); kdec_, bkdec = kdec.next(); vb2, bvb2 = vb.next()
                for j in range(4):
                    h = H0 + j
                    k.ts("dve", kbg_[:, j, :], p1[:, j, :], sct[:, BEG + h:BEG + h + 1], None, ALU.mult, None, [bp1, bsc], [bkbg])
                    k.ts("dve", kdec_[:, j, :], p1[:, j, :], sct[:, EDEC + h:EDEC + h + 1], None, ALU.mult, None, [bp1, bsc], [bkdec])
                    k.ts("dve", vb2[:, j, :], p1[:, 4 + j, :], gt[:, 8 + h:9 + h], None, ALU.mult, None, [bp1, bgt], [bvb2])
                yield
                pKK, bKK = pwP.next()
                for j in range(4):
                    k.mm(pKK[:, j, :], kb_[:, H0 + j, c0:c1], kb_[:, H0 + j, c0:c1], True, True, [bkb], [bKK])
                pQK, bQK = pwP.next()
                for j in range(4):
                    k.mm(pQK[:, j, :], kb_[:, H0 + j, c0:c1], qb_[:, H0 + j, c0:c1], True, True, [bkb, bqb], [bQK])
                t_, bt_ = tmp.next(); e_, be_ = Et.next()
                for j in range(4):
                    h = H0 + j
                    k.stt("dve", t_[:, j, :], psG[:, j, :], sct[:, G + h:G + h + 1], mS[:], ALU.subtract, ALU.subtract, [b_psG, bsc, b_c], [bt_])
                k.act(e_[:], t_[:], AF.Exp, [bt_], [be_], scale=-1.0)
                for j in range(4):
                    h = H0 + j
                    k.stt("dve", NTb[par][hh][:, j, :], pKK[:, j, :], sct[:, NBETA + h:NBETA + h + 1], e_[:, j, :], ALU.mult, ALU.mult,
                          [bKK, bsc, be_], [bNT[par][hh]])
                yield
                t_, bt_ = tmp.next(); e_, be_ = Et.next()
                for j in range(4):
                    h = H0 + j
                    k.stt("dve", t_[:, j, :], psG[:, j, :], sct[:, G + h:G + h + 1], mIT[:], ALU.subtract, ALU.add, [b_psG, bsc, b_c], [bt_])
                k.act(e_[:], t_[:], AF.Exp, [bt_], [be_])
                aT_, baT = aT.next()
                k.tt("dve", aT_[:], pQK[:], e_[:], ALU.mult, [bQK, be_], [baT])
                eg_, beg = Eg.next()
                k.act(eg_[:], psG[:], AF.Exp, [b_psG], [beg])
                qd_, bqd = qd.next()
                k.tt("pool", qd_[:], qb_[:, H0:H0 + 4, c0:c1], eg_[:], ALU.mult, [bqb, beg], [bqd])
                pN, bpN = pT.next()
                for j in range(4):
                    k.tr(pN[:, j, :], NTb[par][hh][:, j, :], ident[:], [bNT[par][hh], b_c], [bpN])
                k.tt("dve", Pk[par][hh][0][:], pN[:, 0:4, :], maskd[:], ALU.mult, [bpN, b_c], [bPk[par][hh][0]])
                k.tt("pool", Qk[par][hh][0][:], NTb[par][hh][:], maskd[:], ALU.mult, [bNT[par][hh], b_c], [bQk[par][hh][0]])
                k.tt("pool", Ak[par][hh][0][:], Pk[par][hh][0][:], identf4[:], ALU.add, [bPk[par][hh][0], b_c], [bAk[par][hh][0]])
                k.tt("pool", Bk[par][hh][0][:], Qk[par][hh][0][:], identf4[:], ALU.add, [bQk[par][hh][0], b_c], [bBk[par][hh][0]])
                cx[hh] = dict(kbg=(kbg_, bkbg), kdec=(kdec_, bkdec), vb=(vb2, bvb2), aT=(aT_, baT), qd=(qd_, bqd))
                yield

        def stream_D(ti):
            cx = ctx[ti]
            par = cx["par"]

            def grp(hh):
                return (Pk[par][hh], Qk[par][hh], Ak[par][hh], Bk[par][hh], bPk[par][hh], bQk[par][hh], bAk[par][hh], bBk[par][hh])

            def mm4(ps_, bps_, lhs, blhs, rhs, brhs):
                for j in range(4):
                    k.mm(ps_[:, j, :], lhs[:, j, :], rhs[:, j, :], True, True, [blhs, brhs], [bps_])

            for hh in range(2):
                Pc, Qc, Ac, Bc, bPc, bQc, bAc, bBc = grp(hh)
                p_, bp_ = pwD.next(); mm4(p_, bp_, Qc[0], bQc[0], Pc[0], bPc[0])
                q_, bq_ = pwD.next(); mm4(q_, bq_, Pc[0], bPc[0], Qc[0], bQc[0])
                k.copy("act", Pc[1][:], p_[:], [bp_], [bPc[1]])
                k.copy("dve", Qc[1][:], q_[:], [bq_], [bQc[1]])
                yield
            pa, qa, aa = 1, 1, 0
            for lvl in (1, 2, 3):
                for hh in range(2):
                    Pc, Qc, Ac, Bc, bPc, bQc, bAc, bBc = grp(hh)
                    a_, ba_ = pwD.next(); mm4(a_, ba_, Qc[qa], bQc[qa], Ac[aa], bAc[aa])
                    b_, bb_ = pwD.next(); mm4(b_, bb_, Ac[aa], bAc[aa], Qc[qa], bQc[qa])
                    k.tt("dve", Ac[1 - aa][:], a_[:], Ac[aa][:], ALU.add, [ba_, bAc[aa]], [bAc[1 - aa]])
                    k.tt("dve", Bc[1 - aa][:], b_[:], Bc[aa][:], ALU.add, [bb_, bBc[aa]], [bBc[1 - aa]])
                    if lvl < 3:
                        q_, bq_ = pwD.next(); mm4(q_, bq_, Pc[pa], bPc[pa], Qc[qa], bQc[qa])
                        if lvl < 2:
                            p_, bp_ = pwD.next(); mm4(p_, bp_, Qc[qa], bQc[qa], Pc[pa], bPc[pa])
                            k.copy("act", Pc[1 - pa][:], p_[:], [bp_], [bPc[1 - pa]])
                        k.copy("act", Qc[1 - qa][:], q_[:], [bq_], [bQc[1 - qa]])
                    yield
                aa = 1 - aa
                pa, qa = 1 - pa, 1 - qa
            for lvl in (1, 2, 3):
                mX = (mX1, mX2, mX3)[lvl - 1]
                for hh in range(2):
                    Pc, Qc, Ac, Bc, bPc, bQc, bAc, bBc = grp(hh)
                    x_, bx_ = pwD.next(); mm4(x_, bx_, NTb[par][hh], bNT[par][hh], Ac[aa], bAc[aa])
                    k.tt("dve", Xk[par][hh][:], x_[:], mX[:], ALU.mult, [bx_, b_c], [bXk[par][hh]])
                    yield
                for hh in range(2):
                    Pc, Qc, Ac, Bc, bPc, bQc, bAc, bBc = grp(hh)
                    u_, bu_ = pwD.next(); mm4(u_, bu_, Bc[aa], bBc[aa], Xk[par][hh], bXk[par][hh])
                    if lvl < 3:
                        k.tt("dve", Ac[1 - aa][:], u_[:], Ac[aa][:], ALU.add, [bu_, bAc[aa]], [bAc[1 - aa]])
                        v_, bv_ = pwD.next(); mm4(v_, bv_, Xk[par][hh], bXk[par][hh], Bc[aa], bBc[aa])
                        k.tt("dve", Bc[1 - aa][:], v_[:], Bc[aa][:], ALU.add, [bv_, bBc[aa]], [bBc[1 - aa]])
                    else:
                        TT_, bTT = TT.next()
                        k.tt("dve", TT_[:], u_[:], Ac[aa][:], ALU.add, [bu_, bAc[aa]], [bTT])
                        cx[hh]["TT"] = (TT_, bTT)
                    yield
                aa = 1 - aa

        def stream_R(ti):
            cx = ctx[ti]
            sct, bsc = cx["sc"]
            B = cx["B"]; c0, c1 = cx["c0"], cx["c1"]
            gb_, bgb = B["g"]; ob_, bob = B["o"]
            for hh in range(2):
                H0 = hh * 4
                TT_, bTT = cx[hh]["TT"]
                vb2, bvb2 = cx[hh]["vb"]; kbg_, bkbg = cx[hh]["kbg"]
                aT_, baT = cx[hh]["aT"]; qd_, bqd = cx[hh]["qd"]; kdec_, bkdec = cx[hh]["kdec"]
                pu, bpu = pwR.next()
                for j in range(4):
                    k.mm(pu[:, j, :], TT_[:, j, :], vb2[:, j, :], True, True, [bTT, bvb2], [bpu])
                uf_, buf_ = uf.next()
                k.copy("act", uf_[:], pu[:], [bpu], [buf_])
                pW, bpW = pwR.next()
                for j in range(4):
                    k.mm(pW[:, j, :], kbg_[:, j, :], TT_[:, j, :], True, True, [bkbg, bTT], [bpW])
                nw_, bnw = nwT.next()
                k.ts("dve", nw_[:], pW[:], -1.0, None, ALU.mult, None, [bpW], [bnw])
                yield
                pws, bpws = pwR.next()
                for j in range(4):
                    k.mm(pws[:, j, :], nw_[:, j, :], Sb[:, H0 + j, :], True, True, [bnw, b_Sbg[hh]], [bpws])
                vn_, bvn = vn.next()
                k.tt("dve", vn_[:], pws[:], uf_[:], ALU.add, [bpws, buf_], [bvn])
                po, bpo = pwR.next()
                for j in range(4):
                    k.mm(po[:, j, :], Sb[:, H0 + j, :], qd_[:, j, :], True, False, [b_Sbg[hh], bqd], [bpo])
                    k.mm(po[:, j, :], vn_[:, j, :], aT_[:, j, :], False, True, [bvn, baT], [bpo])
                of_, bof = of.next()
                k.copy("act", of_[:], po[:], [bpo], [bof])
                yield
                pS, bpS = pwR.next()
                for j in range(4):
                    k.mm(pS[:, j, :], kdec_[:, j, :], vn_[:, j, :], True, True, [bkdec, bvn], [bpS])
                for j in range(4):
                    h = H0 + j
                    k.stt("dve", Sf[:, h, :], Sf[:, h, :], sct[:, EGL + h:EGL + h + 1], pS[:, j, :], ALU.mult, ALU.add,
                          [b_Sg[hh], bsc, bpS], [b_Sg[hh]])
                k.copy("act", Sb[:, H0:H0 + 4, :], Sf[:, H0:H0 + 4, :], [b_Sg[hh]], [b_Sbg[hh]])
                sq_, bsq = sq.next()
                k.tt("pool", sq_[:], of_[:], of_[:], ALU.mult, [bof], [bsq])
                yield
                pss, bpss = pwR.next()
                k.mm(pss[:], ones[:], sq_[:], True, True, [b_c, bsq], [bpss])
                rs_, brs = rs.next()
                k.act(rs_[:], pss[:], AF.Ln, [bpss, b_c], [brs], bias=epst[:, 0:1], scale=1.0 / 128)
                k.act(rs_[:], rs_[:], AF.Exp, [brs], [brs], scale=-0.5)
                on_, bon = on.next()
                k.stt("dve", on_[:], of_[:], ggdn[:, 0:1], rs_[:], ALU.mult, ALU.mult, [bof, b_c, brs], [bon])
                k.tt("pool", ob_[:, H0:H0 + 4, c0:c1], on_[:], gb_[:, H0:H0 + 4, c0:c1], ALU.mult, [bon, bgb], [bob])
                yield
            s, n = cx["s"], cx["n"]
            if n % 2 == 1:
                t0 = (n // 2) * 256
                P.dma("go%d" % (n // 2 % 2), T["oaT_s"][s].rearrange("(h d) t -> d h t", d=128)[:, :, t0:t0 + 256], ob_[:], r=[bob])

        def drain(g):
            for _ in g:
                pass

        def interleave(main, others):
            others = [o for o in others if o is not None]
            for _ in main:
                for o in list(others):
                    try:
                        next(o)
                    except StopIteration:
                        others.remove(o)
            for o in others:
                drain(o)

        NTt = len(tiles)
        for ti in range(NTt):
            s, n = tiles[ti]
            if n == 0:
                if ti > 0:
                    drain(stream_R(ti - 1))
                for hh in range(2):
                    k.memset("dve", Sf[:, hh * 4:hh * 4 + 4, :], 0.0, [b_Sg[hh]])
                    k.memset("pool", Sb[:, hh * 4:hh * 4 + 4, :], 0.0, [b_Sbg[hh]])
                drain(stream_P(ti))
            nxtP = stream_P(ti + 1) if (ti + 1 < NTt and tiles[ti + 1][1] != 0) else None
            prvR = stream_R(ti - 1) if (ti > 0 and n != 0) else None
            interleave(stream_D(ti), [nxtP, prvR])
        drain(stream_R(NTt - 1))
        P.emit()
    P.barrier()


def phase_M(nc, P, k, S, NS, T):
    NB = S // 512
    NT = S // 128
    scale = float(192 ** -0.5)
    with contextlib.ExitStack() as ph:
        def sb(name, shape, dt):
            return ph.enter_context(nc.sbuf_tensor(name, shape, dt))

        def psum(name, shape, dt):
            return ph.enter_context(nc.psum_tensor(name, shape, dt))

        ident = sb("identM", [128, 128], BF16)
        tri = sb("triM", [128, 128], BF16)
        wuq = sb("wuqM", [128, 3, 2048], BF16)
        wuk = sb("wukM", [128, 2, 1024], BF16)
        wuv = sb("wuvM", [128, 2, 1024], BF16)
        cosT = sb("cosM", [64, S], F32)
        sinT = sb("sinM", [64, S], F32)
        b_c = P.buf()
        P.dma("c0", ident[:], T["ident_bf"], w=[b_c])
        P.dma("c0", tri[:], T["tri_bf"], w=[b_c])
        P.dma("c1", wuq[:], T["wuq_bf"].rearrange("(c p) n -> p c n", p=128), w=[b_c])
        P.dma("c2", wuk[:], T["wuk_bf"].rearrange("(c p) n -> p c n", p=128), w=[b_c])
        P.dma("c3", wuv[:], T["wuv_bf"].rearrange("(c p) n -> p c n", p=128), w=[b_c])
        P.dma("c1", cosT[:], T["cosT"], w=[b_c])
        P.dma("c2", sinT[:], T["sinT"], w=[b_c])
        cq = sb("cqM", [128, 3, S], BF16)
        ckv = sb("ckvM", [128, 2, S], BF16)
        kpe = sb("kpeM", [128, S], BF16)
        b_in = P.buf()
        qn = sb("qnM", [128, S], BF16)
        qr = sb("qrM", [128, S], BF16)
        kn = sb("knM", [128, S], BF16)
        vv = sb("vvM", [128, NT, 128], BF16)
        b_qn, b_qr, b_kn, b_vv = P.buf(), P.buf(), P.buf(), P.buf()
        b_pad = P.buf()
        k.memset("pool", kpe[64:128, :], 0.0, [b_pad])
        k.memset("pool", qr[64:128, :], 0.0, [b_pad])
        t1 = Rot(P, [sb("t1M%d" % i, [64, 512], F32) for i in range(2)])
        t2 = Rot(P, [sb("t2M%d" % i, [64, 512], F32) for i in range(2)])
        pT = Rot(P, [sb("pTM%d" % i, [128, 512], BF16) for i in range(4)])
        rec = Rot(P, [sb("recM%d" % i, [128, 512], F32) for i in range(2)])
        acc = Rot(P, [sb("accM%d" % i, [128, 512], F32) for i in range(2)])
        onesf = sb("onesfM", [128, 128], F32)
        k.memset("dve", onesf[:], 1.0, [b_c])
        obT = Rot(P, [sb("obTM%d" % i, [128, 512], BF16) for i in range(2)])
        ps_s = Rot(P, [psum("pssM%d" % i, [128, 512], F32) for i in range(4)])
        po = Rot(P, [psum("poM%d" % i, [128, 512], F32) for i in range(2)])
        prs_r = Rot(P, [psum("prsM", [128, 512], F32)])
        ppv = Rot(P, [psum("ppvM", [128, 4, 128], F32)])

        for s in range(NS):
            P.dma("mi0", cq[:], T["cqT_s"][s].rearrange("(c p) t -> p c t", p=128), w=[b_in])
            P.dma("mi1", ckv[:], T["ckvT_s"][s].rearrange("(c p) t -> p c t", p=128), w=[b_in])
            P.dma("mi2", kpe[0:64, :], T["kpeT_s"][s], w=[b_in])
            for h in range(8):
                for tb in range(NB):
                    t0 = tb * 512
                    pz, bpz = ps_s.next()
                    for r in range(3):
                        k.mm(pz[:], wuq[:, r, h * 256:h * 256 + 128], cq[:, r, t0:t0 + 512], r == 0, r == 2, [b_c, b_in], [bpz])
                    k.copy("act", qn[:, t0:t0 + 512], pz[:], [bpz], [b_qn])
                    pa, bpa = ps_s.next()
                    for r in range(3):
                        k.mm(pa[0:64, :], wuq[:, r, h * 256 + 128:h * 256 + 192], cq[:, r, t0:t0 + 512], r == 0, r == 2, [b_c, b_in], [bpa])
                    a1, ba1 = t1.next()
                    k.tt("dve", a1[:], pa[0:64, :], cosT[:, t0:t0 + 512], ALU.mult, [bpa, b_c], [ba1])
                    pb, bpb = ps_s.next()
                    for r in range(3):
                        k.mm(pb[0:64, :], wuq[:, r, h * 256 + 192:h * 256 + 256], cq[:, r, t0:t0 + 512], r == 0, r == 2, [b_c, b_in], [bpb])
                    a2, ba2 = t2.next()
                    k.tt("dve", a2[:], pb[0:64, :], sinT[:, t0:t0 + 512], ALU.mult, [bpb, b_c], [ba2])
                    k.tt("pool", qr[0:64, t0:t0 + 512], a1[:], a2[:], ALU.add, [ba1, ba2], [b_qr])
                    pk, bpk = ps_s.next()
                    for r in range(2):
                        k.mm(pk[:], wuk[:, r, h * 128:(h + 1) * 128], ckv[:, r, t0:t0 + 512], r == 0, r == 1, [b_c, b_in], [bpk])
                    k.copy("act", kn[:, t0:t0 + 512], pk[:], [bpk], [b_kn])
                    pv, bpv = ppv.next()
                    for tt in range(4):
                        for r in range(2):
                            k.mm(pv[:, tt, :], ckv[:, r, t0 + tt * 128:t0 + (tt + 1) * 128], wuv[:, r, h * 128:(h + 1) * 128],
                                 r == 0, r == 1, [b_c, b_in], [bpv])
                    k.copy("dve", vv[:, tb * 4:(tb + 1) * 4, :], pv[:], [bpv], [b_vv])
                items = [(qg, kb) for qg in range(NB) for kb in range(4 * (qg + 1))]

                def st_S(it, cx):
                    qg, kb = it
                    q0 = qg * 512
                    r_ = kb - 4 * qg
                    qlo = 128 * r_ if r_ > 0 else 0
                    pss, bps = ps_s.next()
                    k.mm(pss[:, qlo:512], kn[:, kb * 128:(kb + 1) * 128], qn[:, q0 + qlo:q0 + 512], True, False, [b_kn, b_qn], [bps])
                    k.mm(pss[:, qlo:512], kpe[:, kb * 128:(kb + 1) * 128], qr[:, q0 + qlo:q0 + 512], False, True, [b_in, b_qr, b_pad], [bps])
                    cx["pss"] = (pss, bps); cx["r"] = r_; cx["qlo"] = qlo

                def st_E(it, cx):
                    qg, kb = it
                    pss, bps = cx["pss"]; r_ = cx["r"]; qlo = cx["qlo"]
                    pt_, bpt = pT.next()
                    k.act(pt_[:, qlo:512], pss[:, qlo:512], AF.Exp, [bps], [bpt], scale=scale)
                    if r_ >= 0:
                        k.tt("pool", pt_[:, 128 * r_:128 * (r_ + 1)], pt_[:, 128 * r_:128 * (r_ + 1)], tri[:], ALU.mult, [bpt, b_c], [bpt])
                    if kb == 0:
                        st_E.acc = acc.next()
                        k.copy("dve", st_E.acc[0][:], pt_[:], [bpt], [st_E.acc[1]])
                    else:
                        a_, ba_ = st_E.acc
                        k.tt("dve", a_[:, qlo:512], a_[:, qlo:512], pt_[:, qlo:512], ALU.add, [bpt, ba_], [ba_])
                    cx["pt"] = (pt_, bpt); cx["acc"] = st_E.acc

                def st_V(it, cx, s=s, h=h):
                    qg, kb = it
                    q0 = qg * 512
                    pt_, bpt = cx["pt"]; qlo = cx["qlo"]
                    if kb == 0:
                        st_V.po = po.next()
                    po_, bpo = st_V.po
                    lastk = (kb == 4 * qg + 3)
                    k.mm(po_[:, qlo:512], vv[:, kb, :], pt_[:, qlo:512], kb == 0, lastk, [bpt, b_vv], [bpo])
                    if lastk:
                        a_, ba_ = cx["acc"]
                        prs, bprs = prs_r.next()
                        k.mm(prs[:], onesf[:], a_[:], True, True, [ba_, b_c], [bprs])
                        rc, brc = rec.next()
                        k.act(rc[:], prs[:], AF.Ln, [bprs], [brc])
                        k.act(rc[:], rc[:], AF.Exp, [brc], [brc], scale=-1.0)
                        oT, boT = obT.next()
                        k.tt("dve", oT[:], po_[:], rc[:], ALU.mult, [bpo, brc], [boT])
                        P.dma("mo%d" % obT.i, T["obT_s"][s, h * 128:(h + 1) * 128, q0:q0 + 512], oT[:], r=[boT], q="sp")

                pipelined(items, [st_S, st_E, (lambda it, cx: None), st_V])
        P.emit()
    P.barrier()


def phase_C1(nc, P, k, S, NS, T):
    NB = S // 512
    with contextlib.ExitStack() as ph:
        def sb(name, shape, dt):
            return ph.enter_context(nc.sbuf_tensor(name, shape, dt))

        def psum(name, shape, dt):
            return ph.enter_context(nc.psum_tensor(name, shape, dt))

        ident = sb("identC", [128, 128], BF16)
        wog = sb("wogC", [128, 8, D], BF16)
        wom = sb("womC", [128, 8, D], BF16)
        wout = sb("woutC", [128, 8, D], BF16)
        gffn = sb("gffnC", [128, D], F32)
        epst = sb("epsC", [128, 1], F32)
        b_c = P.buf()
        P.dma("c0", ident[:], T["ident_bf"], w=[b_c])
        P.dma("c1", wog[:], T["wog_bf"].rearrange("(c p) n -> p c n", p=128), w=[b_c])
        P.dma("c2", wom[:], T["wom_bf"].rearrange("(c p) n -> p c n", p=128), w=[b_c])
        P.dma("c3", wout[:], T["wout_bf"].rearrange("(c p) n -> p c n", p=128), w=[b_c])
        P.dma("c0", gffn[:], T["gffn_bc"], w=[b_c])
        k.memset("dve", epst[:], EPS, [b_c])
        oa = Rot(P, [sb("oaC%d" % i, [128, 8, 512], BF16) for i in range(2)])
        ob = Rot(P, [sb("obC%d" % i, [128, 8, 512], BF16) for i in range(2)])
        ga = Rot(P, [sb("gaC%d" % i, [128, 8, 512], BF16) for i in range(2)])
        gb = Rot(P, [sb("gbC%d" % i, [128, 8, 512], BF16) for i in range(2)])
        mg = Rot(P, [sb("mgC%d" % i, [128, 8, 512], BF16) for i in range(2)])
        h2T = Rot(P, [sb("h2TC%d" % i, [128, 8, 512], BF16) for i in range(2)])
        ta = Rot(P, [sb("taC%d" % i, [128, 512], F32) for i in range(2)])
        tb_ = Rot(P, [sb("tbC%d" % i, [128, 512], F32) for i in range(2)])
        xt = Rot(P, [sb("xtC%d" % i, [128, D], F32) for i in range(2)])
        x1 = Rot(P, [sb("x1C%d" % i, [128, D], F32) for i in range(2)])
        hb = Rot(P, [sb("hbC%d" % i, [128, D], BF16) for i in range(2)])
        ss = Rot(P, [sb("ssC%d" % i, [128, 1], F32) for i in range(2)])
        junk = sb("junkC", [128, D], BF16)
        b_junk = P.buf()
        ps = Rot(P, [psum("psC%d" % i, [128, 512], F32) for i in range(6)])
        ps_t = Rot(P, [psum("pstC%d" % i, [128, 8, 128], BF16) for i in range(2)])
        for s in range(NS):
            for tb in range(NB):
                t0 = tb * 512
                oa_, boa = oa.next(); ob_, bob = ob.next(); ga_, bga = ga.next(); gb_, bgb = gb.next()
                vw = lambda nm: T[nm][s].rearrange("(c p) t -> p c t", p=128)[:, :, t0:t0 + 512]
                P.dma("ca%d" % oa.i, oa_[:], vw("oaT_s"), w=[boa])
                P.dma("cb%d" % ob.i, ob_[:], vw("obT_s"), w=[bob])
                P.dma("cc%d" % ga.i, ga_[:], vw("gbaT_s"), w=[bga])
                P.dma("cd%d" % gb.i, gb_[:], vw("gbbT_s"), w=[bgb])
                mg_, bmg = mg.next()
                for dch in range(8):
                    pa, bpa = ps.next()
                    for e in range(8):
                        k.mm(pa[:], wog[:, e, dch * 128:(dch + 1) * 128], oa_[:, e, :], e == 0, e == 7, [b_c, boa], [bpa])
                    pb, bpb = ps.next()
                    for e in range(8):
                        k.mm(pb[:], wom[:, e, dch * 128:(dch + 1) * 128], ob_[:, e, :], e == 0, e == 7, [b_c, bob], [bpb])
                    a_, ba_ = ta.next(); b2, bb2 = tb_.next()
                    k.tt("dve", a_[:], pa[:], ga_[:, dch, :], ALU.mult, [bpa, bga], [ba_])
                    k.tt("dve", b2[:], pb[:], gb_[:, dch, :], ALU.mult, [bpb, bgb], [bb2])
                    k.tt("pool", mg_[:, dch, :], a_[:], b2[:], ALU.add, [ba_, bb2], [bmg])
                h2_, bh2 = h2T.next()
                for tt in range(4):
                    c0, c1 = tt * 128, (tt + 1) * 128
                    x_, bx = xt.next()
                    P.dma("cx%d" % xt.i, x_[:], T["x"][s, t0 + c0:t0 + c1, :], w=[bx])
                    x1_, bx1 = x1.next()
                    for dh in range(2):
                        po_, bpo = ps.next()
                        for c in range(8):
                            k.mm(po_[:], mg_[:, c, c0:c1], wout[:, c, dh * 512:(dh + 1) * 512], c == 0, c == 7, [bmg, b_c], [bpo])
                        k.tt("dve", x1_[:, dh * 512:(dh + 1) * 512], po_[:], x_[:, dh * 512:(dh + 1) * 512], ALU.add, [bpo, bx], [bx1])
                    P.dma("cs%d" % x1.i, T["x1_s"][s, t0 + c0:t0 + c1, :], x1_[:], r=[bx1])
                    ss_, bss = ss.next()
                    k.act(junk[:], x1_[:], AF.Square, [bx1], [b_junk, bss], accum=ss_[:])
                    k.act(ss_[:], ss_[:], AF.Ln, [bss, b_c], [bss], bias=epst[:, 0:1], scale=1.0 / D)
                    k.act(ss_[:], ss_[:], AF.Exp, [bss], [bss], scale=-0.5)
                    hb_, bhb = hb.next()
                    k.stt("dve", hb_[:], x1_[:], ss_[:, 0:1], gffn[:], ALU.mult, ALU.mult, [bx1, bss, b_c], [bhb])
                    pt, bpt = ps_t.next()
                    for c in range(8):
                        k.tr(pt[:, c, :], hb_[:, c * 128:(c + 1) * 128], ident[:], [bhb, b_c], [bpt])
                    k.copy("act", h2_[:, :, c0:c1], pt[:], [bpt], [bh2])
                P.dma("ch%d" % h2T.i, T["h2T_s"][s].rearrange("(c p) t -> p c t", p=128)[:, :, t0:t0 + 512], h2_[:], r=[bh2])
        P.emit()
    P.barrier()


def phase_C2(nc, P, k, S, NS, T):
    NB = S // 512
    with contextlib.ExitStack() as ph:
        def sb(name, shape, dt):
            return ph.enter_context(nc.sbuf_tensor(name, shape, dt))

        def psum(name, shape, dt):
            return ph.enter_context(nc.psum_tensor(name, shape, dt))

        wup = sb("wupF", [128, 8, 2 * DFF], BF16)
        wdn = sb("wdnF", [128, 22, D], BF16)
        cwf = sb("cwF", [128, 44, 3], F32)
        gfin = sb("gfinF", [128, D], F32)
        epst = sb("epsF", [128, 1], F32)
        b_c = P.buf()
        wv = T["wup_bf"].rearrange("(c p) n -> p c n", p=128)
        for i in range(4):
            P.dma("c%d" % i, wup[:, :, i * 1408:(i + 1) * 1408], wv[:, :, i * 1408:(i + 1) * 1408], w=[b_c])
        P.dma("c0", wdn[:], T["wdn_bf"].rearrange("(c p) n -> p c n", p=128), w=[b_c])
        P.dma("c0", cwf[:], T["cw_ffn"], w=[b_c])
        P.dma("c2", gfin[:], T["gfin_bc"], w=[b_c])
        k.memset("dve", epst[:], EPS, [b_c])
        halo = sb("haloF", [128, 44, 2], F32)
        b_halo = [P.buf() for _ in range(44)]
        h2T = Rot(P, [sb("h2TF", [128, 8, 512], BF16)])
        aT = Rot(P, [sb("aTF", [128, 22, 512], BF16)])
        zc = Rot(P, [sb("zcF%d" % i, [128, 514], F32) for i in range(4)])
        yc = Rot(P, [sb("ycF%d" % i, [128, 512], F32) for i in range(4)])
        sg = Rot(P, [sb("sgF%d" % i, [128, 512], F32) for i in range(2)])
        x1 = Rot(P, [sb("x1F%d" % i, [128, D], F32) for i in range(2)])
        ot = Rot(P, [sb("otF", [128, D], F32)])
        ss = Rot(P, [sb("ssF%d" % i, [128, 1], F32) for i in range(2)])
        junk = sb("junkF", [128, D], BF16)
        b_junk = P.buf()
        ps = Rot(P, [psum("psF%d" % i, [128, 512], F32) for i in range(8)])
        ncv = 0
        for s in range(NS):
            for tb in range(NB):
                t0 = tb * 512
                h2_, bh2 = h2T.next()
                P.dma("fh", h2_[:], T["h2T_s"][s].rearrange("(c p) t -> p c t", p=128)[:, :, t0:t0 + 512], w=[bh2])
                aT_, baT = aT.next()
                def f0(i, cx):
                    cx["pz"] = []
                    for e in (i, 22 + i):
                        pz, bpz = ps.next()
                        for c in range(8):
                            k.mm(pz[:], wup[:, c, e * 128:(e + 1) * 128], h2_[:, c, :], c == 0, c == 7, [b_c, bh2], [bpz])
                        cx["pz"].append((pz, bpz))

                def f1(i, cx):
                    cx["z"] = []
                    for (pz, bpz), e in zip(cx["pz"], (i, 22 + i)):
                        z, bz = zc.next()
                        if tb == 0:
                            k.memset("pool", z[:, 0:2], 0.0, [bz])
                        else:
                            k.copy("pool", z[:, 0:2], halo[:, e, :], [b_halo[e]], [bz])
                        k.copy("act", z[:, 2:514], pz[:], [bpz], [bz])
                        cx["z"].append((z, bz))

                def f2(i, cx):
                    cx["y"] = []
                    for (z, bz), e in zip(cx["z"], (i, 22 + i)):
                        y, by = yc.next()
                        k.ts("dve", y[:], z[:, 0:512], cwf[:, e, 0:1], None, ALU.mult, None, [bz, b_c], [by])
                        for j in range(1, 3):
                            k.stt("dve", y[:], z[:, j:j + 512], cwf[:, e, j:j + 1], y[:], ALU.mult, ALU.add, [bz, by, b_c], [by])
                        k.copy("pool", halo[:, e, :], z[:, 512:514], [bz], [b_halo[e]])
                        cx["y"].append((y, by))

                def f3(i, cx):
                    ys = cx["y"]
                    g_, bg_ = sg.next()
                    k.act(g_[:], ys[0][0][:], AF.Silu, [ys[0][1]], [bg_])
                    k.tt("pool", aT_[:, i, :], g_[:], ys[1][0][:], ALU.mult, [bg_, ys[1][1]], [baT])

                pipelined(list(range(22)), [f0, f1, f2, f3])
                for tt in range(4):
                    c0, c1 = tt * 128, (tt + 1) * 128
                    x1_, bx1 = x1.next()
                    P.dma("fx%d" % x1.i, x1_[:], T["x1_s"][s, t0 + c0:t0 + c1, :], w=[bx1])
                    for dh in range(2):
                        po_, bpo = ps.next()
                        for i in range(22):
                            k.mm(po_[:], aT_[:, i, c0:c1], wdn[:, i, dh * 512:(dh + 1) * 512], i == 0, i == 21, [baT, b_c], [bpo])
                        k.tt("dve", x1_[:, dh * 512:(dh + 1) * 512], po_[:], x1_[:, dh * 512:(dh + 1) * 512], ALU.add, [bpo, bx1], [bx1])
                    ss_, bss = ss.next()
                    k.act(junk[:], x1_[:], AF.Square, [bx1], [b_junk, bss], accum=ss_[:])
                    k.act(ss_[:], ss_[:], AF.Ln, [bss, b_c], [bss], bias=epst[:, 0:1], scale=1.0 / D)
                    k.act(ss_[:], ss_[:], AF.Exp, [bss], [bss], scale=-0.5)
                    o_, bo = ot.next()
                    k.stt("dve", o_[:], x1_[:], ss_[:, 0:1], gfin[:], ALU.mult, ALU.mult, [bx1, bss, b_c], [bo])
                    P.dma("fo", T["out"][s, t0 + c0:t0 + c1, :], o_[:], r=[bo])
        P.emit()
    P.barrier()


def build(S, NS, upto="all", dbg=False):
    nc = bass.Bass("TRN2", target_bir_lowering=False)
    T = {}

    def din(name, shape, dt=F32):
        T[name] = nc.dram_tensor(name, list(shape), dt, kind="ExternalInput").ap()

    def dscr(name, shape, dt):
        kind = "ExternalOutput" if dbg else "Internal"
        T[name] = nc.dram_tensor(name, list(shape), dt, kind=kind).ap()

    din("x", [NS, S, D])
    din("w_in_p", [D, DINP]); din("w_uq_p", [384, 2048]); din("w_uk", [256, 1024]); din("w_uv", [256, 1024])
    din("w_og", [D, D]); din("w_om", [D, D]); din("w_out", [D, D]); din("w_up", [D, 2 * DFF]); din("w_dn", [DFF, D])
    din("ident_bf", [128, 128], BF16); din("ident_f", [128, 128])
    din("gmix_bc", [128, D]); din("gffn_bc", [128, D]); din("gfin_bc", [128, D])
    din("cw_qkv", [128, 24, 4]); din("cw_ffn", [128, 44, 3])
    din("dtb_bc", [128, 8]); din("alog_bc", [128, 8])
    din("gq_p", [128, 3]); din("gkv_p", [128, 2]); din("ggdn_p", [128, 1])
    din("cosT", [64, S]); din("sinT", [64, S])
    din("m_strict", [128, 128]); din("m_strictT", [128, 128]); din("m_inclT", [128, 128]); din("utri", [128, 128])
    din("tri_bf", [128, 128], BF16)
    din("maskd4", [128, 4, 128]); din("mX1_4", [128, 4, 128]); din("mX2_4", [128, 4, 128]); din("mX3_4", [128, 4, 128])
    for nm, shp in (("win_bf", [D, DINP]), ("wuq_bf", [384, 2048]), ("wuk_bf", [256, 1024]), ("wuv_bf", [256, 1024]),
                    ("wog_bf", [D, D]), ("wom_bf", [D, D]), ("wout_bf", [D, D]), ("wup_bf", [D, 2 * DFF]),
                    ("wdn_bf", [DFF, D])):
        T[nm] = nc.dram_tensor(nm, shp, BF16, kind="Internal").ap()
    dscr("qT_s", [NS, 8, 128, S], BF16); dscr("kT_s", [NS, 8, 128, S], BF16); dscr("vT_s", [NS, 8, 128, S], BF16)
    dscr("gateT_s", [NS, D, S], BF16); dscr("gbaT_s", [NS, D, S], BF16); dscr("gbbT_s", [NS, D, S], BF16)
    dscr("cqT_s", [NS, 384, S], BF16); dscr("ckvT_s", [NS, 256, S], BF16); dscr("kpeT_s", [NS, 64, S], BF16)
    dscr("gbt_s", [NS, S, 24], F32)
    dscr("oaT_s", [NS, D, S], BF16); dscr("obT_s", [NS, D, S], BF16)
    dscr("x1_s", [NS, S, D], F32); dscr("h2T_s", [NS, D, S], BF16)
    T["out"] = nc.dram_tensor("out", [NS, S, D], F32, kind="ExternalOutput").ap()

    with contextlib.ExitStack() as es:
        P = Prog(nc, es)
        k = K(P)
        phase0_convert(nc, P, k, [(T["w_in_p"], T["win_bf"]), (T["w_uq_p"], T["wuq_bf"]), (T["w_uk"], T["wuk_bf"]),
                                  (T["w_uv"], T["wuv_bf"]), (T["w_og"], T["wog_bf"]), (T["w_om"], T["wom_bf"]),
                                  (T["w_out"], T["wout_bf"]), (T["w_up"], T["wup_bf"]), (T["w_dn"], T["wdn_bf"])])
        phase_A(nc, P, k, S, NS, T)
        if upto != "A":
            phase_G(nc, P, k, S, NS, T)
        if upto not in ("A", "G"):
            phase_M(nc, P, k, S, NS, T)
        if upto not in ("A", "G", "M"):
            phase_C1(nc, P, k, S, NS, T)
            phase_C2(nc, P, k, S, NS, T)
        P.finish()
    return nc


def host_consts(inp, S):
    f = np.float32
    bf = ml_dtypes.bfloat16
    w_in = np.asarray(inp["w_in"][0], f)
    offs = np.cumsum([0, 3072, 1024, 8, 8, 384, 256, 64, 1024, 1024])
    qkv, gate, a_, b_, cq, ckv, kpe, gba, gbb = [w_in[:, offs[i]:offs[i + 1]] for i in range(9)]
    kps = np.concatenate([kpe[:, 32:], kpe[:, :32]], axis=1)
    c = {}
    c["w_in_p"] = np.ascontiguousarray(np.concatenate([qkv, gate, gba, gbb, cq, ckv, kpe, kps, a_, b_], axis=1))
    wuq = np.asarray(inp["w_uq"][0], f).reshape(384, 8, 192)
    c["w_uq_p"] = np.ascontiguousarray(np.concatenate(
        [wuq[:, :, :128], wuq[:, :, 128:], wuq[:, :, 160:], wuq[:, :, 128:160]], axis=2).reshape(384, 2048))
    wukv = np.asarray(inp["w_ukv"][0], f).reshape(256, 8, 256)
    c["w_uk"] = np.ascontiguousarray(wukv[:, :, :128].reshape(256, 1024))
    c["w_uv"] = np.ascontiguousarray(wukv[:, :, 128:].reshape(256, 1024))
    c["w_og"] = np.ascontiguousarray(inp["w_o_gdn"][0], f)
    c["w_om"] = np.ascontiguousarray(inp["w_o_mla"][0], f)
    c["w_out"] = np.ascontiguousarray(inp["w_out"][0], f)
    c["w_up"] = np.ascontiguousarray(inp["w_up"][0], f)
    c["w_dn"] = np.ascontiguousarray(inp["w_down"][0], f)
    c["ident_bf"] = np.eye(128, dtype=f).astype(bf)
    c["ident_f"] = np.eye(128, dtype=f)
    bc = lambda v: np.ascontiguousarray(np.broadcast_to(np.asarray(v, f).reshape(1, -1), (128, np.asarray(v).size)))
    c["gmix_bc"] = bc(inp["norm_mix_g"][0]); c["gffn_bc"] = bc(inp["norm_ffn_g"][0]); c["gfin_bc"] = bc(inp["norm_final_g"])
    c["cw_qkv"] = np.ascontiguousarray(np.asarray(inp["conv_qkv_w"][0], f).reshape(4, 24, 128).transpose(2, 1, 0))
    c["cw_ffn"] = np.ascontiguousarray(np.asarray(inp["conv_ffn_w"][0], f).reshape(3, 44, 128).transpose(2, 1, 0))
    c["dtb_bc"] = bc(inp["gdn_dt_bias"][0]); c["alog_bc"] = bc(inp["gdn_a_log"][0])
    c["gq_p"] = np.ascontiguousarray(np.asarray(inp["mla_q_norm_g"][0], f).reshape(3, 128).T)
    c["gkv_p"] = np.ascontiguousarray(np.asarray(inp["mla_kv_norm_g"][0], f).reshape(2, 128).T)
    c["ggdn_p"] = np.ascontiguousarray(np.asarray(inp["gdn_norm_g"][0], f).reshape(128, 1))
    inv = (np.float32(10000.0) ** (-(np.arange(32, dtype=f) / np.float32(32)))).astype(f)
    ang = (np.arange(S, dtype=f)[None, :] * inv[:, None]).astype(f)
    cs, sn = np.cos(ang.astype(np.float64)).astype(f), np.sin(ang.astype(np.float64)).astype(f)
    c["cosT"] = np.ascontiguousarray(np.concatenate([cs, cs], 0))
    c["sinT"] = np.ascontiguousarray(np.concatenate([-sn, sn], 0))
    i = np.arange(128)[:, None]; j = np.arange(128)[None, :]
    c["m_strict"] = np.where(i > j, 0.0, NEG).astype(f)
    c["m_strictT"] = np.where(j > i, 0.0, NEG).astype(f)
    c["m_inclT"] = np.where(j >= i, 0.0, NEG).astype(f)
    c["utri"] = (i <= j).astype(f)
    c["tri_bf"] = (j >= i).astype(f).astype(bf)
    rep4 = lambda m_: np.ascontiguousarray(np.broadcast_to(m_.astype(f)[:, None, :], (128, 4, 128)))
    c["maskd4"] = rep4((i // 16) == (j // 16))
    for l_, b_ in ((1, 16), (2, 32), (3, 64)):
        c["mX%d_4" % l_] = rep4(((i // b_) % 2 == 0) & ((j // b_) == (i // b_) + 1))
    return c


_NC_CACHE = {}


def kernel(**inputs):
    x = np.asarray(inputs["x"], np.float32)
    B, S, _ = x.shape
    n = 8
    NS = B // n
    key = (S, NS)
    if key not in _NC_CACHE:
        _NC_CACHE[key] = build(S, NS)
    nc = _NC_CACHE[key]
    c = host_consts(inputs, S)
    in_maps = []
    for i in range(n):
        m = dict(c)
        m["x"] = np.ascontiguousarray(x[i * NS:(i + 1) * NS])
        in_maps.append(m)
    res = run_bass_kernel_spmd(nc, in_maps, core_ids=list(range(n)))
    return np.concatenate([np.asarray(r["out"], np.float32) for r in res.results], axis=0)
```
